# Optimizing a Trainium2 kernel written in Bass

```python
import math
import jax, jax.numpy as jnp
from jax import lax
import numpy as np

D_MODEL = 2048
BATCH = 2
SEQ = 4096
DEPTH = 1

D_MIX = D_MODEL
D_NSA = D_MIX // 2
D_CONV = D_MIX - D_NSA
HEAD_DIM = 64
N_HEADS = D_NSA // HEAD_DIM
N_KV = 4
HPG = N_HEADS // N_KV
CMP_LEN = 32
CMP_STRIDE = 16
CMP_HIDDEN = 256
SLC_LEN = 64
N_SLC_SEL = 16
WINDOW = 512
Q_BLOCK = 128
FORCE_BONUS = 1.0e4
CONV_WIDTH = 31
PEER_HEADS = 8
N_KEYS = 128
N_EXPERTS = N_KEYS * N_KEYS
PK_DIM = 256
PK_HALF = PK_DIM // 2
PK_TOPK = 16
PEER_BLOCK = 128
Q_COLS = N_HEADS * HEAD_DIM
KV_COLS = 6 * N_KV * HEAD_DIM
GATE_COLS = 3 * N_HEADS
CONV_COLS = 2 * D_CONV
D_IN = Q_COLS + KV_COLS + GATE_COLS + CONV_COLS
LN_EPS = 1e-5
DN_ALPHA = (2 * DEPTH) ** 0.25
DN_BETA = (8 * DEPTH) ** -0.25

kernel_name = "hybrid_nsa_conformer_peer_block"


def layer_norm(x, g, b):
    xf = x.astype(jnp.float32)
    mu = jnp.mean(xf, axis=-1, keepdims=True)
    var = jnp.mean(jnp.square(xf - mu), axis=-1, keepdims=True)
    return ((xf - mu) * lax.rsqrt(var + LN_EPS) * g + b).astype(x.dtype)


def masked_softmax(s, mask):
    s = jnp.where(mask, s.astype(jnp.float32), -1e30)
    p = jax.nn.softmax(s, axis=-1)
    return jnp.where(mask, p, 0.0)


def alibi_slopes():
    sl = 2.0 ** (-8.0 * np.arange(1, N_HEADS + 1) / N_HEADS)
    return jnp.asarray(sl, jnp.float32).reshape(N_KV, HPG)


def compress_kv(kv, pos, w1, w2):
    bsz, seq = kv.shape[0], kv.shape[1]
    n_cmp = (seq - CMP_LEN) // CMP_STRIDE + 1
    idx = np.arange(n_cmp)[:, None] * CMP_STRIDE + np.arange(CMP_LEN)[None, :]
    blk = kv[:, idx] + pos[:, None, :]
    blk = jnp.transpose(blk, (0, 1, 3, 2, 4)).reshape(bsz, n_cmp, N_KV, CMP_LEN * HEAD_DIM)
    return jax.nn.gelu(blk @ w1) @ w2


def nsa_group(q, k_c, v_c, k_s, v_s, k_w, v_w, gates, pos_k, w1k, w2k, pos_v, w1v, w2v):
    bsz, seq = q.shape[0], q.shape[1]
    scale = HEAD_DIM ** -0.5
    slopes = alibi_slopes()
    kc = compress_kv(k_c, pos_k, w1k, w2k)
    vc = compress_kv(v_c, pos_v, w1v, w2v)
    n_cmp = kc.shape[1]
    n_slc = seq // SLC_LEN
    n_sel = min(N_SLC_SEL, n_slc)
    cmp_start = np.arange(n_cmp) * CMP_STRIDE
    cmp_end = cmp_start + CMP_LEN - 1
    slc_start = np.arange(n_slc) * SLC_LEN
    overlap = ((cmp_start[:, None] < slc_start[None, :] + SLC_LEN) &
               (cmp_start[:, None] + CMP_LEN > slc_start[None, :])).astype(np.float32)
    ks_blk = jnp.transpose(k_s.reshape(bsz, n_slc, SLC_LEN, N_KV, HEAD_DIM), (0, 3, 1, 2, 4))
    vs_blk = jnp.transpose(v_s.reshape(bsz, n_slc, SLC_LEN, N_KV, HEAD_DIM), (0, 3, 1, 2, 4))
    kw_pad = jnp.pad(k_w, ((0, 0), (WINDOW, 0), (0, 0), (0, 0)))
    vw_pad = jnp.pad(v_w, ((0, 0), (WINDOW, 0), (0, 0), (0, 0)))
    b_ix = jnp.arange(bsz)[:, None, None, None]
    g_ix = jnp.arange(N_KV)[None, None, :, None]
    jb = jnp.arange(n_slc)

    def block(j):
        t0 = j * Q_BLOCK
        qb = lax.dynamic_slice_in_dim(q, t0, Q_BLOCK, axis=1)
        gb = lax.dynamic_slice_in_dim(gates, t0, Q_BLOCK, axis=1).reshape(bsz, Q_BLOCK, N_KV, HPG, 3)
        tpos = t0 + jnp.arange(Q_BLOCK)
        dist_c = (tpos[:, None] - cmp_end[None, :]).astype(jnp.float32)
        s = jnp.einsum('bqghd,bngd->bghqn', qb, kc) * scale
        s = s - slopes[None, :, :, None, None] * dist_c
        p_c = masked_softmax(s, cmp_end[None, :] <= tpos[:, None])
        o_cmp = jnp.einsum('bghqn,bngd->bqghd', p_c, vc)
        imp = jnp.einsum('bghqn,nj->bqgj', p_c, overlap)
        cur = tpos // SLC_LEN
        forced = (jb[None, :] == 0) | (jb[None, :] == cur[:, None]) | (jb[None, :] == cur[:, None] - 1)
        imp = jnp.where(forced[None, :, None, :], imp + FORCE_BONUS, imp)
        imp = jnp.where((slc_start[None, :] <= tpos[:, None])[None, :, None, :], imp, -1.0)
        _, sel = lax.top_k(imp, n_sel)
        kg = ks_blk[b_ix, g_ix, sel]
        vg = vs_blk[b_ix, g_ix, sel]
        spos = sel[..., None] * SLC_LEN + jnp.arange(SLC_LEN)
        dist_s = (tpos[None, :, None, None, None] - spos).astype(jnp.float32)
        s = jnp.einsum('bqghd,bqgkld->bqghkl', qb, kg) * scale
        s = s - slopes[None, None, :, :, None, None] * dist_s[:, :, :, None]
        valid = jnp.broadcast_to((dist_s >= 0)[:, :, :, None], s.shape)
        p_s = masked_softmax(s.reshape(s.shape[:4] + (-1,)), valid.reshape(s.shape[:4] + (-1,)))
        o_slc = jnp.einsum('bqghkl,bqgkld->bqghd', p_s.reshape(s.shape), vg)
        kwb = lax.dynamic_slice_in_dim(kw_pad, t0, WINDOW + Q_BLOCK, axis=1)
        vwb = lax.dynamic_slice_in_dim(vw_pad, t0, WINDOW + Q_BLOCK, axis=1)
        wpos = t0 - WINDOW + jnp.arange(WINDOW + Q_BLOCK)
        dw = tpos[:, None] - wpos[None, :]
        mask_w = (dw >= 0) & (dw < WINDOW) & (wpos[None, :] >= 0)
        s = jnp.einsum('bqghd,bsgd->bghqs', qb, kwb) * scale
        s = s - slopes[None, :, :, None, None] * dw.astype(jnp.float32)
        p_w = masked_softmax(s, mask_w)
        o_win = jnp.einsum('bghqs,bsgd->bqghd', p_w, vwb)
        return gb[..., 0:1] * o_cmp + gb[..., 1:2] * o_slc + gb[..., 2:3] * o_win

    out = lax.map(block, jnp.arange(seq // Q_BLOCK))
    return jnp.transpose(out, (1, 0, 2, 3, 4, 5)).reshape(bsz, seq, D_NSA)


def conformer_conv_group(ab, dw_w, dw_b, ln_g, ln_b):
    a, gate = jnp.split(ab, 2, axis=-1)
    u = a * jax.nn.sigmoid(gate)
    u = jnp.pad(u, ((0, 0), (CONV_WIDTH - 1, 0), (0, 0)))
    c = lax.conv_general_dilated(u, dw_w[:, None, :], window_strides=(1,), padding='VALID',
                                 dimension_numbers=('NWC', 'WIO', 'NWC'),
                                 feature_group_count=D_CONV) + dw_b
    return jax.nn.silu(layer_norm(c, ln_g, ln_b))


def peer_ffn(y, wq, sub_keys, u_tab, v_tab):
    bsz, seq = y.shape[0], y.shape[1]
    yb_all = jnp.transpose(y.reshape(bsz, seq // PEER_BLOCK, PEER_BLOCK, D_MODEL), (1, 0, 2, 3))

    def block(yb):
        q = (yb @ wq).reshape(bsz, PEER_BLOCK, PEER_HEADS, 2, PK_HALF)
        s1 = jnp.einsum('bthd,hkd->bthk', q[..., 0, :], sub_keys[:, 0])
        s2 = jnp.einsum('bthd,hkd->bthk', q[..., 1, :], sub_keys[:, 1])
        v1, i1 = lax.top_k(s1, PK_TOPK)
        v2, i2 = lax.top_k(s2, PK_TOPK)
        cand = (v1[..., :, None] + v2[..., None, :]).reshape(bsz, PEER_BLOCK, PEER_HEADS, PK_TOPK * PK_TOPK)
        cidx = (i1[..., :, None] * N_KEYS + i2[..., None, :]).reshape(cand.shape)
        top, pos = lax.top_k(cand, PK_TOPK)
        eidx = jnp.take_along_axis(cidx, pos, axis=-1)
        g = jax.nn.softmax(top.astype(jnp.float32), axis=-1)
        u = u_tab[eidx]
        v = v_tab[eidx]
        h = jax.nn.gelu(jnp.einsum('btd,bthkd->bthk', yb, u))
        return jnp.einsum('bthk,bthkd->btd', g * h, v).astype(y.dtype)

    out = lax.map(block, yb_all)
    return jnp.transpose(out, (1, 0, 2, 3)).reshape(bsz, seq, D_MODEL)


def setup_inputs(seed: int = 0) -> dict:
    key = jax.random.key(seed)
    ks = jax.random.split(key, 24)
    f32 = jnp.float32
    L = DEPTH
    def nrm(k, shape, scale):
        return jax.random.normal(k, shape, f32) * scale
    return {
        "x": nrm(ks[0], (BATCH, SEQ, D_MODEL), 1.0),
        "w_in": nrm(ks[1], (L, D_MODEL, D_IN), D_MODEL ** -0.5),
        "cmp_pos_k": nrm(ks[2], (L, CMP_LEN, HEAD_DIM), 0.1),
        "cmp_w1_k": nrm(ks[3], (L, CMP_LEN * HEAD_DIM, CMP_HIDDEN), (CMP_LEN * HEAD_DIM) ** -0.5),
        "cmp_w2_k": nrm(ks[4], (L, CMP_HIDDEN, HEAD_DIM), CMP_HIDDEN ** -0.5),
        "cmp_pos_v": nrm(ks[5], (L, CMP_LEN, HEAD_DIM), 0.1),
        "cmp_w1_v": nrm(ks[6], (L, CMP_LEN * HEAD_DIM, CMP_HIDDEN), (CMP_LEN * HEAD_DIM) ** -0.5),
        "cmp_w2_v": nrm(ks[7], (L, CMP_HIDDEN, HEAD_DIM), CMP_HIDDEN ** -0.5),
        "dw_w": nrm(ks[8], (L, CONV_WIDTH, D_CONV), CONV_WIDTH ** -0.5),
        "dw_b": nrm(ks[9], (L, D_CONV), 0.02),
        "conv_ln_g": 1.0 + nrm(ks[10], (L, D_CONV), 0.02),
        "conv_ln_b": nrm(ks[11], (L, D_CONV), 0.02),
        "w_out": nrm(ks[12], (L, D_MIX, D_MODEL), D_MIX ** -0.5 * DN_BETA),
        "ln1_g": 1.0 + nrm(ks[13], (L, D_MODEL), 0.02),
        "ln1_b": nrm(ks[14], (L, D_MODEL), 0.02),
        "peer_wq": nrm(ks[15], (L, D_MODEL, PEER_HEADS * PK_DIM), D_MODEL ** -0.5),
        "peer_keys": nrm(ks[16], (L, PEER_HEADS, 2, N_KEYS, PK_HALF), PK_HALF ** -0.5),
        "peer_u": nrm(ks[17], (L, N_EXPERTS, D_MODEL), D_MODEL ** -0.5),
        "peer_v": nrm(ks[18], (L, N_EXPERTS, D_MODEL), DN_BETA * PEER_HEADS ** -0.5),
        "ln2_g": 1.0 + nrm(ks[19], (L, D_MODEL), 0.02),
        "ln2_b": nrm(ks[20], (L, D_MODEL), 0.02),
    }


def reference(x, w_in, cmp_pos_k, cmp_w1_k, cmp_w2_k, cmp_pos_v, cmp_w1_v, cmp_w2_v,
              dw_w, dw_b, conv_ln_g, conv_ln_b, w_out, ln1_g, ln1_b,
              peer_wq, peer_keys, peer_u, peer_v, ln2_g, ln2_b):
    bsz, seq = x.shape[0], x.shape[1]
    splits = [Q_COLS, Q_COLS + KV_COLS, Q_COLS + KV_COLS + GATE_COLS]
    for l in range(DEPTH):
        h = x @ w_in[l]
        q, kv, gl, conv_ab = jnp.split(h, splits, axis=-1)
        q = q.reshape(bsz, seq, N_KV, HPG, HEAD_DIM)
        kv = kv.reshape(bsz, seq, 6, N_KV, HEAD_DIM)
        gates = jax.nn.sigmoid(gl).reshape(bsz, seq, N_HEADS, 3)
        o_nsa = nsa_group(q, kv[:, :, 0], kv[:, :, 1], kv[:, :, 2], kv[:, :, 3], kv[:, :, 4], kv[:, :, 5],
                          gates, cmp_pos_k[l], cmp_w1_k[l], cmp_w2_k[l],
                          cmp_pos_v[l], cmp_w1_v[l], cmp_w2_v[l]).astype(x.dtype)
        o_conv = conformer_conv_group(conv_ab, dw_w[l], dw_b[l], conv_ln_g[l], conv_ln_b[l]).astype(x.dtype)
        mix = jnp.concatenate([o_nsa, o_conv], axis=-1) @ w_out[l]
        x = layer_norm(DN_ALPHA * x + mix, ln1_g[l], ln1_b[l])
        ffn = peer_ffn(x, peer_wq[l], peer_keys[l], peer_u[l], peer_v[l])
        x = layer_norm(DN_ALPHA * x + ffn, ln2_g[l], ln2_b[l])
    return x
```

```python
from contextlib import ExitStack
import numpy as np
import ml_dtypes
import concourse.bass as bass
import concourse.mybir as mybir
from concourse.bass_utils import run_bass_kernel_spmd

F32 = mybir.dt.float32
BF16 = mybir.dt.bfloat16
U32 = mybir.dt.uint32
AF = mybir.ActivationFunctionType
ALU = mybir.AluOpType
AX = mybir.AxisListType

NEG = -30000.0
LN_EPS = 1e-5
DN_ALPHA = 2.0 ** 0.25


class KB:
    NDMA = 12
    SEM_ROLL = 30000

    def __init__(self, nc):
        self.nc = nc
        self.st = ExitStack()
        self.E = dict(pe=nc.tensor, act=nc.scalar, dve=nc.vector, pool=nc.gpsimd, sp=nc.sync)
        self.esem = {}
        self.ecnt = {}
        self.nsem = 0
        for e in self.E:
            self._new_esem(e)
        self.dsem = {}
        self.dcnt = {}
        for q in ("sp", "pool"):
            self.dsem[q] = [self._sem() for _ in range(self.NDMA)]
            self.dcnt[q] = 0
        self.waited = {e: {} for e in self.E}
        self.lastw = {}
        self.readers = {}
        self.n_wait = 0
        self.n_ops = 0

    def _sem(self):
        self.nsem += 1
        return self.st.enter_context(self.nc.semaphore("ks%d" % self.nsem))

    def _new_esem(self, e):
        self.esem[e] = self._sem()
        self.ecnt[e] = 0

    def sb(self, name, shape, dt, st=None):
        self.nsem += 1
        name = "%s_%d" % (name, self.nsem)
        return (st or self.st).enter_context(self.nc.sbuf_tensor("s_" + name, list(shape), dt))

    def ps(self, name, shape, dt=F32, st=None):
        self.nsem += 1
        name = "%s_%d" % (name, self.nsem)
        return (st or self.st).enter_context(self.nc.psum_tensor("p_" + name, list(shape), dt))

    def _wait(self, eng, tok):
        sem, val, src = tok
        w = self.waited[eng]
        sid = id(sem)
        if w.get(sid, 0) >= val:
            return
        w[sid] = val
        self.E[eng].wait_ge(sem, val)
        self.n_wait += 1

    def _deps(self, eng, R, W, is_dma):
        for r in R:
            t = self.lastw.get(r)
            if t is not None:
                if is_dma or not (t[2] == eng and eng == "pe"):
                    self._wait(eng, t)
        for w_ in W:
            t = self.lastw.get(w_)
            if t is not None:
                if is_dma or not (t[2] == eng and eng == "pe"):
                    self._wait(eng, t)
            for t in self.readers.get(w_, ()):
                if is_dma or t[2] != eng:
                    self._wait(eng, t)

    def _record(self, tok, R, W):
        for r in R:
            lst = self.readers.setdefault(r, [])
            lst[:] = [t for t in lst if not (t[2] == tok[2] and t[0] is tok[0])]
            lst.append(tok)
        for w_ in W:
            self.lastw[w_] = tok
            self.readers[w_] = []

    def op(self, eng, fn, R=(), W=()):
        self._deps(eng, R, W, False)
        if self.ecnt[eng] >= self.SEM_ROLL:
            self._new_esem(eng)
        inst = fn(self.E[eng])
        self.ecnt[eng] += 1
        inst.then_inc(self.esem[eng], 1)
        tok = (self.esem[eng], self.ecnt[eng], eng)
        self._record(tok, R, W)
        self.n_ops += 1
        return tok

    def dma(self, q, out, in_, R=(), W=(), **kw):
        if out.dtype != in_.dtype:
            q = "pool"
        self._deps(q, R, W, True)
        k = self.dcnt[q]
        slot = k % self.NDMA
        sem = self.dsem[q][slot]
        val = 16 * (k // self.NDMA + 1)
        if val > 16:
            self._wait(q, (sem, val - 16, "dma_" + q))
        self.E[q].dma_start(out=out, in_=in_, **kw).then_inc(sem, 16)
        self.dcnt[q] += 1
        tok = (sem, val, "dma_" + q)
        self._record(tok, R, W)
        return tok

    def wait_keys(self, eng, keys):
        for k_ in keys:
            t = self.lastw.get(k_)
            if t is not None:
                self._wait(eng, t)

    def barrier(self):
        toks = []
        for e in self.E:
            if self.ecnt[e] > 0:
                toks.append((self.esem[e], self.ecnt[e], e))
        for q in self.dsem:
            k = self.dcnt[q]
            for slot in range(self.NDMA):
                n = (k - slot + self.NDMA - 1) // self.NDMA
                if n > 0:
                    toks.append((self.dsem[q][slot], 16 * n, "dma_" + q))
        for e in self.E:
            for t in toks:
                if t[2] == e and e == "pe":
                    continue
                self._wait(e, t)
        self.lastw.clear()
        self.readers.clear()


def bc(ap, n, axis=1):
    shp = list(ap.shape)
    shp.insert(axis, n)
    return ap.unsqueeze(axis).to_broadcast(shp)


def _bf16_round(a):
    return np.asarray(a, np.float32).astype(ml_dtypes.bfloat16).astype(np.float32)


def _common_consts():
    c = {}
    c["ident"] = np.eye(128, dtype=np.float32)
    h = np.arange(16)
    sl = (2.0 ** (-8.0 * (h + 1) / 16)).astype(np.float64)
    s_hi = _bf16_round(sl).astype(np.float64)
    s_lo = _bf16_round(sl - s_hi).astype(np.float64)
    qaug = np.zeros((4, 5, 8, 4, 128), np.float32)
    for g in range(4):
        for hh in range(4):
            hd = 4 * g + hh
            for m in range(8):
                ref = 128 * (4 * m + 3) + 64 - 2048
                qaug[g, 0, m, hh, :] = 8 * s_hi[hd]
                qaug[g, 1, m, hh, :] = 8 * s_lo[hd]
                qaug[g, 2, m, hh, :] = 8 * s_hi[hd]
                qaug[g, 3, m, hh, :] = 8 * s_lo[hd]
                qaug[g, 4, m, hh, :] = -8 * sl[hd] * ref
    c["qaug"] = qaug.reshape(4, 5, 8 * 4 * 128)

    def aug_rows(P):
        P = np.asarray(P, np.int64)
        hi = 128 * np.floor_divide(P, 128)
        lo = P - hi
        return np.stack([hi, hi, lo, lo, np.ones_like(P)]).astype(np.float32)

    c["kaug"] = aug_rows(np.arange(4096) - 2048)
    pc = np.zeros(256, np.int64)
    pc[:255] = 16 * np.arange(255) + 31 - 2048
    c["kcaug"] = aug_rows(pc)
    ex = np.zeros((64, 32, 128), np.float32)
    for kt in range(32):
        ex[2 * kt, kt, :64] = 1
        ex[2 * kt + 1, kt, 64:] = 1
    c["expand"] = ex.reshape(64, 32 * 128)
    n = np.arange(256)[:, None]
    jb = np.arange(64)[None, :]
    ov = ((16 * n < 64 * jb + 64) & (16 * n + 32 > 64 * jb) & (n < 255)).astype(np.float32)
    ovc = np.zeros((256, 65), np.float32)
    ovc[:255, 0] = 1.0
    ovc[:, 1:] = ov
    c["ovc"] = ovc.reshape(2, 128, 65).transpose(1, 0, 2).copy()
    k = np.arange(128)[:, None]
    q = np.arange(128)[None, :]
    c["tri_lo"] = np.where(k > q, NEG, 0.0).astype(np.float32)
    c["tri_up"] = np.where(k <= q, NEG, 0.0).astype(np.float32)
    return c


def _core_consts(s):
    c = {}
    npre = 3 - s
    nn = np.arange(128)[:, None, None, None]
    m = np.arange(8)[None, :, None, None]
    ch = np.arange(2)[None, None, :, None]
    q = np.arange(128)[None, None, None, :]
    n_ = ch * 128 + nn
    valid = (n_ >= 8 * npre) & (n_ <= 254) & (16 * n_ + 31 <= 128 * (4 * m + 3) + q)
    c["maskc"] = np.where(valid, 0.0, NEG).astype(np.float32).reshape(128, 8 * 2 * 128)
    k = np.arange(128)[:, None]
    qq = np.arange(128)[None, :]
    wm = np.zeros((128, 5, 128), np.float32)
    for D in range(5):
        if s - D < 0:
            wm[:, D, :] = NEG
        elif D == 0:
            wm[:, D, :] = np.where(k > qq, NEG, 0.0)
    c["winmask0"] = wm.reshape(128, 5 * 128)
    qv = np.arange(128)[:, None, None]
    mv = np.arange(8)[None, :, None]
    jb = np.arange(64)[None, None, :]
    jt = jb - 2 * npre
    tq = (4 * mv + s) * 128 + qv
    cur = tq // 64
    real = jt >= 0
    vis = real & (64 * jt <= tq)
    forced = (jt == 0) | (jt == cur) | (jt == cur - 1)
    c["vis"] = vis.astype(np.float32).reshape(128, 8 * 64)
    c["cstb"] = np.where(vis, 1.0e4 * forced, np.where(real, -1.0, -2.0)).astype(np.float32).reshape(128, 8 * 64)
    return c


def _prep_inputs(inp):
    x = np.asarray(inp["x"], np.float32)
    w_in = np.asarray(inp["w_in"], np.float32)[0]
    com = _common_consts()
    wg = np.zeros((4, 2048, 652), np.float32)
    for g in range(4):
        cols = list(range(g * 256, g * 256 + 256))
        for br in (0, 1, 2, 4, 3, 5):
            base = 1024 + (br * 4 + g) * 64
            cols += list(range(base, base + 64))
        cols += list(range(2560 + 12 * g, 2560 + 12 * g + 12))
        wg[g] = w_in[:, cols]
    com["wg"] = wg
    com["wconv"] = np.ascontiguousarray(w_in[:, 2608:4656])
    for nm in ("k", "v"):
        com["w1" + nm] = np.asarray(inp["cmp_w1_" + nm], np.float32)[0]
        com["w2" + nm] = np.asarray(inp["cmp_w2_" + nm], np.float32)[0]
        com["posT" + nm] = np.ascontiguousarray(np.asarray(inp["cmp_pos_" + nm], np.float32)[0].T)
    com["dww"] = np.ascontiguousarray(np.asarray(inp["dw_w"], np.float32)[0].T)
    com["dwb"] = np.ascontiguousarray(np.asarray(inp["dw_b"], np.float32)[0].reshape(8, 128).T)
    com["clg"] = np.ascontiguousarray(np.asarray(inp["conv_ln_g"], np.float32)[0].reshape(8, 128).T)
    com["clb"] = np.ascontiguousarray(np.asarray(inp["conv_ln_b"], np.float32)[0].reshape(8, 128).T)
    com["wout"] = np.asarray(inp["w_out"], np.float32)[0]
    com["ln1g"] = np.broadcast_to(np.asarray(inp["ln1_g"], np.float32)[0][None, :], (128, 2048)).copy()
    com["ln1b"] = np.broadcast_to(np.asarray(inp["ln1_b"], np.float32)[0][None, :], (128, 2048)).copy()
    wq = np.asarray(inp["peer_wq"], np.float32)[0]
    com["wq_l"] = np.ascontiguousarray(wq.reshape(16, 128, 16, 128).transpose(2, 1, 0, 3)).reshape(16, 128, 2048)
    keys = np.asarray(inp["peer_keys"], np.float32)[0]
    com["keysT"] = np.ascontiguousarray(keys.transpose(3, 0, 1, 2)).reshape(128, 2048)
    U = np.asarray(inp["peer_u"], np.float32)[0]
    com["UT_l"] = np.ascontiguousarray(U.reshape(128, 128, 16, 128).transpose(0, 3, 2, 1)).reshape(128, 128, 2048)
    com["V"] = np.asarray(inp["peer_v"], np.float32)[0]
    com["ln2g"] = np.broadcast_to(np.asarray(inp["ln2_g"], np.float32)[0][None, :], (128, 2048)).copy()
    com["ln2b"] = np.broadcast_to(np.asarray(inp["ln2_b"], np.float32)[0][None, :], (128, 2048)).copy()
    maps = []
    for c in range(8):
        b, s = c // 4, c % 4
        sh = (3 - s) * 128
        d = dict(com)
        xT = np.zeros((2048, 4096), np.float32)
        xT[:, sh:] = x[b, :4096 - sh].T
        d["xTs"] = xT
        own = np.concatenate([np.arange((4 * m + s) * 128, (4 * m + s + 1) * 128) for m in range(8)])
        d["x_own"] = np.ascontiguousarray(x[b, own])
        d.update(_core_consts(s))
        maps.append(d)
    return maps


def build(debug=False, stop=None, nch=128, peer_only=False):
    nc = bass.Bass("TRN2", target_bir_lowering=False)
    D = {}

    def din(name, shape, dt=F32):
        D[name] = nc.dram_tensor(name, list(shape), dt, kind="ExternalInput").ap()
        return D[name]

    def dout(name, shape, dt=F32):
        D[name] = nc.dram_tensor(name, list(shape), dt, kind="ExternalOutput").ap()
        return D[name]

    if not peer_only:
        din("xTs", [2048, 4096]); din("x_own", [1024, 2048])
        din("qaug", [4, 5, 4096]); din("kaug", [5, 4096]); din("kcaug", [5, 256])
        din("expand", [64, 4096]); din("ovc", [128, 2, 65]); din("tri_lo", [128, 128]); din("tri_up", [128, 128])
        din("maskc", [128, 2048]); din("winmask0", [128, 640]); din("vis", [128, 512]); din("cstb", [128, 512])
        din("wg", [4, 2048, 652]); din("wconv", [2048, 2048])
        for nm in ("k", "v"):
            din("w1" + nm, [2048, 256]); din("w2" + nm, [256, 64]); din("posT" + nm, [64, 32])
        din("dww", [1024, 31]); din("dwb", [128, 8]); din("clg", [128, 8]); din("clb", [128, 8])
        din("wout", [2048, 2048]); din("ln1g", [128, 2048]); din("ln1b", [128, 2048])
    else:
        din("y_in", [1024, 2048])
    din("ident", [128, 128])
    din("wq_l", [16, 128, 2048]); din("keysT", [128, 2048]); din("UT_l", [nch, 128, 2048]); din("V", [nch * 128, 2048])
    din("ln2g", [128, 2048]); din("ln2b", [128, 2048])
    dout("out", [1024, 2048])
    if debug:
        dout("d_acc", [1024, 2048])
    if debug and not peer_only:
        dout("d_onsa", [128, 8, 1024]); dout("d_oconv", [128, 8, 1024])
        dout("d_kc", [64, 4, 256]); dout("d_vc", [128, 4, 2, 64]); dout("d_q", [64, 4096])
        dout("d_ksel", [64, 4096]); dout("d_vsw", [128, 32 * 2 * 65]); dout("d_gates", [128, 8, 48])
        dout("d_imp", [128, 8, 64]); dout("d_negsel", [128, 8, 64])
    if debug:
        dout("d_S1", [128, 8192]); dout("d_S2", [128, 8192]); dout("d_c3", [128, 64])

    kb = KB(nc)
    with kb.st:
        if peer_only:
            _peer_only(nc, kb, D, debug, stop, nch)
        else:
            _program(nc, kb, D, debug, stop, nch)
        kb.barrier()
    return nc


def _program(nc, kb, D, debug, stop=None, nch=128):
    op, dma = kb.op, kb.dma
    ident = kb.sb("ident", [128, 128], F32)
    identb = kb.sb("identb", [128, 128], BF16)
    epsc = kb.sb("epsc", [128, 1], F32)
    y = kb.sb("y", [128, 8, 2048], F32)
    pA = ExitStack()
    kb.st.enter_context(pA)
    ones_f = kb.sb("ones_f", [128, 128], F32, pA)
    conv_scr = nc.dram_tensor("conv_scr", [128, 8192], BF16, kind="Internal").ap()
    dma("sp", ident[:], D["ident"], W=["ident"])
    dma("pool", identb[:], D["ident"], W=["identb"])
    op("dve", lambda e: e.memset(ones_f[:], 1.0), W=["ones_f"])
    op("dve", lambda e: e.memset(epsc[:], LN_EPS), W=["epsc"])

    xT_d = D["xTs"].rearrange("(dc p) t -> p dc t", p=128)

    with ExitStack() as ph:
        xoh = kb.sb("xoh", [128, 16, 8, 160], BF16, ph)
        o_convT = kb.sb("o_convT", [128, 8, 1024], BF16, ph)
        dww = kb.sb("dww", [128, 8, 31], F32, ph)
        dwb = kb.sb("dwb", [128, 8], F32, ph)
        clg = kb.sb("clg", [128, 8], F32, ph)
        clb = kb.sb("clb", [128, 8], F32, ph)
        call = kb.sb("call", [128, 8, 1024], F32, ph)
        wca = [kb.sb("wca%d" % i, [128, 16, 128], BF16, ph) for i in range(2)]
        wcg = [kb.sb("wcg%d" % i, [128, 16, 128], BF16, ph) for i in range(2)]
        sg = [kb.sb("sg%d" % i, [128, 480], F32, ph) for i in range(2)]
        u = [kb.sb("u%d" % i, [128, 8, 160], BF16, ph) for i in range(2)]
        dgw = [kb.sb("dgw%d" % i, [128, 31, 128], BF16, ph) for i in range(2)]
        psC = [kb.ps("psC%d" % i, [128, 512], F32, ph) for i in range(2)]
        psA = [kb.ps("psA%d" % i, [128, 512], F32, ph) for i in range(2)]
        psG = [kb.ps("psG%d" % i, [128, 512], F32, ph) for i in range(2)]
        psL = [kb.ps("psL%d" % i, [128, 512], F32, ph) for i in range(2)]

        for m in range(8):
            t0 = m * 512 + 352
            dma("pool", xoh[:, :, m, :], xT_d[:, :, t0:t0 + 160], W=[("xoh", m)])
        dma("sp", dww[:], D["dww"].rearrange("(ct p) w -> p ct w", p=128), W=["dww"])
        for nm, t in (("dwb", dwb), ("clg", clg), ("clb", clb)):
            dma("sp", t[:], D[nm], W=[nm])
        wc_d = D["wconv"].rearrange("(dc p) c -> p dc c", p=128)
        xoh_keys = [("xoh", m) for m in range(8)]
        chunks = [(0, 3), (3, 3), (6, 2)]
        k = 0
        for ct in range(8):
            wb = ct % 2
            dma("pool", wca[wb][:], wc_d[:, :, ct * 128:(ct + 1) * 128], W=[("wca", wb)])
            dma("pool", wcg[wb][:], wc_d[:, :, 1024 + ct * 128:1024 + (ct + 1) * 128], W=[("wcg", wb)])
            ub = ct % 2
            for (m0, nm_) in chunks:
                pb = k % 2
                k += 1
                n = nm_ * 160
                for dc in range(16):
                    op("pe", lambda e, dc=dc: e.matmul(psA[pb][:, 0:n].rearrange("p (a b) -> p a b", a=nm_), wca[wb][:, dc, :],
                                                       xoh[:, dc, m0:m0 + nm_, :], start=(dc == 0), stop=(dc == 15)),
                       R=[("wca", wb)] + xoh_keys, W=[("psA", pb)])
                for dc in range(16):
                    op("pe", lambda e, dc=dc: e.matmul(psG[pb][:, 0:n].rearrange("p (a b) -> p a b", a=nm_), wcg[wb][:, dc, :],
                                                       xoh[:, dc, m0:m0 + nm_, :], start=(dc == 0), stop=(dc == 15)),
                       R=[("wcg", wb)] + xoh_keys, W=[("psG", pb)])
                op("act", lambda e: e.activation(out=sg[pb][:, 0:n], in_=psG[pb][:, 0:n], func=AF.Sigmoid),
                   R=[("psG", pb)], W=[("sg", pb)])
                op("dve", lambda e: e.tensor_tensor(out=u[ub][:, m0:m0 + nm_, :].rearrange("p a b -> p (a b)"),
                                                    in0=psA[pb][:, 0:n], in1=sg[pb][:, 0:n], op=ALU.mult),
                   R=[("psA", pb), ("sg", pb)], W=[("u", ub)])
            op("dve", lambda e: e.tensor_tensor(out=dgw[ub][:], in0=bc(identb[:], 31, axis=1), in1=bc(dww[:, ct, :], 128, axis=2), op=ALU.mult),
               R=["identb", "dww"], W=[("dgw", ub)])
            for hf in range(2):
                for w in range(31):
                    op("pe", lambda e, w=w: e.matmul(psC[hf][:].rearrange("p (a b) -> p a b", a=4), dgw[ub][:, w, :],
                                                     u[ub][:, 4 * hf:4 * hf + 4, 2 + w:130 + w], start=(w == 0), stop=(w == 30)),
                       R=[("dgw", ub), ("u", ub)], W=[("psC", hf)])
                op("act", lambda e: e.activation(out=call[:, ct, hf * 512:(hf + 1) * 512], in_=psC[hf][:], func=AF.Identity, bias=dwb[:, ct:ct + 1]),
                   R=[("psC", hf), "dwb"], W=[("call", ct)])
        with ExitStack() as ph2:
            csq = kb.sb("csq", [128, 512], F32, ph2)
            mean = kb.sb("cmean", [128, 512], F32, ph2)
            rstd = kb.sb("crstd", [128, 512], F32, ph2)
            tmp = kb.sb("ctmp", [128, 512], F32, ph2)
            for hf in range(2):
                tsl = slice(hf * 512, (hf + 1) * 512)
                for ct in range(8):
                    op("pe", lambda e, ct=ct: e.matmul(psL[0][:], ones_f[:], call[:, ct, tsl], start=(ct == 0), stop=(ct == 7)),
                       R=["ones_f", ("call", ct)], W=["psL0"])
                for ct in range(8):
                    op("act", lambda e, ct=ct: e.activation(out=csq[:], in_=call[:, ct, tsl], func=AF.Square),
                       R=[("call", ct)], W=["csq"])
                    op("pe", lambda e, ct=ct: e.matmul(psL[1][:], ones_f[:], csq[:], start=(ct == 0), stop=(ct == 7)),
                       R=["ones_f", "csq"], W=["psL1"])
                op("dve", lambda e: e.tensor_scalar(out=mean[:], in0=psL[0][:], scalar1=1.0 / 1024, scalar2=None, op0=ALU.mult),
                   R=["psL0"], W=["cmean"])
                op("dve", lambda e: e.tensor_tensor(out=tmp[:], in0=mean[:], in1=mean[:], op=ALU.mult), R=["cmean"], W=["ctmp"])
                op("dve", lambda e: e.scalar_tensor_tensor(out=rstd[:], in0=psL[1][:], scalar=1.0 / 1024, in1=tmp[:],
                                                           op0=ALU.mult, op1=ALU.subtract),
                   R=["psL1", "ctmp"], W=["crstd"])
                op("act", lambda e: e.activation(out=rstd[:], in_=rstd[:], func=AF.Sqrt, bias=epsc[:, 0:1]), R=["crstd", "epsc"], W=["crstd"])
                op("dve", lambda e: e.reciprocal(out=rstd[:], in_=rstd[:]), R=["crstd"], W=["crstd"])
                for ct in range(8):
                    op("dve", lambda e, ct=ct: e.tensor_tensor(out=tmp[:], in0=call[:, ct, tsl], in1=mean[:], op=ALU.subtract),
                       R=[("call", ct), "cmean"], W=["ctmp"])
                    op("dve", lambda e: e.tensor_tensor(out=tmp[:], in0=tmp[:], in1=rstd[:], op=ALU.mult),
                       R=["ctmp", "crstd"], W=["ctmp"])
                    op("act", lambda e, ct=ct: e.activation(out=o_convT[:, ct, tsl], in_=tmp[:], func=AF.Silu,
                                                            bias=clb[:, ct:ct + 1], scale=clg[:, ct:ct + 1]),
                       R=["ctmp", "clg", "clb"], W=[("o_convT", ct)])
        if debug:
            dma("sp", D["d_oconv"], o_convT[:], R=[("o_convT", ct) for ct in range(8)], W=["d_oconv"])
        dma("sp", conv_scr, o_convT[:].rearrange("p a b -> p (a b)"), R=[("o_convT", ct) for ct in range(8)], W=["conv_scr"])
        kb.barrier()
        if stop == "C":
            return

    expand = kb.sb("expand", [64, 32, 128], BF16, pA)
    tri_lo = kb.sb("tri_lo", [128, 128], BF16, pA)
    tri_up = kb.sb("tri_up", [128, 128], BF16, pA)
    maskc = kb.sb("maskc", [128, 8, 2, 128], BF16, pA)
    winmask0 = kb.sb("winmask0", [128, 5, 128], BF16, pA)
    vis = kb.sb("vis", [128, 8, 64], F32, pA)
    cstb = kb.sb("cstb", [128, 8, 64], F32, pA)
    o_nsa = kb.sb("o_nsa", [128, 8, 1024], BF16, pA)
    gates = kb.sb("gates", [128, 8, 48], F32, pA)
    dma("pool", expand[:].rearrange("p a b -> p (a b)"), D["expand"], W=["expand"])
    dma("pool", tri_lo[:], D["tri_lo"], W=["tri_lo"])
    dma("pool", tri_up[:], D["tri_up"], W=["tri_up"])
    dma("pool", maskc[:].rearrange("p a b c -> p (a b c)"), D["maskc"], W=["maskc"])
    dma("pool", winmask0[:].rearrange("p a b -> p (a b)"), D["winmask0"], W=["winmask0"])
    dma("sp", vis[:].rearrange("p a b -> p (a b)"), D["vis"], W=["vis"])
    dma("sp", cstb[:].rearrange("p a b -> p (a b)"), D["cstb"], W=["cstb"])
    if debug:
        op("pool", lambda e: e.memset(o_nsa[:], 0.0), W=["o_nsa_init"])
        op("pool", lambda e: e.memset(gates[:], 0.0), W=["gates_init"])

    with ExitStack() as phg:
        kT_sel = kb.sb("kT_sel", [69, 4096], BF16, phg)
        kT_win = kb.sb("kT_win", [69, 4096], BF16, phg)
        kvc = kb.sb("kvc", [128, 4096], BF16, phg)
        stg = kb.sb("stg", [128, 4096], BF16, phg)
        V_sw = kb.sb("V_sw", [128, 32, 2, 65], BF16, phg)
        qT = kb.sb("qT", [69, 8, 4, 128], BF16, phg)
        kc_aug = kb.sb("kc_aug", [69, 256], BF16, phg)
        vcx = kb.sb("vcx", [128, 2, 129], BF16, phg)
        cbias = kb.sb("cbias", [128, 2, 2], F32, phg)
        dma("pool", kT_sel[64:69, :], D["kaug"], W=["kT_sel_aug"])
        dma("pool", kT_win[64:69, :], D["kaug"], W=["kT_win_aug"])
        dma("pool", kc_aug[64:69, :], D["kcaug"], W=["kc_aug_aug"])
        dma("pool", vcx[:, :, 64:129], D["ovc"], W=["vcx_c"])
        op("pool", lambda e: e.memset(V_sw[:, :, :, 64:65], 1.0), W=["V_ones"])
        op("pool", lambda e: e.memset(kc_aug[0:64, 255:256], 0.0), W=["kc_pad"])

        for g in range(4):
            with ExitStack() as ph:
                wgs = kb.sb("wgs", [128, 16, 652], BF16, ph)
                xt = [kb.sb("xt%d" % i, [128, 16, 512], BF16, ph) for i in range(2)]
                psP = [kb.ps("psP%d" % i, [128, 512], F32, ph) for i in range(2)]
                psQ = kb.ps("psQ", [64, 512], F32, ph)
                psV = kb.ps("psV", [128, 512], F32, ph)
                psGt = kb.ps("psGt", [128, 512], F32, ph)
                dma("pool", wgs[:], D["wg"][g].rearrange("(dc p) c -> p dc c", p=128), W=["wgs"])
                dma("pool", qT[64:69, :, :, :].rearrange("p a b c -> p (a b c)"), D["qaug"][g], W=["qT_aug"])
                pk = 0
                for m in range(8):
                    xb_ = m % 2
                    dma("pool", xt[xb_][:], xT_d[:, :, m * 512:(m + 1) * 512], W=[("xt", xb_)])
                    pairs = [(256, kvc, "kvc_lo", kvc, "kvc_hi"), (384, kT_sel, "kT_sel", stg, "stg")]
                    for (off, dlo, nlo, dhi, nhi) in pairs:
                        pb = pk % 2
                        pk += 1
                        for dc in range(16):
                            op("pe", lambda e, dc=dc: e.matmul(psP[pb][:], wgs[:, dc, off:off + 128], xt[xb_][:, dc, :],
                                                               start=(dc == 0), stop=(dc == 15)),
                               R=["wgs", ("xt", xb_)], W=[("psP", pb)])
                        op("act", lambda e: e.activation(out=dlo[0:64, m * 512:(m + 1) * 512], in_=psP[pb][0:64, :], func=AF.Identity),
                           R=[("psP", pb)], W=[(nlo, m)])
                        op("dve", lambda e: e.tensor_copy(out=dhi[64:128, m * 512:(m + 1) * 512], in_=psP[pb][64:128, :]),
                           R=[("psP", pb)], W=[(nhi, m), ("psP", pb)])
                    for dc in range(16):
                        for hh in range(4):
                            op("pe", lambda e, dc=dc, hh=hh: e.matmul(psQ[:, hh * 128:(hh + 1) * 128],
                                                                      wgs[:, dc, hh * 64:(hh + 1) * 64], xt[xb_][:, dc, 384:512],
                                                                      start=(dc == 0 and hh == 0), stop=(dc == 15),
                                                                      skip_group_check=True),
                               R=["wgs", ("xt", xb_)], W=["psQ"])
                    op("act", lambda e: e.activation(out=qT[0:64, m, :, :].rearrange("p a b -> p (a b)"), in_=psQ[:], func=AF.Identity),
                       R=["psQ"], W=[("qT", m)])
                    for dc in range(16):
                        for sub in range(4):
                            op("pe", lambda e, dc=dc, sub=sub: e.matmul(psV[:, sub * 128:(sub + 1) * 128],
                                                                        xt[xb_][:, dc, sub * 128:(sub + 1) * 128], wgs[:, dc, 512:640],
                                                                        start=(dc == 0 and sub == 0), stop=(dc == 15),
                                                                        skip_group_check=True),
                               R=["wgs", ("xt", xb_)], W=["psV"])
                    op("dve", lambda e: e.tensor_copy(out=V_sw[:, 4 * m:4 * m + 4, :, 0:64],
                                                      in_=psV[:].rearrange("p (a b c) -> p a b c", a=4, b=2)),
                       R=["psV", "V_ones"], W=[("V_sw", m)])
                    for dc in range(16):
                        op("pe", lambda e, dc=dc: e.matmul(psGt[:, 0:12], xt[xb_][:, dc, 384:512], wgs[:, dc, 640:652],
                                                           start=(dc == 0), stop=(dc == 15)),
                           R=["wgs", ("xt", xb_)], W=["psGt"])
                    op("act", lambda e: e.activation(out=gates[:, m, 12 * g:12 * g + 12], in_=psGt[:, 0:12], func=AF.Sigmoid),
                       R=["psGt"], W=[("gates", m, g)])
                dma("sp", kT_win[0:64, :], stg[64:128, :], R=[("stg", m) for m in range(8)], W=[("kT_win", m) for m in range(8)])
                if debug and g == 0:
                    dma("sp", D["d_q"], qT[0:64].rearrange("p a b c -> p (a b c)"), R=[("qT", m) for m in range(8)], W=["d_q"])
                    dma("sp", D["d_ksel"], kT_sel[0:64, :], R=[("kT_sel", m) for m in range(8)], W=["d_ksel"])
                    dma("sp", D["d_vsw"], V_sw[:].rearrange("p a b c -> p (a b c)"), R=[("V_sw", m) for m in range(8)] + ["V_ones"], W=["d_vsw"])
                kb.barrier()
                if stop == "G1":
                    return

            with ExitStack() as ph:
                w1kv = kb.sb("w1kv", [128, 32, 256], BF16, ph)
                w2 = [kb.sb("w2_%d" % i, [128, 2, 64], BF16, ph) for i in range(2)]
                posT = kb.sb("posTkv", [128, 32], BF16, ph)
                hid = [kb.sb("hid%d" % i, [128, 2, 256], BF16, ph) for i in range(2)]
                psH = [kb.ps("psH%d" % i, [128, 512], F32, ph) for i in range(2)]
                psB = kb.ps("psB", [128, 512], F32, ph)
                psO = kb.ps("psO", [128, 512], F32, ph)
                for wi, nm in enumerate(("k", "v")):
                    dma("pool", w1kv[wi * 64:wi * 64 + 64], D["w1" + nm].rearrange("(l d) c -> d l c", d=64), W=[("w1", wi)])
                    dma("pool", w2[wi][:], D["w2" + nm].rearrange("(cc p) d -> p cc d", p=128), W=[("w2", wi)])
                    dma("pool", posT[wi * 64:wi * 64 + 64, :], D["posT" + nm], W=[("posT", wi)])
                for wi in range(2):
                    p0 = wi * 64
                    skeys = [("kvc_lo" if wi == 0 else "kvc_hi", m) for m in range(8)]
                    for cc in range(2):
                        for l in range(32):
                            op("pe", lambda e, l=l, cc=cc: e.matmul(psB[:, wi * 2 + cc:wi * 2 + cc + 1], w1kv[p0:p0 + 64, l, cc * 128:(cc + 1) * 128],
                                                                    posT[p0:p0 + 64, l:l + 1], start=(l == 0 and cc == 0 and wi == 0), stop=(l == 31),
                                                                    skip_group_check=True),
                               R=[("w1", wi), ("posT", wi)], W=["psB"])
                    op("dve", lambda e: e.tensor_copy(out=cbias[:, wi, :], in_=psB[:, wi * 2:wi * 2 + 2]), R=["psB"], W=[("cbias", wi)])
                    for cc in range(2):
                        for l in range(32):
                            op("pe", lambda e, l=l, cc=cc: e.matmul(psH[cc][:, 0:255], w1kv[p0:p0 + 64, l, cc * 128:(cc + 1) * 128],
                                                                    kvc[p0:p0 + 64, l:l + 16 * 254 + 1:16], start=(l == 0), stop=(l == 31)),
                               R=[("w1", wi)] + skeys, W=[("psH", cc)])
                        op("act", lambda e, cc=cc: e.activation(out=hid[wi][:, cc, 0:255], in_=psH[cc][:, 0:255], func=AF.Gelu_apprx_tanh,
                                                                bias=cbias[:, wi, cc:cc + 1]),
                           R=[("psH", cc), ("cbias", wi)], W=[("hid", wi, cc)])
                for cc in range(2):
                    op("pe", lambda e, cc=cc: e.matmul(psO[0:64, 0:255], w2[0][:, cc, :], hid[0][:, cc, 0:255], start=(cc == 0), stop=(cc == 1)),
                       R=[("w2", 0), ("hid", 0, cc)], W=["psO"])
                op("dve", lambda e: e.tensor_copy(out=kc_aug[0:64, 0:255], in_=psO[0:64, 0:255]), R=["psO", "kc_pad"], W=["kc_aug"])
                for ch in range(2):
                    ncol = 128 if ch == 0 else 127
                    for cc in range(2):
                        op("pe", lambda e, cc=cc: e.matmul(psO[0:ncol, 256 + ch * 64:256 + ch * 64 + 64], hid[1][:, cc, ch * 128:ch * 128 + ncol],
                                                           w2[1][:, cc, :], start=(cc == 0), stop=(cc == 1), skip_group_check=True),
                           R=[("w2", 1), ("hid", 1, cc)], W=["psO"])
                op("pool", lambda e: e.memset(vcx[:, 1, 0:64], 0.0), W=["vcx_pad"])
                op("dve", lambda e: e.tensor_copy(out=vcx[:, 0, 0:64], in_=psO[:, 256:320]), R=["psO", "vcx_c"], W=["vcx0"])
                op("dve", lambda e: e.tensor_copy(out=vcx[0:127, 1, 0:64], in_=psO[0:127, 320:384]), R=["psO", "vcx_pad", "vcx_c"], W=["vcx1"])
                if debug:
                    dma("sp", D["d_kc"][:, g, :], kc_aug[0:64, :], R=["kc_aug", "kc_pad"], W=[("d_kc", g)])
                    dma("sp", D["d_vc"][:, g, :, :], vcx[:, :, 0:64], R=["vcx0", "vcx1"], W=[("d_vc", g)])
                kb.barrier()
                if stop == "G2":
                    return

            with ExitStack() as ph:
                _attention(nc, kb, D, debug, ph, g, locals())
                kb.barrier()
                if stop == "G3":
                    dma("sp", D["d_onsa"], o_nsa[:], R=[], W=["d_onsa"])
                    dma("sp", D["d_gates"], gates[:], R=[], W=["d_gates"])
                    return
        if debug:
            dma("sp", D["d_onsa"], o_nsa[:], R=[], W=["d_onsa"])
            dma("sp", D["d_gates"], gates[:], R=[], W=["d_gates"])
            kb.barrier()

    with ExitStack() as ph:
        o_nsaT = kb.sb("o_nsaT", [128, 8, 1024], BF16, ph)
        o_convT = kb.sb("o_convT2", [128, 8, 1024], BF16, ph)
        dma("sp", o_convT[:].rearrange("p a b -> p (a b)"), conv_scr, W=["o_convT2"])
        woutc = [kb.sb("woutc%d" % i, [128, 16, 512], BF16, ph) for i in range(2)]
        g1 = kb.sb("ln1g", [128, 2048], F32, ph)
        b1 = kb.sb("ln1b", [128, 2048], F32, ph)
        xo = [kb.sb("xo%d" % i, [128, 512], F32, ph) for i in range(2)]
        stats = kb.sb("stats", [128, 4, 6], F32, ph)
        mv = kb.sb("mv", [128, 2], F32, ph)
        rs = kb.sb("rs", [128, 1], F32, ph)
        psT = [kb.ps("psT%d" % i, [128, 512], BF16, ph) for i in range(2)]
        psM = [kb.ps("psM%d" % i, [128, 512], F32, ph) for i in range(2)]
        wo_d = D["wout"].rearrange("(fc p) c -> p fc c", p=128)
        dma("sp", g1[:], D["ln1g"], W=["ln1g"])
        dma("sp", b1[:], D["ln1b"], W=["ln1b"])
        k = 0
        for m in range(8):
            for fc in range(8):
                pb = k % 2
                k += 1
                op("pe", lambda e, fc=fc: e.transpose(psT[pb][:, 0:128], o_nsa[:, m, fc * 128:(fc + 1) * 128], identb[:]),
                   R=[("o_nsa", m), "identb"], W=[("psT", pb)])
                if fc % 2 == 0:
                    op("act", lambda e, fc=fc: e.activation(out=o_nsaT[:, fc, m * 128:(m + 1) * 128], in_=psT[pb][:, 0:128], func=AF.Identity),
                       R=[("psT", pb)], W=[("o_nsaT", m)])
                else:
                    op("dve", lambda e, fc=fc: e.tensor_copy(out=o_nsaT[:, fc, m * 128:(m + 1) * 128], in_=psT[pb][:, 0:128]),
                       R=[("psT", pb)], W=[("o_nsaT", m)])
        k = 0
        for cc in range(4):
            wb = cc % 2
            dma("pool", woutc[wb][:], wo_d[:, :, cc * 512:(cc + 1) * 512], W=[("woutc", wb)])
            for m in range(8):
                pb = k % 2
                k += 1
                dma("sp", xo[pb][:], D["x_own"][m * 128:(m + 1) * 128, cc * 512:(cc + 1) * 512], W=[("xo", pb)])
                for fc in range(16):
                    src = o_nsaT[:, fc, m * 128:(m + 1) * 128] if fc < 8 else o_convT[:, fc - 8, m * 128:(m + 1) * 128]
                    op("pe", lambda e, fc=fc, src=src: e.matmul(psM[pb][:], src, woutc[wb][:, fc, :], start=(fc == 0), stop=(fc == 15)),
                       R=[("o_nsaT", m), ("woutc", wb), "o_convT2"], W=[("psM", pb)])
                op("dve", lambda e: e.scalar_tensor_tensor(out=y[:, m, cc * 512:(cc + 1) * 512], in0=xo[pb][:], scalar=DN_ALPHA, in1=psM[pb][:],
                                                           op0=ALU.mult, op1=ALU.add),
                   R=[("xo", pb), ("psM", pb)], W=[("y", m)])
                if cc == 3:
                    _layer_norm(kb, y[:, m, :], y[:, m, :], g1, b1, stats, mv, rs, [("y", m)], [("y", m)], ["ln1g", "ln1b"], epsc)
        kb.barrier()

    pA.close()
    if stop == "O":
        for m in range(8):
            dma("sp", D["out"][m * 128:(m + 1) * 128, :], y[:, m, :], R=[("y", m)], W=[("out", m)])
        return
    _peer(nc, kb, D, debug, stop, y, ident, identb, epsc, nch)


def _peer_only(nc, kb, D, debug, stop, nch):
    op, dma = kb.op, kb.dma
    ident = kb.sb("ident", [128, 128], F32)
    identb = kb.sb("identb", [128, 128], BF16)
    epsc = kb.sb("epsc", [128, 1], F32)
    y = kb.sb("y", [128, 8, 2048], F32)
    dma("sp", ident[:], D["ident"], W=["ident"])
    dma("pool", identb[:], D["ident"], W=["identb"])
    op("dve", lambda e: e.memset(epsc[:], LN_EPS), W=["epsc"])
    for m in range(8):
        dma("sp", y[:, m, :], D["y_in"][m * 128:(m + 1) * 128, :], W=[("y", m)])
    _peer(nc, kb, D, debug, stop, y, ident, identb, epsc, nch)


def _peer(nc, kb, D, debug, stop, y, ident, identb, epsc, nch):
    op, dma = kb.op, kb.dma
    with ExitStack() as pp:
        yT = kb.sb("yT", [128, 16, 1024], BF16, pp)
        S2 = kb.sb("S2", [128, 8, 8, 128], F32, pp)
        c3 = kb.sb("c3", [128, 8, 8], F32, pp)
        ec3 = kb.sb("ec3", [128, 8, 8], F32, pp)
        S1d = nc.dram_tensor("S1d", [128, 128, 64], F32, kind="Internal").ap()
        ykeys = [("y", m) for m in range(8)]
        with ExitStack() as ph:
            S1 = kb.sb("S1", [128, 8, 8, 128], F32, ph)
            s1stg = kb.sb("s1stg", [128, 16, 64], F32, ph)
            qc = [kb.sb("qc%d" % i, [128, 1024], BF16, ph) for i in range(2)]
            wqc = [kb.sb("wqc%d" % i, [128, 16, 128], BF16, ph) for i in range(2)]
            keysT = kb.sb("keysT", [128, 16, 128], BF16, ph)
            v32 = kb.sb("v32", [128, 8, 32], F32, ph)
            tmpab = [kb.sb("tmpab%d" % i, [128, 256], F32, ph) for i in range(8)]
            tmpa = [t[:, 0:128] for t in tmpab]
            tmpb = [t[:, 128:256] for t in tmpab]
            cand = [kb.sb("cand%d" % i, [128, 256], F32, ph) for i in range(8)]
            cand2 = tmpab
            t24 = kb.sb("t24", [128, 8, 8, 24], F32, ph)
            d16 = kb.sb("d16", [128, 8, 16], F32, ph)
            sm = kb.sb("sm", [128, 8, 8], F32, ph)
            psA = [kb.ps("ppA%d" % i, [128, 512], F32, ph) for i in range(2)]
            psS = [kb.ps("ppS%d" % i, [128, 1024], F32, ph) for i in range(2)]
            dma("pool", keysT[:].rearrange("p a b -> p (a b)"), D["keysT"], W=["keysT"])
            k = 0
            for m in range(8):
                for dc4 in range(4):
                    pb = k % 2
                    k += 1
                    for i in range(4):
                        dc = dc4 * 4 + i
                        op("pe", lambda e, i=i, dc=dc: e.transpose(psA[pb][:, i * 128:(i + 1) * 128], y[:, m, dc * 128:(dc + 1) * 128], ident[:]),
                           R=[("y", m), "ident"], W=[("ppA", pb)])
                    dst = yT[:, dc4 * 4:dc4 * 4 + 4, m * 128:(m + 1) * 128]
                    if k % 2 == 0:
                        op("act", lambda e: e.activation(out=dst, in_=psA[pb][:].rearrange("p (a b) -> p a b", a=4), func=AF.Identity),
                           R=[("ppA", pb)], W=[("yT", m)])
                    else:
                        op("dve", lambda e: e.tensor_copy(out=dst, in_=psA[pb][:].rearrange("p (a b) -> p a b", a=4)),
                           R=[("ppA", pb)], W=[("yT", m)])
            yTk = [("yT", m) for m in range(8)]
            for m in range(8):
                op("act", lambda e: e.activation(out=y[:, m, :], in_=y[:, m, :], func=AF.Identity, scale=DN_ALPHA),
                   R=[("y", m)], W=[("y", m)])
            k = 0
            for c16 in range(16):
                wb = c16 % 2
                dma("pool", wqc[wb][:].rearrange("p a b -> p (a b)"), D["wq_l"][c16], W=[("wqc", wb)])
                for hf in range(2):
                    pb = k % 2
                    k += 1
                    for dc in range(16):
                        op("pe", lambda e, dc=dc: e.matmul(psA[pb][:], wqc[wb][:, dc, :], yT[:, dc, hf * 512:(hf + 1) * 512],
                                                           start=(dc == 0), stop=(dc == 15)),
                           R=[("wqc", wb)] + yTk, W=[("ppA", pb)])
                    op("act", lambda e: e.activation(out=qc[wb][:, hf * 512:(hf + 1) * 512], in_=psA[pb][:], func=AF.Identity),
                       R=[("ppA", pb)], W=[("qc", wb)])
                for m in range(8):
                    op("pe", lambda e, m=m: e.matmul(psS[wb][:, m * 128:(m + 1) * 128], qc[wb][:, m * 128:(m + 1) * 128], keysT[:, c16, :],
                                                     start=(m % 4 == 0), stop=True, skip_group_check=True),
                       R=[("qc", wb), "keysT"], W=[("ppS", wb)])
                S, sn = (S1, "S1") if c16 % 2 == 0 else (S2, "S2")
                hh = c16 // 2
                op("act", lambda e: e.activation(out=S[:, :, hh, :], in_=psS[wb][:].rearrange("p (a b) -> p a b", a=8), func=AF.Identity),
                   R=[("ppS", wb)], W=[(sn, hh)])
                if c16 % 2 == 1:
                    h = hh
                    chains = []
                    for m in range(8):
                        c = m
                        ta, tb, ca, cb = tmpa[c], tmpb[c], cand[c], cand2[c]
                        vv = v32[:, m, :]
                        tt = t24[:, m, h, :]
                        steps = [
                            lambda m=m, vv=vv, c=c: op("dve", lambda e: e.max(out=vv[:, 0:8], in_=S1[:, m, h, :]), R=[("S1", h)], W=[("v32a", c)]),
                            lambda m=m, vv=vv, c=c: op("dve", lambda e: e.max(out=vv[:, 16:24], in_=S2[:, m, h, :]), R=[("S2", h)], W=[("v32b", c)]),
                            lambda m=m, vv=vv, ta=ta, c=c: op("dve", lambda e: e.match_replace(out=ta, in_to_replace=vv[:, 0:8], in_values=S1[:, m, h, :], imm_value=-1e30),
                                                              R=[("S1", h), ("v32a", c)], W=[("tmpa", c)]),
                            lambda m=m, vv=vv, tb=tb, c=c: op("dve", lambda e: e.match_replace(out=tb, in_to_replace=vv[:, 16:24], in_values=S2[:, m, h, :], imm_value=-1e30),
                                                              R=[("S2", h), ("v32b", c)], W=[("tmpb", c)]),
                            lambda vv=vv, ta=ta, c=c: op("dve", lambda e: e.max(out=vv[:, 8:16], in_=ta), R=[("tmpa", c)], W=[("v32c", c)]),
                            lambda vv=vv, tb=tb, c=c: op("dve", lambda e: e.max(out=vv[:, 24:32], in_=tb), R=[("tmpb", c)], W=[("v32d", c)]),
                            lambda vv=vv, ca=ca, c=c: op("dve", lambda e: e.tensor_tensor(out=ca[:].rearrange("p (a b) -> p a b", a=16), in0=bc(vv[:, 0:16], 16, axis=2),
                                                                                          in1=bc(vv[:, 16:32], 16, axis=1), op=ALU.add),
                                                         R=[("v32a", c), ("v32b", c), ("v32c", c), ("v32d", c)], W=[("cand", c)]),
                            lambda tt=tt, ca=ca, c=c, m=m: op("dve", lambda e: e.max(out=tt[:, 0:8], in_=ca[:]), R=[("cand", c)], W=[("t24a", m, h)]),
                            lambda tt=tt, ca=ca, cb=cb, c=c, m=m: op("dve", lambda e: e.match_replace(out=cb[:], in_to_replace=tt[:, 0:8], in_values=ca[:], imm_value=-1e30),
                                                                     R=[("cand", c), ("t24a", m, h)], W=[("cand2", c)]),
                            lambda tt=tt, cb=cb, c=c, m=m: op("dve", lambda e: e.max(out=tt[:, 8:16], in_=cb[:]), R=[("cand2", c)], W=[("t24b", m, h)]),
                            lambda tt=tt, ca=ca, cb=cb, c=c, m=m: op("dve", lambda e: e.match_replace(out=ca[:], in_to_replace=tt[:, 8:16], in_values=cb[:], imm_value=-1e30),
                                                                     R=[("cand2", c), ("t24b", m, h)], W=[("cand", c)]),
                            lambda tt=tt, ca=ca, c=c, m=m: op("dve", lambda e: e.max(out=tt[:, 16:24], in_=ca[:]), R=[("cand", c)], W=[("t24c", m, h)]),
                        ]
                        chains.append(steps)
                    for si in range(len(chains[0])):
                        for c in range(8):
                            chains[c][si]()
            S1k = [("S1", h) for h in range(8)]
            S2k = [("S2", h) for h in range(8)]
            for m in range(8):
                t24k = [(k_, m, h) for k_ in ("t24a", "t24b", "t24c") for h in range(8)]
                op("dve", lambda e: e.tensor_copy(out=sm[:, 0, :], in_=t24[:, m, :, 0]), R=t24k, W=["sm0"])
                op("dve", lambda e: e.tensor_tensor(out=d16[:], in0=t24[:, m, :, 0:16], in1=bc(sm[:, 0, :], 16, axis=2), op=ALU.subtract),
                   R=t24k + ["sm0"], W=["d16"])
                op("act", lambda e: e.activation(out=d16[:], in_=d16[:], func=AF.Exp), R=["d16"], W=["d16"])
                op("dve", lambda e: e.tensor_reduce(out=sm[:, 1, :], in_=d16[:], axis=AX.X, op=ALU.add), R=["d16"], W=["sm1"])
                op("act", lambda e: e.activation(out=sm[:, 2, :], in_=sm[:, 1, :], func=AF.Ln), R=["sm1"], W=["sm2"])
                op("dve", lambda e: e.tensor_tensor(out=sm[:, 3, :], in0=sm[:, 0, :], in1=sm[:, 2, :], op=ALU.add), R=["sm0", "sm2"], W=["sm3"])
                op("dve", lambda e: e.tensor_tensor(out=sm[:, 4, :], in0=t24[:, m, :, 15], in1=t24[:, m, :, 16], op=ALU.add), R=t24k, W=["sm4"])
                op("dve", lambda e: e.scalar_tensor_tensor(out=c3[:, m, :], in0=sm[:, 4, :], scalar=0.5, in1=sm[:, 3, :], op0=ALU.mult, op1=ALU.subtract),
                   R=["sm4", "sm3"], W=[("c3", m)])
                op("dve", lambda e: e.tensor_scalar(out=sm[:, 5, :], in0=sm[:, 4, :], scalar1=0.5, scalar2=None, op0=ALU.mult), R=["sm4"], W=["sm5"])
                op("dve", lambda e: e.tensor_tensor(out=S2[:, m, :, :], in0=S2[:, m, :, :], in1=bc(sm[:, 5, :], 128, axis=2), op=ALU.subtract),
                   R=S2k + ["sm5"], W=[("S2f", m)])
                op("act", lambda e: e.activation(out=S2[:, m, :, :].rearrange("p a b -> p (a b)"), in_=S2[:, m, :, :].rearrange("p a b -> p (a b)"), func=AF.Exp),
                   R=[("S2f", m)], W=[("S2f", m)])
            op("act", lambda e: e.activation(out=ec3[:], in_=c3[:], func=AF.Exp), R=[("c3", m) for m in range(8)], W=["ec3"])
            if debug:
                dma("sp", D["d_S1"], S1[:].rearrange("p a b c -> p (a b c)"), R=S1k, W=["d_S1"])
            for m in range(8):
                op("act", lambda e: e.activation(out=S1[:, m, :, :].rearrange("p a b -> p (a b)"), in_=S1[:, m, :, :].rearrange("p a b -> p (a b)"), func=AF.Exp),
                   R=S1k + ["d_S1"], W=S1k)
            S1v = S1[:].rearrange("p m h i -> p i (m h)")
            for ib in range(8):
                op("act", lambda e: e.activation(out=s1stg[:], in_=S1v[:, ib * 16:(ib + 1) * 16, :], func=AF.Identity), R=S1k, W=["s1stg"])
                dma("sp", S1d[:, ib * 16:(ib + 1) * 16, :], s1stg[:], R=["s1stg"], W=[("S1d", ib)])
            if debug:
                dma("sp", D["d_S2"], S2[:].rearrange("p a b c -> p (a b c)"), R=[("S2f", m) for m in range(8)], W=["d_S2"])
                dma("sp", D["d_c3"], c3[:].rearrange("p a b -> p (a b)"), R=[("c3", m) for m in range(8)], W=["d_c3"])
            kb.barrier()
        if stop == "P4":
            return
        NCH = nch
        GC = 2
        with ExitStack() as ph:
            UT = [kb.sb("UT%d" % i, [128, 16, 128], BF16, ph) for i in range(2)]
            Vg = [kb.sb("Vg%d" % i, [128, GC, 2048], BF16, ph) for i in range(2)]
            GH = [kb.sb("GH%d" % i, [128, GC, 1024], BF16, ph) for i in range(2)]
            NRB = 3
            POOL_TILES = (2, 6)
            EtB = [kb.sb("EtB%d" % i, [128, 8, 128], F32, ph) for i in range(NRB)]
            NRM = 3
            mkB = [kb.sb("mkB%d" % i, [128, 8, 128], BF16, ph) for i in range(NRM)]
            s1c = [kb.sb("s1c%d" % i, [128, 64], F32, ph) for i in range(3)]
            Dg = kb.sb("Dg", [128, 8, 8, 128], BF16, ph)
            for m in range(8):
                for h in range(8):
                    op("pool", lambda e: e.tensor_scalar(out=Dg[:, m, h, :], in0=identb[:], scalar1=ec3[:, m, h:h + 1], scalar2=None, op0=ALU.mult),
                       R=[], W=[("Dg", m)])
            psH = kb.ps("psH", [128, 1024], F32, ph)
            psG = kb.ps("psG", [128, 1024], F32, ph)
            psY = [kb.ps("psY%d" % i, [128, 1024], F32, ph) for i in range(2)]
            yTk = [("yT", m) for m in range(8)]
            skeys = [("S1", m) for m in range(8)] + [("S2", m) for m in range(8)] + [("c3", m) for m in range(8)]

            def load_u(i):
                dma("pool", UT[i % 2][:].rearrange("p a b -> p (a b)"), D["UT_l"][i], W=[("UT", i % 2)])

            def load_v(i):
                gi = i // GC
                dma("pool", Vg[gi % 2][:, i % GC, :], D["V"][i * 128:(i + 1) * 128, :], W=[("Vg", gi % 2, i % GC)])

            def a_part(i, p):
                hf = p // 4
                for dc in range((p % 4) * 4, (p % 4) * 4 + 4):
                    op("pe", lambda e, dc=dc: e.matmul(psH[:, hf * 512:(hf + 1) * 512], UT[i % 2][:, dc, :], yT[:, dc, hf * 512:(hf + 1) * 512],
                                                       start=(dc == 0), stop=(dc == 15)),
                       R=[("UT", i % 2)] + yTk, W=["psH"])
                if p == 7:
                    op("act", lambda e: e.activation(out=GH[(i // GC) % 2][:, i % GC, :], in_=psH[:], func=AF.Gelu_apprx_tanh),
                       R=["psH"], W=[("GH", (i // GC) % 2, i % GC)])

            cnt = {"r": 0, "y": 0}

            def load_s1(i):
                dma("sp", s1c[i % 3][:], S1d[:, i, :], R=[("S1d", ib) for ib in range(8)], W=[("s1c", i % 3)])

            def b_front(i, m):
                re = cnt["r"] % NRB
                r = cnt["r"] % NRM
                cnt["r"] += 1
                ek = [("EtB", re, h) for h in range(8)]
                if m in POOL_TILES:
                    op("pool", lambda e: e.tensor_tensor(out=EtB[re][:], in0=S2[:, m, :, :], in1=bc(s1c[i % 3][:, m * 8:(m + 1) * 8], 128, axis=2), op=ALU.mult),
                       R=[("s1c", i % 3)], W=ek)
                else:
                    for h in range(8):
                        op("act", lambda e: e.activation(out=EtB[re][:, h, :], in_=S2[:, m, h, :], func=AF.Identity, scale=s1c[i % 3][:, m * 8 + h:m * 8 + h + 1]),
                           R=[("s1c", i % 3)], W=[("EtB", re, h)])
                ef = EtB[re][:].rearrange("p a b -> p (a b)")
                op("dve", lambda e: e.scalar_tensor_tensor(out=mkB[r][:].rearrange("p a b -> p (a b)"), in0=ef, scalar=1.0, in1=ef,
                                                           op0=ALU.is_ge, op1=ALU.mult),
                   R=ek, W=[("mkB", r)] + ek)
                return r

            def b_back(i, m, r):
                for h in range(8):
                    op("pe", lambda e, h=h: e.matmul(psG[:, m * 128:(m + 1) * 128], mkB[r][:, h, :], Dg[:, m, h, :],
                                                     start=(h == 0 and m % 4 == 0), stop=(h == 7), skip_group_check=True),
                       R=[("mkB", r), ("Dg", m)], W=["psG"])

            def gh(i):
                gi = i // GC
                for hf in range(2):
                    dst = GH[gi % 2][:, i % GC, hf * 512:(hf + 1) * 512]
                    op("dve", lambda e: e.tensor_tensor(out=dst, in0=psG[:, hf * 512:(hf + 1) * 512], in1=dst, op=ALU.mult),
                       R=["psG", ("GH", gi % 2, i % GC)], W=[("GH", gi % 2, i % GC)])

            def c_unit(gi, u):
                m, hf = u // 2, u % 2
                yb = cnt["y"] % 2
                cnt["y"] += 1
                for ci in range(GC):
                    for c2 in range(2):
                        col = hf * 1024 + c2 * 512
                        op("pe", lambda e, ci=ci, c2=c2, col=col: e.matmul(psY[yb][:, c2 * 512:(c2 + 1) * 512], GH[gi % 2][:, ci, m * 128:(m + 1) * 128],
                                                                           Vg[gi % 2][:, ci, col:col + 512], start=(ci == 0), stop=(ci == GC - 1)),
                           R=[("GH", gi % 2, c) for c in range(GC)] + [("Vg", gi % 2, c) for c in range(GC)], W=[("psY", yb)])
                op("dve", lambda e: e.tensor_tensor(out=y[:, m, hf * 1024:(hf + 1) * 1024], in0=psY[yb][:], in1=y[:, m, hf * 1024:(hf + 1) * 1024], op=ALU.add),
                   R=[("psY", yb), ("y", m)], W=[("y", m)])

            load_u(0); load_u(1)
            for i in range(2 * GC):
                load_v(i)
            load_s1(0)
            if NCH > 1:
                load_s1(1)
            for p in range(8):
                a_part(0, p)
            queue = []
            for i in range(NCH):
                if i + 2 < NCH:
                    load_u(i + 2)
                    load_s1(i + 2)
                plan = [1] * 8 if i % 2 == 0 else [2, 2, 2, 2, 0, 0, 0, 0]
                for m in range(8):
                    r = b_front(i, m)
                    if i + 1 < NCH:
                        a_part(i + 1, m)
                    for _ in range(plan[m]):
                        if queue:
                            c_unit(*queue.pop(0))
                    b_back(i, m, r)
                gh(i)
                if i % GC == GC - 1:
                    queue += [(i // GC, u) for u in range(16)]
                    g_next = i // GC + 1
                    if g_next >= 2:
                        for ii in range(g_next * GC, (g_next + 1) * GC):
                            if ii < NCH:
                                load_v(ii)
            while queue:
                c_unit(*queue.pop(0))
            kb.barrier()
        if debug:
            for m in range(8):
                dma("sp", D["d_acc"][m * 128:(m + 1) * 128, :], y[:, m, :], R=[("y", m)], W=[("d_acc", m)])
        with ExitStack() as ph:
            g2 = kb.sb("ln2g", [128, 2048], F32, ph)
            b2 = kb.sb("ln2b", [128, 2048], F32, ph)
            stats = kb.sb("stats2", [128, 4, 6], F32, ph)
            mv = kb.sb("mv2", [128, 2], F32, ph)
            rs = kb.sb("rs2", [128, 1], F32, ph)
            dma("sp", g2[:], D["ln2g"], W=["ln2g"])
            dma("sp", b2[:], D["ln2b"], W=["ln2b"])
            for m in range(8):
                _layer_norm(kb, y[:, m, :], y[:, m, :], g2, b2, stats, mv, rs, [("y", m)], [("y", m)], ["ln2g", "ln2b"], epsc)
                dma("sp", D["out"][m * 128:(m + 1) * 128, :], y[:, m, :], R=[("y", m)], W=[("out", m)])
            kb.wait_keys("sp", [("out", m) for m in range(8)])
            kb.barrier()


def _layer_norm(kb, z, out_ap, g_t, b_t, stats, mv, rs, zkeys, okeys, gkeys, epsc):
    op = kb.op
    for c4 in range(4):
        op("dve", lambda e, c4=c4: e.bn_stats(out=stats[:, c4, :], in_=z[:, c4 * 512:(c4 + 1) * 512]), R=zkeys, W=["stats"])
    op("dve", lambda e: e.bn_aggr(out=mv[:], in_=stats[:].rearrange("p a b -> p (a b)")), R=["stats"], W=["mv"])
    op("act", lambda e: e.activation(out=rs[:], in_=mv[:, 1:2], func=AF.Sqrt, bias=epsc[:, 0:1]), R=["mv", "epsc"], W=["rs"])
    op("dve", lambda e: e.reciprocal(out=rs[:], in_=rs[:]), R=["rs"], W=["rs"])
    op("dve", lambda e: e.tensor_scalar(out=mv[:, 1:2], in0=mv[:, 0:1], scalar1=rs[:, 0:1], scalar2=-1.0, op0=ALU.mult, op1=ALU.mult),
       R=["mv", "rs"], W=["mv"])
    op("act", lambda e: e.activation(out=z, in_=z, func=AF.Identity, scale=rs[:, 0:1], bias=mv[:, 1:2]), R=zkeys + ["mv", "rs"], W=zkeys)
    op("dve", lambda e: e.tensor_tensor(out=z, in0=z, in1=g_t[:], op=ALU.mult), R=zkeys + gkeys, W=zkeys)
    op("pool", lambda e: e.tensor_tensor(out=out_ap, in0=z, in1=b_t[:], op=ALU.add), R=zkeys + gkeys, W=okeys)


def _attention(nc, kb, D, debug, ph, g, L):
    op, dma = kb.op, kb.dma
    kT_sel, kT_win, V_sw, qT, kc_aug, vcx = L["kT_sel"], L["kT_win"], L["V_sw"], L["qT"], L["kc_aug"], L["vcx"]
    identb, ident, expand, tri_lo, tri_up = L["identb"], L["ident"], L["expand"], L["tri_lo"], L["tri_up"]
    maskc, winmask0, vis, cstb, o_nsa, gates = L["maskc"], L["winmask0"], L["vis"], L["cstb"], L["o_nsa"], L["gates"]
    NE = 4
    Eb = [kb.sb("Eb%d" % i, [128, 512], BF16, ph) for i in range(NE)]
    NS = 3
    psS = [kb.ps("psS%d" % i, [128, 512], F32, ph) for i in range(NS)]
    pc = kb.ps("pc", [128, 4, 256], F32, ph)
    psel = kb.ps("psel", [128, 512], F32, ph)
    pwin = kb.ps("pwin", [128, 512], F32, ph)
    pT = kb.ps("pT", [128, 512], F32, ph)
    zc = kb.sb("zc", [128, 12], F32, ph)
    rz = kb.sb("rz", [128, 12], F32, ph)
    coef = kb.sb("coef", [128, 12], F32, ph)
    imp = kb.sb("imp", [128, 64], F32, ph)
    impa = kb.sb("impa", [128, 64], F32, ph)
    imp2 = kb.sb("imp2", [128, 64], F32, ph)
    m8 = kb.sb("m8", [128, 16], F32, ph)
    nsel = kb.sb("nsel", [128, 64], F32, ph)
    nselT = kb.sb("nselT", [64, 128], BF16, ph)
    oacc = kb.sb("oacc", [128, 4, 64], F32, ph)
    kaug_keys_s = ["kT_sel_aug"] + [("kT_sel", m) for m in range(8)]
    kaug_keys_w = ["kT_win_aug"] + [("kT_win", m) for m in range(8)]
    st = {"s": 0, "e": 0}

    def emit_score(job):
        mm_list, nrow = job["mm"], job.get("nrow", 128)
        sb_ = st["s"] % NS
        st["s"] += 1
        eb = st["e"] % NE
        st["e"] += 1
        n = len(mm_list)
        for i, (lh, rh, rk) in enumerate(mm_list):
            op("pe", lambda e, lh=lh, rh=rh, i=i: e.matmul(psS[sb_][0:nrow, :] if len(rh.shape) == 2 else
                                                           psS[sb_][0:nrow, :].rearrange("p (a b) -> p a b", a=4),
                                                           lh, rh, start=(i == 0), stop=(i == n - 1)),
               R=rk, W=[("psS", sb_)])
        op("act", lambda e: e.activation(out=Eb[eb][0:nrow, :], in_=psS[sb_][0:nrow, :], func=AF.Exp, scale=0.125),
           R=[("psS", sb_)], W=[("Eb", eb)])
        return (job, eb)

    def run_jobs(jobs):
        pending = []
        for job in jobs:
            pending.append(emit_score(job))
            if len(pending) > 2:
                j_, eb_ = pending.pop(0)
                j_["pv"](eb_)
        for j_, eb_ in pending:
            j_["pv"](eb_)

    for m in range(8):
        J = 4 * m + 3
        qrhs = qT[0:69, m, :, :].rearrange("p a b -> p (a b)")
        qk = [("qT", m), "qT_aug"]
        vkeys = ["V_ones"] + [("V_sw", i) for i in range(8)]
        nch = 1 if 8 * J + 6 < 128 else 2
        jobs = []
        for ch in range(nch):
            nr = 128 if ch == 0 else 127

            def pv_c(eb, ch=ch, nr=nr):
                for hh in range(4):
                    op("pe", lambda e, hh=hh: e.matmul(pc[:, hh, 0:129], Eb[eb][0:nr, hh * 128:(hh + 1) * 128], vcx[0:nr, ch, :],
                                                       start=(ch == 0 and hh % 2 == 0), stop=(ch == nch - 1), skip_group_check=True),
                       R=[("Eb", eb), "vcx0", "vcx1", "vcx_c", "vcx_pad"], W=["pc"])
            jobs.append(dict(mm=[
                (kc_aug[0:69, ch * 128:ch * 128 + nr], qrhs, qk + ["kc_aug", "kc_aug_aug", "kc_pad"]),
                (identb[0:nr, 0:nr], bc(maskc[0:nr, m, ch, :], 4), ["identb", "maskc"]),
            ], nrow=nr, pv=pv_c))
        run_jobs(jobs)
        op("dve", lambda e: e.tensor_scalar(out=zc[:, 0:4], in0=pc[:, :, 64], scalar1=1e-30, scalar2=None, op0=ALU.max), R=["pc"], W=["zc0"])
        op("dve", lambda e: e.reciprocal(out=rz[:, 0:4], in_=zc[:, 0:4]), R=["zc0"], W=["rz0"])
        op("dve", lambda e: e.tensor_scalar(out=imp[:], in0=pc[:, 0, 65:129], scalar1=rz[:, 0:1], scalar2=None, op0=ALU.mult),
           R=["pc", "rz0"], W=["imp"])
        for hh in range(1, 4):
            op("dve", lambda e, hh=hh: e.scalar_tensor_tensor(out=imp[:], in0=pc[:, hh, 65:129], scalar=rz[:, hh:hh + 1], in1=imp[:],
                                                              op0=ALU.mult, op1=ALU.add), R=["pc", "rz0", "imp"], W=["imp"])
        op("dve", lambda e: e.tensor_tensor(out=impa[:], in0=imp[:], in1=vis[:, m, :], op=ALU.mult), R=["imp", "vis"], W=["impa"])
        op("dve", lambda e: e.tensor_tensor(out=impa[:], in0=impa[:], in1=cstb[:, m, :], op=ALU.add), R=["impa", "cstb"], W=["impa"])
        op("dve", lambda e: e.max(out=m8[:, 0:8], in_=impa[:]), R=["impa"], W=["m8a"])
        op("dve", lambda e: e.match_replace(out=imp2[:], in_to_replace=m8[:, 0:8], in_values=impa[:], imm_value=-1e30),
           R=["impa", "m8a"], W=["imp2"])
        op("dve", lambda e: e.max(out=m8[:, 8:16], in_=imp2[:]), R=["imp2"], W=["m8b"])
        op("dve", lambda e: e.tensor_scalar(out=nsel[:], in0=impa[:], scalar1=m8[:, 15:16], scalar2=None, op0=ALU.is_ge),
           R=["impa", "m8b"], W=["nsel"])
        op("dve", lambda e: e.tensor_scalar(out=nsel[:], in0=nsel[:], scalar1=-NEG, scalar2=NEG, op0=ALU.mult, op1=ALU.add),
           R=["nsel"], W=["nsel"])
        if debug:
            dma("sp", D["d_imp"][:, m, :], impa[:], R=["impa"], W=[("d_imp", m, g)])
            dma("sp", D["d_negsel"][:, m, :], nsel[:], R=["nsel"], W=[("d_negsel", m, g)])
        op("dve", lambda e: e.tensor_tensor(out=coef[:, 0:4], in0=rz[:, 0:4], in1=gates[:, m, 12 * g:12 * g + 12].rearrange("p (b a) -> p a b", a=3)[:, 0, :], op=ALU.mult),
           R=["rz0", ("gates", m, g)], W=["coef0"])
        for hh in range(4):
            op("dve", lambda e, hh=hh: e.tensor_scalar(out=oacc[:, hh, :], in0=pc[:, hh, 0:64], scalar1=coef[:, hh:hh + 1], scalar2=None, op0=ALU.mult),
               R=["pc", "coef0"], W=[("oacc", hh)])
        jobs = []
        wlist = [Dd for Dd in (4, 3, 2, 1, 0) if J - Dd >= 0]
        for Dd in wlist:
            kt = J - Dd
            mm = [(kT_win[0:69, kt * 128:(kt + 1) * 128], qrhs, qk + kaug_keys_w)]
            if m == 0:
                mm.append((identb[:], bc(winmask0[:, Dd, :], 4), ["identb", "winmask0"]))
            elif Dd == 0:
                mm.append((identb[:], bc(tri_lo[:], 4), ["identb", "tri_lo"]))
            elif Dd == 4:
                mm.append((identb[:], bc(tri_up[:], 4), ["identb", "tri_up"]))

            def pv_w(eb, kt=kt, Dd=Dd):
                for hh in range(4):
                    op("pe", lambda e, hh=hh: e.matmul(pwin[:, hh * 65:hh * 65 + 65], Eb[eb][:, hh * 128:(hh + 1) * 128], V_sw[:, kt, 1, :],
                                                       start=(Dd == wlist[0] and hh == 0), stop=(Dd == 0), skip_group_check=True),
                       R=[("Eb", eb)] + vkeys, W=["pwin"])
            jobs.append(dict(mm=mm, pv=pv_w))
        run_jobs(jobs)
        op("pe", lambda e: e.transpose(pT[0:64, 0:128], nsel[:], ident[:]), R=["nsel", "ident"], W=["pT"])
        op("dve", lambda e: e.tensor_copy(out=nselT[:], in_=pT[0:64, 0:128]), R=["pT"], W=["nselT"])
        jobs = []
        for kt in range(J + 1):
            mm = [(kT_sel[0:69, kt * 128:(kt + 1) * 128], qrhs, qk + kaug_keys_s),
                  (expand[:, kt, :], bc(nselT[:], 4), ["expand", "nselT"])]
            if kt == J:
                mm.append((identb[:], bc(tri_lo[:], 4), ["identb", "tri_lo"]))

            def pv_s(eb, kt=kt):
                for hh in range(4):
                    op("pe", lambda e, hh=hh: e.matmul(psel[:, hh * 65:hh * 65 + 65], Eb[eb][:, hh * 128:(hh + 1) * 128], V_sw[:, kt, 0, :],
                                                       start=(kt == 0 and hh == 0), stop=(kt == J), skip_group_check=True),
                       R=[("Eb", eb)] + vkeys, W=["psel"])
            jobs.append(dict(mm=mm, pv=pv_s))
        run_jobs(jobs)
        pselv = psel[:, 0:260].rearrange("p (a b) -> p a b", a=4)
        pwinv = pwin[:, 0:260].rearrange("p (a b) -> p a b", a=4)
        op("dve", lambda e: e.tensor_scalar(out=zc[:, 4:8], in0=pselv[:, :, 64], scalar1=1e-30, scalar2=None, op0=ALU.max), R=["psel"], W=["zc1"])
        op("dve", lambda e: e.tensor_scalar(out=zc[:, 8:12], in0=pwinv[:, :, 64], scalar1=1e-30, scalar2=None, op0=ALU.max), R=["pwin"], W=["zc2"])
        op("dve", lambda e: e.reciprocal(out=rz[:, 4:12], in_=zc[:, 4:12]), R=["zc1", "zc2"], W=["rz1"])
        op("dve", lambda e: e.tensor_tensor(out=coef[:, 4:12].rearrange("p (a b) -> p a b", a=2),
                                            in0=rz[:, 4:12].rearrange("p (a b) -> p a b", a=2),
                                            in1=gates[:, m, 12 * g:12 * g + 12].rearrange("p (b a) -> p a b", a=3)[:, 1:3, :], op=ALU.mult),
           R=["rz1", ("gates", m, g)], W=["coef"])
        for hh in range(4):
            op("dve", lambda e, hh=hh: e.scalar_tensor_tensor(out=oacc[:, hh, :], in0=pselv[:, hh, 0:64], scalar=coef[:, 4 + hh:5 + hh], in1=oacc[:, hh, :],
                                                              op0=ALU.mult, op1=ALU.add), R=["psel", "coef", ("oacc", hh)], W=[("oacc", hh)])
            op("dve", lambda e, hh=hh: e.scalar_tensor_tensor(out=o_nsa[:, m, g * 256 + hh * 64:g * 256 + hh * 64 + 64], in0=pwinv[:, hh, 0:64],
                                                              scalar=coef[:, 8 + hh:9 + hh], in1=oacc[:, hh, :], op0=ALU.mult, op1=ALU.add),
               R=["pwin", "coef", ("oacc", hh)], W=[("o_nsa", m)])


_CACHE = {}


def kernel(**inputs):
    maps = _prep_inputs(inputs)
    if "nc" not in _CACHE:
        _CACHE["nc"] = build(False)
    nc = _CACHE["nc"]
    res = run_bass_kernel_spmd(nc, maps, core_ids=list(range(8)))
    out = np.zeros((2, 4096, 2048), np.float32)
    for c in range(8):
        b, s = c // 4, c % 4
        r = res.results[c]["out"]
        for m in range(8):
            j = 4 * m + s
            out[b, j * 128:(j + 1) * 128, :] = r[m * 128:(m + 1) * 128, :]
    return out
```

```python
from contextlib import ExitStack
import numpy as np
import ml_dtypes
import concourse.bass as bass
import concourse.mybir as mybir
from concourse.bass_utils import run_bass_kernel_spmd

F32 = mybir.dt.float32
BF16 = mybir.dt.bfloat16
U32 = mybir.dt.uint32
AF = mybir.ActivationFunctionType
ALU = mybir.AluOpType
AX = mybir.AxisListType

NEG = -30000.0
LN_EPS = 1e-5
DN_ALPHA = 2.0 ** 0.25


class KB:
    NDMA = 12
    SEM_ROLL = 30000

    def __init__(self, nc):
        self.nc = nc
        self.st = ExitStack()
        self.E = dict(pe=nc.tensor, act=nc.scalar, dve=nc.vector, pool=nc.gpsimd, sp=nc.sync)
        self.esem = {}
        self.ecnt = {}
        self.nsem = 0
        for e in self.E:
            self._new_esem(e)
        self.dsem = {}
        self.dcnt = {}
        for q in ("sp", "pool"):
            self.dsem[q] = [self._sem() for _ in range(self.NDMA)]
            self.dcnt[q] = 0
        self.waited = {e: {} for e in self.E}
        self.lastw = {}
        self.readers = {}
        self.n_wait = 0
        self.n_ops = 0

    def _sem(self):
        self.nsem += 1
        return self.st.enter_context(self.nc.semaphore("ks%d" % self.nsem))

    def _new_esem(self, e):
        self.esem[e] = self._sem()
        self.ecnt[e] = 0

    def sb(self, name, shape, dt, st=None):
        self.nsem += 1
        name = "%s_%d" % (name, self.nsem)
        return (st or self.st).enter_context(self.nc.sbuf_tensor("s_" + name, list(shape), dt))

    def ps(self, name, shape, dt=F32, st=None):
        self.nsem += 1
        name = "%s_%d" % (name, self.nsem)
        return (st or self.st).enter_context(self.nc.psum_tensor("p_" + name, list(shape), dt))

    def _wait(self, eng, tok):
        sem, val, src = tok
        w = self.waited[eng]
        sid = id(sem)
        if w.get(sid, 0) >= val:
            return
        w[sid] = val
        self.E[eng].wait_ge(sem, val)
        self.n_wait += 1

    def _deps(self, eng, R, W, is_dma):
        for r in R:
            t = self.lastw.get(r)
            if t is not None:
                if is_dma or not (t[2] == eng and eng == "pe"):
                    self._wait(eng, t)
        for w_ in W:
            t = self.lastw.get(w_)
            if t is not None:
                if is_dma or not (t[2] == eng and eng == "pe"):
                    self._wait(eng, t)
            for t in self.readers.get(w_, ()):
                if is_dma or t[2] != eng:
                    self._wait(eng, t)

    def _record(self, tok, R, W):
        for r in R:
            lst = self.readers.setdefault(r, [])
            lst[:] = [t for t in lst if not (t[2] == tok[2] and t[0] is tok[0])]
            lst.append(tok)
        for w_ in W:
            self.lastw[w_] = tok
            self.readers[w_] = []

    def op(self, eng, fn, R=(), W=()):
        self._deps(eng, R, W, False)
        if self.ecnt[eng] >= self.SEM_ROLL:
            self._new_esem(eng)
        inst = fn(self.E[eng])
        self.ecnt[eng] += 1
        inst.then_inc(self.esem[eng], 1)
        tok = (self.esem[eng], self.ecnt[eng], eng)
        self._record(tok, R, W)
        self.n_ops += 1
        return tok

    def dma(self, q, out, in_, R=(), W=(), **kw):
        if out.dtype != in_.dtype:
            q = "pool"
        self._deps(q, R, W, True)
        k = self.dcnt[q]
        slot = k % self.NDMA
        sem = self.dsem[q][slot]
        val = 16 * (k // self.NDMA + 1)
        if val > 16:
            self._wait(q, (sem, val - 16, "dma_" + q))
        self.E[q].dma_start(out=out, in_=in_, **kw).then_inc(sem, 16)
        self.dcnt[q] += 1
        tok = (sem, val, "dma_" + q)
        self._record(tok, R, W)
        return tok

    def wait_keys(self, eng, keys):
        for k_ in keys:
            t = self.lastw.get(k_)
            if t is not None:
                self._wait(eng, t)

    def barrier(self):
        toks = []
        for e in self.E:
            if self.ecnt[e] > 0:
                toks.append((self.esem[e], self.ecnt[e], e))
        for q in self.dsem:
            k = self.dcnt[q]
            for slot in range(self.NDMA):
                n = (k - slot + self.NDMA - 1) // self.NDMA
                if n > 0:
                    toks.append((self.dsem[q][slot], 16 * n, "dma_" + q))
        for e in self.E:
            for t in toks:
                if t[2] == e and e == "pe":
                    continue
                self._wait(e, t)
        self.lastw.clear()
        self.readers.clear()


def bc(ap, n, axis=1):
    shp = list(ap.shape)
    shp.insert(axis, n)
    return ap.unsqueeze(axis).to_broadcast(shp)


def _bf16_round(a):
    return np.asarray(a, np.float32).astype(ml_dtypes.bfloat16).astype(np.float32)


def _common_consts():
    c = {}
    c["ident"] = np.eye(128, dtype=np.float32)
    h = np.arange(16)
    sl = (2.0 ** (-8.0 * (h + 1) / 16)).astype(np.float64)
    s_hi = _bf16_round(sl).astype(np.float64)
    s_lo = _bf16_round(sl - s_hi).astype(np.float64)
    qaug = np.zeros((4, 5, 8, 4, 128), np.float32)
    for g in range(4):
        for hh in range(4):
            hd = 4 * g + hh
            for m in range(8):
                ref = 128 * (4 * m + 3) + 64 - 2048
                qaug[g, 0, m, hh, :] = 8 * s_hi[hd]
                qaug[g, 1, m, hh, :] = 8 * s_lo[hd]
                qaug[g, 2, m, hh, :] = 8 * s_hi[hd]
                qaug[g, 3, m, hh, :] = 8 * s_lo[hd]
                qaug[g, 4, m, hh, :] = -8 * sl[hd] * ref
    c["qaug"] = qaug.reshape(4, 5, 8 * 4 * 128)

    def aug_rows(P):
        P = np.asarray(P, np.int64)
        hi = 128 * np.floor_divide(P, 128)
        lo = P - hi
        return np.stack([hi, hi, lo, lo, np.ones_like(P)]).astype(np.float32)

    c["kaug"] = aug_rows(np.arange(4096) - 2048)
    pc = np.zeros(256, np.int64)
    pc[:255] = 16 * np.arange(255) + 31 - 2048
    c["kcaug"] = aug_rows(pc)
    ex = np.zeros((64, 32, 128), np.float32)
    for kt in range(32):
        ex[2 * kt, kt, :64] = 1
        ex[2 * kt + 1, kt, 64:] = 1
    c["expand"] = ex.reshape(64, 32 * 128)
    n = np.arange(256)[:, None]
    jb = np.arange(64)[None, :]
    ov = ((16 * n < 64 * jb + 64) & (16 * n + 32 > 64 * jb) & (n < 255)).astype(np.float32)
    ovc = np.zeros((256, 65), np.float32)
    ovc[:255, 0] = 1.0
    ovc[:, 1:] = ov
    c["ovc"] = ovc.reshape(2, 128, 65).transpose(1, 0, 2).copy()
    k = np.arange(128)[:, None]
    q = np.arange(128)[None, :]
    c["tri_lo"] = np.where(k > q, NEG, 0.0).astype(np.float32)
    c["tri_up"] = np.where(k <= q, NEG, 0.0).astype(np.float32)
    return c


def _core_consts(s):
    c = {}
    npre = 3 - s
    nn = np.arange(128)[:, None, None, None]
    m = np.arange(8)[None, :, None, None]
    ch = np.arange(2)[None, None, :, None]
    q = np.arange(128)[None, None, None, :]
    n_ = ch * 128 + nn
    valid = (n_ >= 8 * npre) & (n_ <= 254) & (16 * n_ + 31 <= 128 * (4 * m + 3) + q)
    c["maskc"] = np.where(valid, 0.0, NEG).astype(np.float32).reshape(128, 8 * 2 * 128)
    k = np.arange(128)[:, None]
    qq = np.arange(128)[None, :]
    wm = np.zeros((128, 5, 128), np.float32)
    for D in range(5):
        if s - D < 0:
            wm[:, D, :] = NEG
        elif D == 0:
            wm[:, D, :] = np.where(k > qq, NEG, 0.0)
    c["winmask0"] = wm.reshape(128, 5 * 128)
    qv = np.arange(128)[:, None, None]
    mv = np.arange(8)[None, :, None]
    jb = np.arange(64)[None, None, :]
    jt = jb - 2 * npre
    tq = (4 * mv + s) * 128 + qv
    cur = tq // 64
    real = jt >= 0
    vis = real & (64 * jt <= tq)
    forced = (jt == 0) | (jt == cur) | (jt == cur - 1)
    c["vis"] = vis.astype(np.float32).reshape(128, 8 * 64)
    c["cstb"] = np.where(vis, 1.0e4 * forced, np.where(real, -1.0, -2.0)).astype(np.float32).reshape(128, 8 * 64)
    return c


def _prep_inputs(inp):
    x = np.asarray(inp["x"], np.float32)
    w_in = np.asarray(inp["w_in"], np.float32)[0]
    com = _common_consts()
    wg = np.zeros((4, 2048, 652), np.float32)
    for g in range(4):
        cols = list(range(g * 256, g * 256 + 256))
        for br in (0, 1, 2, 4, 3, 5):
            base = 1024 + (br * 4 + g) * 64
            cols += list(range(base, base + 64))
        cols += list(range(2560 + 12 * g, 2560 + 12 * g + 12))
        wg[g] = w_in[:, cols]
    com["wg"] = wg
    com["wconv"] = np.ascontiguousarray(w_in[:, 2608:4656])
    for nm in ("k", "v"):
        com["w1" + nm] = np.asarray(inp["cmp_w1_" + nm], np.float32)[0]
        com["w2" + nm] = np.asarray(inp["cmp_w2_" + nm], np.float32)[0]
        com["posT" + nm] = np.ascontiguousarray(np.asarray(inp["cmp_pos_" + nm], np.float32)[0].T)
    com["dww"] = np.ascontiguousarray(np.asarray(inp["dw_w"], np.float32)[0].T)
    com["dwb"] = np.ascontiguousarray(np.asarray(inp["dw_b"], np.float32)[0].reshape(8, 128).T)
    com["clg"] = np.ascontiguousarray(np.asarray(inp["conv_ln_g"], np.float32)[0].reshape(8, 128).T)
    com["clb"] = np.ascontiguousarray(np.asarray(inp["conv_ln_b"], np.float32)[0].reshape(8, 128).T)
    com["wout"] = np.asarray(inp["w_out"], np.float32)[0]
    com["ln1g"] = np.broadcast_to(np.asarray(inp["ln1_g"], np.float32)[0][None, :], (128, 2048)).copy()
    com["ln1b"] = np.broadcast_to(np.asarray(inp["ln1_b"], np.float32)[0][None, :], (128, 2048)).copy()
    wq = np.asarray(inp["peer_wq"], np.float32)[0]
    com["wq_l"] = np.ascontiguousarray(wq.reshape(16, 128, 16, 128).transpose(2, 1, 0, 3)).reshape(16, 128, 2048)
    keys = np.asarray(inp["peer_keys"], np.float32)[0]
    com["keysT"] = np.ascontiguousarray(keys.transpose(3, 0, 1, 2)).reshape(128, 2048)
    U = np.asarray(inp["peer_u"], np.float32)[0]
    com["UT_l"] = np.ascontiguousarray(U.reshape(128, 128, 16, 128).transpose(0, 3, 2, 1)).reshape(128, 128, 2048)
    com["V"] = np.asarray(inp["peer_v"], np.float32)[0]
    com["ln2g"] = np.broadcast_to(np.asarray(inp["ln2_g"], np.float32)[0][None, :], (128, 2048)).copy()
    com["ln2b"] = np.broadcast_to(np.asarray(inp["ln2_b"], np.float32)[0][None, :], (128, 2048)).copy()
    maps = []
    for c in range(8):
        b, s = c // 4, c % 4
        sh = (3 - s) * 128
        d = dict(com)
        xT = np.zeros((2048, 4096), np.float32)
        xT[:, sh:] = x[b, :4096 - sh].T
        d["xTs"] = xT
        own = np.concatenate([np.arange((4 * m + s) * 128, (4 * m + s + 1) * 128) for m in range(8)])
        d["x_own"] = np.ascontiguousarray(x[b, own])
        d.update(_core_consts(s))
        maps.append(d)
    return maps


def build(debug=False, stop=None, nch=128, peer_only=False):
    nc = bass.Bass("TRN2", target_bir_lowering=False)
    D = {}

    def din(name, shape, dt=F32):
        D[name] = nc.dram_tensor(name, list(shape), dt, kind="ExternalInput").ap()
        return D[name]

    def dout(name, shape, dt=F32):
        D[name] = nc.dram_tensor(name, list(shape), dt, kind="ExternalOutput").ap()
        return D[name]

    if not peer_only:
        din("xTs", [2048, 4096]); din("x_own", [1024, 2048])
        din("qaug", [4, 5, 4096]); din("kaug", [5, 4096]); din("kcaug", [5, 256])
        din("expand", [64, 4096]); din("ovc", [128, 2, 65]); din("tri_lo", [128, 128]); din("tri_up", [128, 128])
        din("maskc", [128, 2048]); din("winmask0", [128, 640]); din("vis", [128, 512]); din("cstb", [128, 512])
        din("wg", [4, 2048, 652]); din("wconv", [2048, 2048])
        for nm in ("k", "v"):
            din("w1" + nm, [2048, 256]); din("w2" + nm, [256, 64]); din("posT" + nm, [64, 32])
        din("dww", [1024, 31]); din("dwb", [128, 8]); din("clg", [128, 8]); din("clb", [128, 8])
        din("wout", [2048, 2048]); din("ln1g", [128, 2048]); din("ln1b", [128, 2048])
    else:
        din("y_in", [1024, 2048])
    din("ident", [128, 128])
    din("wq_l", [16, 128, 2048]); din("keysT", [128, 2048]); din("UT_l", [nch, 128, 2048]); din("V", [nch * 128, 2048])
    din("ln2g", [128, 2048]); din("ln2b", [128, 2048])
    dout("out", [1024, 2048])
    if debug:
        dout("d_acc", [1024, 2048])
    if debug and not peer_only:
        dout("d_onsa", [128, 8, 1024]); dout("d_oconv", [128, 8, 1024])
        dout("d_kc", [64, 4, 256]); dout("d_vc", [128, 4, 2, 64]); dout("d_q", [64, 4096])
        dout("d_ksel", [64, 4096]); dout("d_vsw", [128, 32 * 2 * 65]); dout("d_gates", [128, 8, 48])
        dout("d_imp", [128, 8, 64]); dout("d_negsel", [128, 8, 64])
    if debug:
        dout("d_S1", [128, 8192]); dout("d_S2", [128, 8192]); dout("d_c3", [128, 64])

    kb = KB(nc)
    with kb.st:
        if peer_only:
            _peer_only(nc, kb, D, debug, stop, nch)
        else:
            _program(nc, kb, D, debug, stop, nch)
        kb.barrier()
    return nc


def _program(nc, kb, D, debug, stop=None, nch=128):
    op, dma = kb.op, kb.dma
    ident = kb.sb("ident", [128, 128], F32)
    identb = kb.sb("identb", [128, 128], BF16)
    epsc = kb.sb("epsc", [128, 1], F32)
    y = kb.sb("y", [128, 8, 2048], F32)
    pA = ExitStack()
    kb.st.enter_context(pA)
    ones_f = kb.sb("ones_f", [128, 128], F32, pA)
    conv_scr = nc.dram_tensor("conv_scr", [128, 8192], BF16, kind="Internal").ap()
    dma("sp", ident[:], D["ident"], W=["ident"])
    dma("pool", identb[:], D["ident"], W=["identb"])
    op("dve", lambda e: e.memset(ones_f[:], 1.0), W=["ones_f"])
    op("dve", lambda e: e.memset(epsc[:], LN_EPS), W=["epsc"])

    xT_d = D["xTs"].rearrange("(dc p) t -> p dc t", p=128)

    with ExitStack() as ph:
        xoh = kb.sb("xoh", [128, 16, 8, 160], BF16, ph)
        o_convT = kb.sb("o_convT", [128, 8, 1024], BF16, ph)
        dww = kb.sb("dww", [128, 8, 31], F32, ph)
        dwb = kb.sb("dwb", [128, 8], F32, ph)
        clg = kb.sb("clg", [128, 8], F32, ph)
        clb = kb.sb("clb", [128, 8], F32, ph)
        call = kb.sb("call", [128, 8, 1024], F32, ph)
        wca = [kb.sb("wca%d" % i, [128, 16, 128], BF16, ph) for i in range(2)]
        wcg = [kb.sb("wcg%d" % i, [128, 16, 128], BF16, ph) for i in range(2)]
        sg = [kb.sb("sg%d" % i, [128, 480], F32, ph) for i in range(2)]
        u = [kb.sb("u%d" % i, [128, 8, 160], BF16, ph) for i in range(2)]
        dgw = [kb.sb("dgw%d" % i, [128, 31, 128], BF16, ph) for i in range(2)]
        psC = [kb.ps("psC%d" % i, [128, 512], F32, ph) for i in range(2)]
        psA = [kb.ps("psA%d" % i, [128, 512], F32, ph) for i in range(2)]
        psG = [kb.ps("psG%d" % i, [128, 512], F32, ph) for i in range(2)]
        psL = [kb.ps("psL%d" % i, [128, 512], F32, ph) for i in range(2)]

        for m in range(8):
            t0 = m * 512 + 352
            dma("pool", xoh[:, :, m, :], xT_d[:, :, t0:t0 + 160], W=[("xoh", m)])
        dma("sp", dww[:], D["dww"].rearrange("(ct p) w -> p ct w", p=128), W=["dww"])
        for nm, t in (("dwb", dwb), ("clg", clg), ("clb", clb)):
            dma("sp", t[:], D[nm], W=[nm])
        wc_d = D["wconv"].rearrange("(dc p) c -> p dc c", p=128)
        xoh_keys = [("xoh", m) for m in range(8)]
        chunks = [(0, 3), (3, 3), (6, 2)]
        k = 0
        for ct in range(8):
            wb = ct % 2
            dma("pool", wca[wb][:], wc_d[:, :, ct * 128:(ct + 1) * 128], W=[("wca", wb)])
            dma("pool", wcg[wb][:], wc_d[:, :, 1024 + ct * 128:1024 + (ct + 1) * 128], W=[("wcg", wb)])
            ub = ct % 2
            for (m0, nm_) in chunks:
                pb = k % 2
                k += 1
                n = nm_ * 160
                for dc in range(16):
                    op("pe", lambda e, dc=dc: e.matmul(psA[pb][:, 0:n].rearrange("p (a b) -> p a b", a=nm_), wca[wb][:, dc, :],
                                                       xoh[:, dc, m0:m0 + nm_, :], start=(dc == 0), stop=(dc == 15)),
                       R=[("wca", wb)] + xoh_keys, W=[("psA", pb)])
                for dc in range(16):
                    op("pe", lambda e, dc=dc: e.matmul(psG[pb][:, 0:n].rearrange("p (a b) -> p a b", a=nm_), wcg[wb][:, dc, :],
                                                       xoh[:, dc, m0:m0 + nm_, :], start=(dc == 0), stop=(dc == 15)),
                       R=[("wcg", wb)] + xoh_keys, W=[("psG", pb)])
                op("act", lambda e: e.activation(out=sg[pb][:, 0:n], in_=psG[pb][:, 0:n], func=AF.Sigmoid),
                   R=[("psG", pb)], W=[("sg", pb)])
                op("dve", lambda e: e.tensor_tensor(out=u[ub][:, m0:m0 + nm_, :].rearrange("p a b -> p (a b)"),
                                                    in0=psA[pb][:, 0:n], in1=sg[pb][:, 0:n], op=ALU.mult),
                   R=[("psA", pb), ("sg", pb)], W=[("u", ub)])
            op("dve", lambda e: e.tensor_tensor(out=dgw[ub][:], in0=bc(identb[:], 31, axis=1), in1=bc(dww[:, ct, :], 128, axis=2), op=ALU.mult),
               R=["identb", "dww"], W=[("dgw", ub)])
            for hf in range(2):
                for w in range(31):
                    op("pe", lambda e, w=w: e.matmul(psC[hf][:].rearrange("p (a b) -> p a b", a=4), dgw[ub][:, w, :],
                                                     u[ub][:, 4 * hf:4 * hf + 4, 2 + w:130 + w], start=(w == 0), stop=(w == 30)),
                       R=[("dgw", ub), ("u", ub)], W=[("psC", hf)])
                op("act", lambda e: e.activation(out=call[:, ct, hf * 512:(hf + 1) * 512], in_=psC[hf][:], func=AF.Identity, bias=dwb[:, ct:ct + 1]),
                   R=[("psC", hf), "dwb"], W=[("call", ct)])
        with ExitStack() as ph2:
            csq = kb.sb("csq", [128, 512], F32, ph2)
            mean = kb.sb("cmean", [128, 512], F32, ph2)
            rstd = kb.sb("crstd", [128, 512], F32, ph2)
            tmp = kb.sb("ctmp", [128, 512], F32, ph2)
            for hf in range(2):
                tsl = slice(hf * 512, (hf + 1) * 512)
                for ct in range(8):
                    op("pe", lambda e, ct=ct: e.matmul(psL[0][:], ones_f[:], call[:, ct, tsl], start=(ct == 0), stop=(ct == 7)),
                       R=["ones_f", ("call", ct)], W=["psL0"])
                for ct in range(8):
                    op("act", lambda e, ct=ct: e.activation(out=csq[:], in_=call[:, ct, tsl], func=AF.Square),
                       R=[("call", ct)], W=["csq"])
                    op("pe", lambda e, ct=ct: e.matmul(psL[1][:], ones_f[:], csq[:], start=(ct == 0), stop=(ct == 7)),
                       R=["ones_f", "csq"], W=["psL1"])
                op("dve", lambda e: e.tensor_scalar(out=mean[:], in0=psL[0][:], scalar1=1.0 / 1024, scalar2=None, op0=ALU.mult),
                   R=["psL0"], W=["cmean"])
                op("dve", lambda e: e.tensor_tensor(out=tmp[:], in0=mean[:], in1=mean[:], op=ALU.mult), R=["cmean"], W=["ctmp"])
                op("dve", lambda e: e.scalar_tensor_tensor(out=rstd[:], in0=psL[1][:], scalar=1.0 / 1024, in1=tmp[:],
                                                           op0=ALU.mult, op1=ALU.subtract),
                   R=["psL1", "ctmp"], W=["crstd"])
                op("act", lambda e: e.activation(out=rstd[:], in_=rstd[:], func=AF.Sqrt, bias=epsc[:, 0:1]), R=["crstd", "epsc"], W=["crstd"])
                op("dve", lambda e: e.reciprocal(out=rstd[:], in_=rstd[:]), R=["crstd"], W=["crstd"])
                for ct in range(8):
                    op("dve", lambda e, ct=ct: e.tensor_tensor(out=tmp[:], in0=call[:, ct, tsl], in1=mean[:], op=ALU.subtract),
                       R=[("call", ct), "cmean"], W=["ctmp"])
                    op("dve", lambda e: e.tensor_tensor(out=tmp[:], in0=tmp[:], in1=rstd[:], op=ALU.mult),
                       R=["ctmp", "crstd"], W=["ctmp"])
                    op("act", lambda e, ct=ct: e.activation(out=o_convT[:, ct, tsl], in_=tmp[:], func=AF.Silu,
                                                            bias=clb[:, ct:ct + 1], scale=clg[:, ct:ct + 1]),
                       R=["ctmp", "clg", "clb"], W=[("o_convT", ct)])
        if debug:
            dma("sp", D["d_oconv"], o_convT[:], R=[("o_convT", ct) for ct in range(8)], W=["d_oconv"])
        dma("sp", conv_scr, o_convT[:].rearrange("p a b -> p (a b)"), R=[("o_convT", ct) for ct in range(8)], W=["conv_scr"])
        kb.barrier()
        if stop == "C":
            return

    expand = kb.sb("expand", [64, 32, 128], BF16, pA)
    tri_lo = kb.sb("tri_lo", [128, 128], BF16, pA)
    tri_up = kb.sb("tri_up", [128, 128], BF16, pA)
    maskc = kb.sb("maskc", [128, 8, 2, 128], BF16, pA)
    winmask0 = kb.sb("winmask0", [128, 5, 128], BF16, pA)
    vis = kb.sb("vis", [128, 8, 64], F32, pA)
    cstb = kb.sb("cstb", [128, 8, 64], F32, pA)
    o_nsa = kb.sb("o_nsa", [128, 8, 1024], BF16, pA)
    gates = kb.sb("gates", [128, 8, 48], F32, pA)
    dma("pool", expand[:].rearrange("p a b -> p (a b)"), D["expand"], W=["expand"])
    dma("pool", tri_lo[:], D["tri_lo"], W=["tri_lo"])
    dma("pool", tri_up[:], D["tri_up"], W=["tri_up"])
    dma("pool", maskc[:].rearrange("p a b c -> p (a b c)"), D["maskc"], W=["maskc"])
    dma("pool", winmask0[:].rearrange("p a b -> p (a b)"), D["winmask0"], W=["winmask0"])
    dma("sp", vis[:].rearrange("p a b -> p (a b)"), D["vis"], W=["vis"])
    dma("sp", cstb[:].rearrange("p a b -> p (a b)"), D["cstb"], W=["cstb"])
    if debug:
        op("pool", lambda e: e.memset(o_nsa[:], 0.0), W=["o_nsa_init"])
        op("pool", lambda e: e.memset(gates[:], 0.0), W=["gates_init"])

    with ExitStack() as phg:
        kT_sel = kb.sb("kT_sel", [69, 4096], BF16, phg)
        kT_win = kb.sb("kT_win", [69, 4096], BF16, phg)
        kvc = kb.sb("kvc", [128, 4096], BF16, phg)
        stg = kb.sb("stg", [128, 4096], BF16, phg)
        V_sw = kb.sb("V_sw", [128, 32, 2, 65], BF16, phg)
        qT = kb.sb("qT", [69, 8, 4, 128], BF16, phg)
        kc_aug = kb.sb("kc_aug", [69, 256], BF16, phg)
        vcx = kb.sb("vcx", [128, 2, 129], BF16, phg)
        cbias = kb.sb("cbias", [128, 2, 2], F32, phg)
        dma("pool", kT_sel[64:69, :], D["kaug"], W=["kT_sel_aug"])
        dma("pool", kT_win[64:69, :], D["kaug"], W=["kT_win_aug"])
        dma("pool", kc_aug[64:69, :], D["kcaug"], W=["kc_aug_aug"])
        dma("pool", vcx[:, :, 64:129], D["ovc"], W=["vcx_c"])
        op("pool", lambda e: e.memset(V_sw[:, :, :, 64:65], 1.0), W=["V_ones"])
        op("pool", lambda e: e.memset(kc_aug[0:64, 255:256], 0.0), W=["kc_pad"])

        for g in range(4):
            with ExitStack() as ph:
                wgs = kb.sb("wgs", [128, 16, 652], BF16, ph)
                xt = [kb.sb("xt%d" % i, [128, 16, 512], BF16, ph) for i in range(2)]
                psP = [kb.ps("psP%d" % i, [128, 512], F32, ph) for i in range(2)]
                psQ = kb.ps("psQ", [64, 512], F32, ph)
                psV = kb.ps("psV", [128, 512], F32, ph)
                psGt = kb.ps("psGt", [128, 512], F32, ph)
                dma("pool", wgs[:], D["wg"][g].rearrange("(dc p) c -> p dc c", p=128), W=["wgs"])
                dma("pool", qT[64:69, :, :, :].rearrange("p a b c -> p (a b c)"), D["qaug"][g], W=["qT_aug"])
                pk = 0
                for m in range(8):
                    xb_ = m % 2
                    dma("pool", xt[xb_][:], xT_d[:, :, m * 512:(m + 1) * 512], W=[("xt", xb_)])
                    pairs = [(256, kvc, "kvc_lo", kvc, "kvc_hi"), (384, kT_sel, "kT_sel", stg, "stg")]
                    for (off, dlo, nlo, dhi, nhi) in pairs:
                        pb = pk % 2
                        pk += 1
                        for dc in range(16):
                            op("pe", lambda e, dc=dc: e.matmul(psP[pb][:], wgs[:, dc, off:off + 128], xt[xb_][:, dc, :],
                                                               start=(dc == 0), stop=(dc == 15)),
                               R=["wgs", ("xt", xb_)], W=[("psP", pb)])
                        op("act", lambda e: e.activation(out=dlo[0:64, m * 512:(m + 1) * 512], in_=psP[pb][0:64, :], func=AF.Identity),
                           R=[("psP", pb)], W=[(nlo, m)])
                        op("dve", lambda e: e.tensor_copy(out=dhi[64:128, m * 512:(m + 1) * 512], in_=psP[pb][64:128, :]),
                           R=[("psP", pb)], W=[(nhi, m), ("psP", pb)])
                    for dc in range(16):
                        for hh in range(4):
                            op("pe", lambda e, dc=dc, hh=hh: e.matmul(psQ[:, hh * 128:(hh + 1) * 128],
                                                                      wgs[:, dc, hh * 64:(hh + 1) * 64], xt[xb_][:, dc, 384:512],
                                                                      start=(dc == 0 and hh == 0), stop=(dc == 15),
                                                                      skip_group_check=True),
                               R=["wgs", ("xt", xb_)], W=["psQ"])
                    op("act", lambda e: e.activation(out=qT[0:64, m, :, :].rearrange("p a b -> p (a b)"), in_=psQ[:], func=AF.Identity),
                       R=["psQ"], W=[("qT", m)])
                    for dc in range(16):
                        for sub in range(4):
                            op("pe", lambda e, dc=dc, sub=sub: e.matmul(psV[:, sub * 128:(sub + 1) * 128],
                                                                        xt[xb_][:, dc, sub * 128:(sub + 1) * 128], wgs[:, dc, 512:640],
                                                                        start=(dc == 0 and sub == 0), stop=(dc == 15),
                                                                        skip_group_check=True),
                               R=["wgs", ("xt", xb_)], W=["psV"])
                    op("dve", lambda e: e.tensor_copy(out=V_sw[:, 4 * m:4 * m + 4, :, 0:64],
                                                      in_=psV[:].rearrange("p (a b c) -> p a b c", a=4, b=2)),
                       R=["psV", "V_ones"], W=[("V_sw", m)])
                    for dc in range(16):
                        op("pe", lambda e, dc=dc: e.matmul(psGt[:, 0:12], xt[xb_][:, dc, 384:512], wgs[:, dc, 640:652],
                                                           start=(dc == 0), stop=(dc == 15)),
                           R=["wgs", ("xt", xb_)], W=["psGt"])
                    op("act", lambda e: e.activation(out=gates[:, m, 12 * g:12 * g + 12], in_=psGt[:, 0:12], func=AF.Sigmoid),
                       R=["psGt"], W=[("gates", m, g)])
                dma("sp", kT_win[0:64, :], stg[64:128, :], R=[("stg", m) for m in range(8)], W=[("kT_win", m) for m in range(8)])
                if debug and g == 0:
                    dma("sp", D["d_q"], qT[0:64].rearrange("p a b c -> p (a b c)"), R=[("qT", m) for m in range(8)], W=["d_q"])
                    dma("sp", D["d_ksel"], kT_sel[0:64, :], R=[("kT_sel", m) for m in range(8)], W=["d_ksel"])
                    dma("sp", D["d_vsw"], V_sw[:].rearrange("p a b c -> p (a b c)"), R=[("V_sw", m) for m in range(8)] + ["V_ones"], W=["d_vsw"])
                kb.barrier()
                if stop == "G1":
                    return

            with ExitStack() as ph:
                w1kv = kb.sb("w1kv", [128, 32, 256], BF16, ph)
                w2 = [kb.sb("w2_%d" % i, [128, 2, 64], BF16, ph) for i in range(2)]
                posT = kb.sb("posTkv", [128, 32], BF16, ph)
                hid = [kb.sb("hid%d" % i, [128, 2, 256], BF16, ph) for i in range(2)]
                psH = [kb.ps("psH%d" % i, [128, 512], F32, ph) for i in range(2)]
                psB = kb.ps("psB", [128, 512], F32, ph)
                psO = kb.ps("psO", [128, 512], F32, ph)
                for wi, nm in enumerate(("k", "v")):
                    dma("pool", w1kv[wi * 64:wi * 64 + 64], D["w1" + nm].rearrange("(l d) c -> d l c", d=64), W=[("w1", wi)])
                    dma("pool", w2[wi][:], D["w2" + nm].rearrange("(cc p) d -> p cc d", p=128), W=[("w2", wi)])
                    dma("pool", posT[wi * 64:wi * 64 + 64, :], D["posT" + nm], W=[("posT", wi)])
                for wi in range(2):
                    p0 = wi * 64
                    skeys = [("kvc_lo" if wi == 0 else "kvc_hi", m) for m in range(8)]
                    for cc in range(2):
                        for l in range(32):
                            op("pe", lambda e, l=l, cc=cc: e.matmul(psB[:, wi * 2 + cc:wi * 2 + cc + 1], w1kv[p0:p0 + 64, l, cc * 128:(cc + 1) * 128],
                                                                    posT[p0:p0 + 64, l:l + 1], start=(l == 0 and cc == 0 and wi == 0), stop=(l == 31),
                                                                    skip_group_check=True),
                               R=[("w1", wi), ("posT", wi)], W=["psB"])
                    op("dve", lambda e: e.tensor_copy(out=cbias[:, wi, :], in_=psB[:, wi * 2:wi * 2 + 2]), R=["psB"], W=[("cbias", wi)])
                    for cc in range(2):
                        for l in range(32):
                            op("pe", lambda e, l=l, cc=cc: e.matmul(psH[cc][:, 0:255], w1kv[p0:p0 + 64, l, cc * 128:(cc + 1) * 128],
                                                                    kvc[p0:p0 + 64, l:l + 16 * 254 + 1:16], start=(l == 0), stop=(l == 31)),
                               R=[("w1", wi)] + skeys, W=[("psH", cc)])
                        op("act", lambda e, cc=cc: e.activation(out=hid[wi][:, cc, 0:255], in_=psH[cc][:, 0:255], func=AF.Gelu_apprx_tanh,
                                                                bias=cbias[:, wi, cc:cc + 1]),
                           R=[("psH", cc), ("cbias", wi)], W=[("hid", wi, cc)])
                for cc in range(2):
                    op("pe", lambda e, cc=cc: e.matmul(psO[0:64, 0:255], w2[0][:, cc, :], hid[0][:, cc, 0:255], start=(cc == 0), stop=(cc == 1)),
                       R=[("w2", 0), ("hid", 0, cc)], W=["psO"])
                op("dve", lambda e: e.tensor_copy(out=kc_aug[0:64, 0:255], in_=psO[0:64, 0:255]), R=["psO", "kc_pad"], W=["kc_aug"])
                for ch in range(2):
                    ncol = 128 if ch == 0 else 127
                    for cc in range(2):
                        op("pe", lambda e, cc=cc: e.matmul(psO[0:ncol, 256 + ch * 64:256 + ch * 64 + 64], hid[1][:, cc, ch * 128:ch * 128 + ncol],
                                                           w2[1][:, cc, :], start=(cc == 0), stop=(cc == 1), skip_group_check=True),
                           R=[("w2", 1), ("hid", 1, cc)], W=["psO"])
                op("pool", lambda e: e.memset(vcx[:, 1, 0:64], 0.0), W=["vcx_pad"])
                op("dve", lambda e: e.tensor_copy(out=vcx[:, 0, 0:64], in_=psO[:, 256:320]), R=["psO", "vcx_c"], W=["vcx0"])
                op("dve", lambda e: e.tensor_copy(out=vcx[0:127, 1, 0:64], in_=psO[0:127, 320:384]), R=["psO", "vcx_pad", "vcx_c"], W=["vcx1"])
                if debug:
                    dma("sp", D["d_kc"][:, g, :], kc_aug[0:64, :], R=["kc_aug", "kc_pad"], W=[("d_kc", g)])
                    dma("sp", D["d_vc"][:, g, :, :], vcx[:, :, 0:64], R=["vcx0", "vcx1"], W=[("d_vc", g)])
                kb.barrier()
                if stop == "G2":
                    return

            with ExitStack() as ph:
                _attention(nc, kb, D, debug, ph, g, locals())
                kb.barrier()
                if stop == "G3":
                    dma("sp", D["d_onsa"], o_nsa[:], R=[], W=["d_onsa"])
                    dma("sp", D["d_gates"], gates[:], R=[], W=["d_gates"])
                    return
        if debug:
            dma("sp", D["d_onsa"], o_nsa[:], R=[], W=["d_onsa"])
            dma("sp", D["d_gates"], gates[:], R=[], W=["d_gates"])
            kb.barrier()

    with ExitStack() as ph:
        o_nsaT = kb.sb("o_nsaT", [128, 8, 1024], BF16, ph)
        o_convT = kb.sb("o_convT2", [128, 8, 1024], BF16, ph)
        dma("sp", o_convT[:].rearrange("p a b -> p (a b)"), conv_scr, W=["o_convT2"])
        woutc = [kb.sb("woutc%d" % i, [128, 16, 512], BF16, ph) for i in range(2)]
        g1 = kb.sb("ln1g", [128, 2048], F32, ph)
        b1 = kb.sb("ln1b", [128, 2048], F32, ph)
        xo = [kb.sb("xo%d" % i, [128, 512], F32, ph) for i in range(2)]
        stats = kb.sb("stats", [128, 4, 6], F32, ph)
        mv = kb.sb("mv", [128, 2], F32, ph)
        rs = kb.sb("rs", [128, 1], F32, ph)
        psT = [kb.ps("psT%d" % i, [128, 512], BF16, ph) for i in range(2)]
        psM = [kb.ps("psM%d" % i, [128, 512], F32, ph) for i in range(2)]
        wo_d = D["wout"].rearrange("(fc p) c -> p fc c", p=128)
        dma("sp", g1[:], D["ln1g"], W=["ln1g"])
        dma("sp", b1[:], D["ln1b"], W=["ln1b"])
        k = 0
        for m in range(8):
            for fc in range(8):
                pb = k % 2
                k += 1
                op("pe", lambda e, fc=fc: e.transpose(psT[pb][:, 0:128], o_nsa[:, m, fc * 128:(fc + 1) * 128], identb[:]),
                   R=[("o_nsa", m), "identb"], W=[("psT", pb)])
                if fc % 2 == 0:
                    op("act", lambda e, fc=fc: e.activation(out=o_nsaT[:, fc, m * 128:(m + 1) * 128], in_=psT[pb][:, 0:128], func=AF.Identity),
                       R=[("psT", pb)], W=[("o_nsaT", m)])
                else:
                    op("dve", lambda e, fc=fc: e.tensor_copy(out=o_nsaT[:, fc, m * 128:(m + 1) * 128], in_=psT[pb][:, 0:128]),
                       R=[("psT", pb)], W=[("o_nsaT", m)])
        k = 0
        for cc in range(4):
            wb = cc % 2
            dma("pool", woutc[wb][:], wo_d[:, :, cc * 512:(cc + 1) * 512], W=[("woutc", wb)])
            for m in range(8):
                pb = k % 2
                k += 1
                dma("sp", xo[pb][:], D["x_own"][m * 128:(m + 1) * 128, cc * 512:(cc + 1) * 512], W=[("xo", pb)])
                for fc in range(16):
                    src = o_nsaT[:, fc, m * 128:(m + 1) * 128] if fc < 8 else o_convT[:, fc - 8, m * 128:(m + 1) * 128]
                    op("pe", lambda e, fc=fc, src=src: e.matmul(psM[pb][:], src, woutc[wb][:, fc, :], start=(fc == 0), stop=(fc == 15)),
                       R=[("o_nsaT", m), ("woutc", wb), "o_convT2"], W=[("psM", pb)])
                op("dve", lambda e: e.scalar_tensor_tensor(out=y[:, m, cc * 512:(cc + 1) * 512], in0=xo[pb][:], scalar=DN_ALPHA, in1=psM[pb][:],
                                                           op0=ALU.mult, op1=ALU.add),
                   R=[("xo", pb), ("psM", pb)], W=[("y", m)])
                if cc == 3:
                    _layer_norm(kb, y[:, m, :], y[:, m, :], g1, b1, stats, mv, rs, [("y", m)], [("y", m)], ["ln1g", "ln1b"], epsc)
        kb.barrier()

    pA.close()
    if stop == "O":
        for m in range(8):
            dma("sp", D["out"][m * 128:(m + 1) * 128, :], y[:, m, :], R=[("y", m)], W=[("out", m)])
        return
    _peer(nc, kb, D, debug, stop, y, ident, identb, epsc, nch)


def _peer_only(nc, kb, D, debug, stop, nch):
    op, dma = kb.op, kb.dma
    ident = kb.sb("ident", [128, 128], F32)
    identb = kb.sb("identb", [128, 128], BF16)
    epsc = kb.sb("epsc", [128, 1], F32)
    y = kb.sb("y", [128, 8, 2048], F32)
    dma("sp", ident[:], D["ident"], W=["ident"])
    dma("pool", identb[:], D["ident"], W=["identb"])
    op("dve", lambda e: e.memset(epsc[:], LN_EPS), W=["epsc"])
    for m in range(8):
        dma("sp", y[:, m, :], D["y_in"][m * 128:(m + 1) * 128, :], W=[("y", m)])
    _peer(nc, kb, D, debug, stop, y, ident, identb, epsc, nch)


def _peer(nc, kb, D, debug, stop, y, ident, identb, epsc, nch):
    op, dma = kb.op, kb.dma
    with ExitStack() as pp:
        yT = kb.sb("yT", [128, 16, 1024], BF16, pp)
        S2 = kb.sb("S2", [128, 8, 8, 128], F32, pp)
        c3 = kb.sb("c3", [128, 8, 8], F32, pp)
        ec3 = kb.sb("ec3", [128, 8, 8], F32, pp)
        S1d = nc.dram_tensor("S1d", [128, 128, 64], F32, kind="Internal").ap()
        ykeys = [("y", m) for m in range(8)]
        with ExitStack() as ph:
            S1 = kb.sb("S1", [128, 8, 8, 128], F32, ph)
            s1stg = kb.sb("s1stg", [128, 16, 64], F32, ph)
            qc = [kb.sb("qc%d" % i, [128, 1024], BF16, ph) for i in range(2)]
            wqc = [kb.sb("wqc%d" % i, [128, 16, 128], BF16, ph) for i in range(2)]
            keysT = kb.sb("keysT", [128, 16, 128], BF16, ph)
            v32 = kb.sb("v32", [128, 8, 32], F32, ph)
            tmpab = [kb.sb("tmpab%d" % i, [128, 256], F32, ph) for i in range(8)]
            tmpa = [t[:, 0:128] for t in tmpab]
            tmpb = [t[:, 128:256] for t in tmpab]
            cand = [kb.sb("cand%d" % i, [128, 256], F32, ph) for i in range(8)]
            cand2 = tmpab
            t24 = kb.sb("t24", [128, 8, 8, 24], F32, ph)
            d16 = kb.sb("d16", [128, 8, 16], F32, ph)
            sm = kb.sb("sm", [128, 8, 8], F32, ph)
            psA = [kb.ps("ppA%d" % i, [128, 512], F32, ph) for i in range(2)]
            psS = [kb.ps("ppS%d" % i, [128, 1024], F32, ph) for i in range(2)]
            dma("pool", keysT[:].rearrange("p a b -> p (a b)"), D["keysT"], W=["keysT"])
            k = 0
            for m in range(8):
                for dc4 in range(4):
                    pb = k % 2
                    k += 1
                    for i in range(4):
                        dc = dc4 * 4 + i
                        op("pe", lambda e, i=i, dc=dc: e.transpose(psA[pb][:, i * 128:(i + 1) * 128], y[:, m, dc * 128:(dc + 1) * 128], ident[:]),
                           R=[("y", m), "ident"], W=[("ppA", pb)])
                    dst = yT[:, dc4 * 4:dc4 * 4 + 4, m * 128:(m + 1) * 128]
                    if k % 2 == 0:
                        op("act", lambda e: e.activation(out=dst, in_=psA[pb][:].rearrange("p (a b) -> p a b", a=4), func=AF.Identity),
                           R=[("ppA", pb)], W=[("yT", m)])
                    else:
                        op("dve", lambda e: e.tensor_copy(out=dst, in_=psA[pb][:].rearrange("p (a b) -> p a b", a=4)),
                           R=[("ppA", pb)], W=[("yT", m)])
            yTk = [("yT", m) for m in range(8)]
            for m in range(8):
                op("act", lambda e: e.activation(out=y[:, m, :], in_=y[:, m, :], func=AF.Identity, scale=DN_ALPHA),
                   R=[("y", m)], W=[("y", m)])
            k = 0
            for c16 in range(16):
                wb = c16 % 2
                dma("pool", wqc[wb][:].rearrange("p a b -> p (a b)"), D["wq_l"][c16], W=[("wqc", wb)])
                for hf in range(2):
                    pb = k % 2
                    k += 1
                    for dc in range(16):
                        op("pe", lambda e, dc=dc: e.matmul(psA[pb][:], wqc[wb][:, dc, :], yT[:, dc, hf * 512:(hf + 1) * 512],
                                                           start=(dc == 0), stop=(dc == 15)),
                           R=[("wqc", wb)] + yTk, W=[("ppA", pb)])
                    op("act", lambda e: e.activation(out=qc[wb][:, hf * 512:(hf + 1) * 512], in_=psA[pb][:], func=AF.Identity),
                       R=[("ppA", pb)], W=[("qc", wb)])
                for m in range(8):
                    op("pe", lambda e, m=m: e.matmul(psS[wb][:, m * 128:(m + 1) * 128], qc[wb][:, m * 128:(m + 1) * 128], keysT[:, c16, :],
                                                     start=(m % 4 == 0), stop=True, skip_group_check=True),
                       R=[("qc", wb), "keysT"], W=[("ppS", wb)])
                S, sn = (S1, "S1") if c16 % 2 == 0 else (S2, "S2")
                hh = c16 // 2
                op("act", lambda e: e.activation(out=S[:, :, hh, :], in_=psS[wb][:].rearrange("p (a b) -> p a b", a=8), func=AF.Identity),
                   R=[("ppS", wb)], W=[(sn, hh)])
                if c16 % 2 == 1:
                    h = hh
                    chains = []
                    for m in range(8):
                        c = m
                        ta, tb, ca, cb = tmpa[c], tmpb[c], cand[c], cand2[c]
                        vv = v32[:, m, :]
                        tt = t24[:, m, h, :]
                        steps = [
                            lambda m=m, vv=vv, c=c: op("dve", lambda e: e.max(out=vv[:, 0:8], in_=S1[:, m, h, :]), R=[("S1", h)], W=[("v32a", c)]),
                            lambda m=m, vv=vv, c=c: op("dve", lambda e: e.max(out=vv[:, 16:24], in_=S2[:, m, h, :]), R=[("S2", h)], W=[("v32b", c)]),
                            lambda m=m, vv=vv, ta=ta, c=c: op("dve", lambda e: e.match_replace(out=ta, in_to_replace=vv[:, 0:8], in_values=S1[:, m, h, :], imm_value=-1e30),
                                                              R=[("S1", h), ("v32a", c)], W=[("tmpa", c)]),
                            lambda m=m, vv=vv, tb=tb, c=c: op("dve", lambda e: e.match_replace(out=tb, in_to_replace=vv[:, 16:24], in_values=S2[:, m, h, :], imm_value=-1e30),
                                                              R=[("S2", h), ("v32b", c)], W=[("tmpb", c)]),
                            lambda vv=vv, ta=ta, c=c: op("dve", lambda e: e.max(out=vv[:, 8:16], in_=ta), R=[("tmpa", c)], W=[("v32c", c)]),
                            lambda vv=vv, tb=tb, c=c: op("dve", lambda e: e.max(out=vv[:, 24:32], in_=tb), R=[("tmpb", c)], W=[("v32d", c)]),
                            lambda vv=vv, ca=ca, c=c: op("dve", lambda e: e.tensor_tensor(out=ca[:].rearrange("p (a b) -> p a b", a=16), in0=bc(vv[:, 0:16], 16, axis=2),
                                                                                          in1=bc(vv[:, 16:32], 16, axis=1), op=ALU.add),
                                                         R=[("v32a", c), ("v32b", c), ("v32c", c), ("v32d", c)], W=[("cand", c)]),
                            lambda tt=tt, ca=ca, c=c, m=m: op("dve", lambda e: e.max(out=tt[:, 0:8], in_=ca[:]), R=[("cand", c)], W=[("t24a", m, h)]),
                            lambda tt=tt, ca=ca, cb=cb, c=c, m=m: op("dve", lambda e: e.match_replace(out=cb[:], in_to_replace=tt[:, 0:8], in_values=ca[:], imm_value=-1e30),
                                                                     R=[("cand", c), ("t24a", m, h)], W=[("cand2", c)]),
                            lambda tt=tt, cb=cb, c=c, m=m: op("dve", lambda e: e.max(out=tt[:, 8:16], in_=cb[:]), R=[("cand2", c)], W=[("t24b", m, h)]),
                            lambda tt=tt, ca=ca, cb=cb, c=c, m=m: op("dve", lambda e: e.match_replace(out=ca[:], in_to_replace=tt[:, 8:16], in_values=cb[:], imm_value=-1e30),
                                                                     R=[("cand2", c), ("t24b", m, h)], W=[("cand", c)]),
                            lambda tt=tt, ca=ca, c=c, m=m: op("dve", lambda e: e.max(out=tt[:, 16:24], in_=ca[:]), R=[("cand", c)], W=[("t24c", m, h)]),
                        ]
                        chains.append(steps)
                    for si in range(len(chains[0])):
                        for c in range(8):
                            chains[c][si]()
            S1k = [("S1", h) for h in range(8)]
            S2k = [("S2", h) for h in range(8)]
            for m in range(8):
                t24k = [(k_, m, h) for k_ in ("t24a", "t24b", "t24c") for h in range(8)]
                op("dve", lambda e: e.tensor_copy(out=sm[:, 0, :], in_=t24[:, m, :, 0]), R=t24k, W=["sm0"])
                op("dve", lambda e: e.tensor_tensor(out=d16[:], in0=t24[:, m, :, 0:16], in1=bc(sm[:, 0, :], 16, axis=2), op=ALU.subtract),
                   R=t24k + ["sm0"], W=["d16"])
                op("act", lambda e: e.activation(out=d16[:], in_=d16[:], func=AF.Exp), R=["d16"], W=["d16"])
                op("dve", lambda e: e.tensor_reduce(out=sm[:, 1, :], in_=d16[:], axis=AX.X, op=ALU.add), R=["d16"], W=["sm1"])
                op("act", lambda e: e.activation(out=sm[:, 2, :], in_=sm[:, 1, :], func=AF.Ln), R=["sm1"], W=["sm2"])
                op("dve", lambda e: e.tensor_tensor(out=sm[:, 3, :], in0=sm[:, 0, :], in1=sm[:, 2, :], op=ALU.add), R=["sm0", "sm2"], W=["sm3"])
                op("dve", lambda e: e.tensor_tensor(out=sm[:, 4, :], in0=t24[:, m, :, 15], in1=t24[:, m, :, 16], op=ALU.add), R=t24k, W=["sm4"])
                op("dve", lambda e: e.scalar_tensor_tensor(out=c3[:, m, :], in0=sm[:, 4, :], scalar=0.5, in1=sm[:, 3, :], op0=ALU.mult, op1=ALU.subtract),
                   R=["sm4", "sm3"], W=[("c3", m)])
                op("dve", lambda e: e.tensor_scalar(out=sm[:, 5, :], in0=sm[:, 4, :], scalar1=0.5, scalar2=None, op0=ALU.mult), R=["sm4"], W=["sm5"])
                op("dve", lambda e: e.tensor_tensor(out=S2[:, m, :, :], in0=S2[:, m, :, :], in1=bc(sm[:, 5, :], 128, axis=2), op=ALU.subtract),
                   R=S2k + ["sm5"], W=[("S2f", m)])
                op("act", lambda e: e.activation(out=S2[:, m, :, :].rearrange("p a b -> p (a b)"), in_=S2[:, m, :, :].rearrange("p a b -> p (a b)"), func=AF.Exp),
                   R=[("S2f", m)], W=[("S2f", m)])
            op("act", lambda e: e.activation(out=ec3[:], in_=c3[:], func=AF.Exp), R=[("c3", m) for m in range(8)], W=["ec3"])
            if debug:
                dma("sp", D["d_S1"], S1[:].rearrange("p a b c -> p (a b c)"), R=S1k, W=["d_S1"])
            for m in range(8):
                op("act", lambda e: e.activation(out=S1[:, m, :, :].rearrange("p a b -> p (a b)"), in_=S1[:, m, :, :].rearrange("p a b -> p (a b)"), func=AF.Exp),
                   R=S1k + ["d_S1"], W=S1k)
            S1v = S1[:].rearrange("p m h i -> p i (m h)")
            for ib in range(8):
                op("act", lambda e: e.activation(out=s1stg[:], in_=S1v[:, ib * 16:(ib + 1) * 16, :], func=AF.Identity), R=S1k, W=["s1stg"])
                dma("sp", S1d[:, ib * 16:(ib + 1) * 16, :], s1stg[:], R=["s1stg"], W=[("S1d", ib)])
            if debug:
                dma("sp", D["d_S2"], S2[:].rearrange("p a b c -> p (a b c)"), R=[("S2f", m) for m in range(8)], W=["d_S2"])
                dma("sp", D["d_c3"], c3[:].rearrange("p a b -> p (a b)"), R=[("c3", m) for m in range(8)], W=["d_c3"])
            kb.barrier()
        if stop == "P4":
            return
        NCH = nch
        GC = 2
        with ExitStack() as ph:
            UT = [kb.sb("UT%d" % i, [128, 16, 128], BF16, ph) for i in range(2)]
            Vg = [kb.sb("Vg%d" % i, [128, GC, 2048], BF16, ph) for i in range(2)]
            GH = [kb.sb("GH%d" % i, [128, GC, 1024], BF16, ph) for i in range(2)]
            NRB = 3
            POOL_TILES = (4,)
            EtB = [kb.sb("EtB%d" % i, [128, 8, 128], F32, ph) for i in range(NRB)]
            NRM = 3
            mkB = [kb.sb("mkB%d" % i, [128, 8, 128], BF16, ph) for i in range(NRM)]
            s1c = [kb.sb("s1c%d" % i, [128, 64], F32, ph) for i in range(3)]
            Dg = kb.sb("Dg", [128, 8, 8, 128], BF16, ph)
            for m in range(8):
                for h in range(8):
                    op("dve", lambda e: e.tensor_scalar(out=Dg[:, m, h, :], in0=identb[:], scalar1=ec3[:, m, h:h + 1], scalar2=None, op0=ALU.mult),
                       R=[], W=[("Dg", m)])
            psH = kb.ps("psH", [128, 1024], F32, ph)
            psG = kb.ps("psG", [128, 1024], F32, ph)
            psY = [kb.ps("psY%d" % i, [128, 1024], F32, ph) for i in range(2)]
            yTk = [("yT", m) for m in range(8)]
            skeys = [("S1", m) for m in range(8)] + [("S2", m) for m in range(8)] + [("c3", m) for m in range(8)]

            def load_u(i):
                dma("pool", UT[i % 2][:].rearrange("p a b -> p (a b)"), D["UT_l"][i], W=[("UT", i % 2)])

            def load_v(i):
                gi = i // GC
                dma("pool", Vg[gi % 2][:, i % GC, :], D["V"][i * 128:(i + 1) * 128, :], W=[("Vg", gi % 2, i % GC)])

            def a_part(i, p):
                hf = p // 4
                for dc in range((p % 4) * 4, (p % 4) * 4 + 4):
                    op("pe", lambda e, dc=dc: e.matmul(psH[:, hf * 512:(hf + 1) * 512], UT[i % 2][:, dc, :], yT[:, dc, hf * 512:(hf + 1) * 512],
                                                       start=(dc == 0), stop=(dc == 15)),
                       R=[("UT", i % 2)] + yTk, W=["psH"])
                if p == 7:
                    op("act", lambda e: e.activation(out=GH[(i // GC) % 2][:, i % GC, :], in_=psH[:], func=AF.Gelu_apprx_tanh),
                       R=["psH"], W=[("GH", (i // GC) % 2, i % GC)])

            cnt = {"r": 0, "y": 0}

            def load_s1(i):
                dma("sp", s1c[i % 3][:], S1d[:, i, :], R=[("S1d", ib) for ib in range(8)], W=[("s1c", i % 3)])

            def b_front(i, m):
                re = cnt["r"] % NRB
                r = cnt["r"] % NRM
                cnt["r"] += 1
                ek = [("EtB", re, h) for h in range(8)]
                if m in POOL_TILES:
                    op("pool", lambda e: e.tensor_tensor(out=EtB[re][:], in0=S2[:, m, :, :], in1=bc(s1c[i % 3][:, m * 8:(m + 1) * 8], 128, axis=2), op=ALU.mult),
                       R=[("s1c", i % 3)], W=ek)
                else:
                    for h in range(8):
                        op("act", lambda e: e.activation(out=EtB[re][:, h, :], in_=S2[:, m, h, :], func=AF.Identity, scale=s1c[i % 3][:, m * 8 + h:m * 8 + h + 1]),
                           R=[("s1c", i % 3)], W=[("EtB", re, h)])
                ef = EtB[re][:].rearrange("p a b -> p (a b)")
                op("dve", lambda e: e.scalar_tensor_tensor(out=mkB[r][:].rearrange("p a b -> p (a b)"), in0=ef, scalar=1.0, in1=ef,
                                                           op0=ALU.is_ge, op1=ALU.mult),
                   R=ek, W=[("mkB", r)] + ek)
                return r

            def b_back(i, m, r):
                for h in range(8):
                    op("pe", lambda e, h=h: e.matmul(psG[:, m * 128:(m + 1) * 128], mkB[r][:, h, :], Dg[:, m, h, :],
                                                     start=(h == 0 and m % 4 == 0), stop=(h == 7), skip_group_check=True),
                       R=[("mkB", r), ("Dg", m)], W=["psG"])

            def gh(i):
                gi = i // GC
                for hf in range(2):
                    dst = GH[gi % 2][:, i % GC, hf * 512:(hf + 1) * 512]
                    op("dve", lambda e: e.tensor_tensor(out=dst, in0=psG[:, hf * 512:(hf + 1) * 512], in1=dst, op=ALU.mult),
                       R=["psG", ("GH", gi % 2, i % GC)], W=[("GH", gi % 2, i % GC)])

            def c_unit(gi, u):
                m, hf = u // 2, u % 2
                yb = cnt["y"] % 2
                cnt["y"] += 1
                for ci in range(GC):
                    for c2 in range(2):
                        col = hf * 1024 + c2 * 512
                        op("pe", lambda e, ci=ci, c2=c2, col=col: e.matmul(psY[yb][:, c2 * 512:(c2 + 1) * 512], GH[gi % 2][:, ci, m * 128:(m + 1) * 128],
                                                                           Vg[gi % 2][:, ci, col:col + 512], start=(ci == 0), stop=(ci == GC - 1)),
                           R=[("GH", gi % 2, c) for c in range(GC)] + [("Vg", gi % 2, c) for c in range(GC)], W=[("psY", yb)])
                op("dve", lambda e: e.tensor_tensor(out=y[:, m, hf * 1024:(hf + 1) * 1024], in0=psY[yb][:], in1=y[:, m, hf * 1024:(hf + 1) * 1024], op=ALU.add),
                   R=[("psY", yb), ("y", m)], W=[("y", m)])

            load_u(0); load_u(1)
            for i in range(2 * GC):
                load_v(i)
            load_s1(0)
            if NCH > 1:
                load_s1(1)
            for p in range(8):
                a_part(0, p)
            queue = []
            for i in range(NCH):
                if i + 2 < NCH:
                    load_u(i + 2)
                    load_s1(i + 2)
                plan = [1] * 8 if i % 2 == 0 else [2, 2, 2, 2, 0, 0, 0, 0]
                for m in range(8):
                    r = b_front(i, m)
                    if i + 1 < NCH:
                        a_part(i + 1, m)
                    for _ in range(plan[m]):
                        if queue:
                            c_unit(*queue.pop(0))
                    b_back(i, m, r)
                gh(i)
                if i % GC == GC - 1:
                    queue += [(i // GC, u) for u in range(16)]
                    g_next = i // GC + 1
                    if g_next >= 2:
                        for ii in range(g_next * GC, (g_next + 1) * GC):
                            if ii < NCH:
                                load_v(ii)
            while queue:
                c_unit(*queue.pop(0))
            kb.barrier()
        if debug:
            for m in range(8):
                dma("sp", D["d_acc"][m * 128:(m + 1) * 128, :], y[:, m, :], R=[("y", m)], W=[("d_acc", m)])
        with ExitStack() as ph:
            g2 = kb.sb("ln2g", [128, 2048], F32, ph)
            b2 = kb.sb("ln2b", [128, 2048], F32, ph)
            stats = kb.sb("stats2", [128, 4, 6], F32, ph)
            mv = kb.sb("mv2", [128, 2], F32, ph)
            rs = kb.sb("rs2", [128, 1], F32, ph)
            dma("sp", g2[:], D["ln2g"], W=["ln2g"])
            dma("sp", b2[:], D["ln2b"], W=["ln2b"])
            for m in range(8):
                _layer_norm(kb, y[:, m, :], y[:, m, :], g2, b2, stats, mv, rs, [("y", m)], [("y", m)], ["ln2g", "ln2b"], epsc)
                dma("sp", D["out"][m * 128:(m + 1) * 128, :], y[:, m, :], R=[("y", m)], W=[("out", m)])
            kb.wait_keys("sp", [("out", m) for m in range(8)])
            kb.barrier()


def _layer_norm(kb, z, out_ap, g_t, b_t, stats, mv, rs, zkeys, okeys, gkeys, epsc):
    op = kb.op
    for c4 in range(4):
        op("dve", lambda e, c4=c4: e.bn_stats(out=stats[:, c4, :], in_=z[:, c4 * 512:(c4 + 1) * 512]), R=zkeys, W=["stats"])
    op("dve", lambda e: e.bn_aggr(out=mv[:], in_=stats[:].rearrange("p a b -> p (a b)")), R=["stats"], W=["mv"])
    op("act", lambda e: e.activation(out=rs[:], in_=mv[:, 1:2], func=AF.Sqrt, bias=epsc[:, 0:1]), R=["mv", "epsc"], W=["rs"])
    op("dve", lambda e: e.reciprocal(out=rs[:], in_=rs[:]), R=["rs"], W=["rs"])
    op("dve", lambda e: e.tensor_scalar(out=mv[:, 1:2], in0=mv[:, 0:1], scalar1=rs[:, 0:1], scalar2=-1.0, op0=ALU.mult, op1=ALU.mult),
       R=["mv", "rs"], W=["mv"])
    op("act", lambda e: e.activation(out=z, in_=z, func=AF.Identity, scale=rs[:, 0:1], bias=mv[:, 1:2]), R=zkeys + ["mv", "rs"], W=zkeys)
    op("dve", lambda e: e.tensor_tensor(out=z, in0=z, in1=g_t[:], op=ALU.mult), R=zkeys + gkeys, W=zkeys)
    op("pool", lambda e: e.tensor_tensor(out=out_ap, in0=z, in1=b_t[:], op=ALU.add), R=zkeys + gkeys, W=okeys)


def _attention(nc, kb, D, debug, ph, g, L):
    op, dma = kb.op, kb.dma
    kT_sel, kT_win, V_sw, qT, kc_aug, vcx = L["kT_sel"], L["kT_win"], L["V_sw"], L["qT"], L["kc_aug"], L["vcx"]
    identb, ident, expand, tri_lo, tri_up = L["identb"], L["ident"], L["expand"], L["tri_lo"], L["tri_up"]
    maskc, winmask0, vis, cstb, o_nsa, gates = L["maskc"], L["winmask0"], L["vis"], L["cstb"], L["o_nsa"], L["gates"]
    NE = 4
    Eb = [kb.sb("Eb%d" % i, [128, 512], BF16, ph) for i in range(NE)]
    NS = 3
    psS = [kb.ps("psS%d" % i, [128, 512], F32, ph) for i in range(NS)]
    pc = kb.ps("pc", [128, 4, 256], F32, ph)
    psel = kb.ps("psel", [128, 512], F32, ph)
    pwin = kb.ps("pwin", [128, 512], F32, ph)
    pT = kb.ps("pT", [128, 512], F32, ph)
    zc = kb.sb("zc", [128, 12], F32, ph)
    rz = kb.sb("rz", [128, 12], F32, ph)
    coef = kb.sb("coef", [128, 12], F32, ph)
    imp = kb.sb("imp", [128, 64], F32, ph)
    impa = kb.sb("impa", [128, 64], F32, ph)
    imp2 = kb.sb("imp2", [128, 64], F32, ph)
    m8 = kb.sb("m8", [128, 16], F32, ph)
    nsel = kb.sb("nsel", [128, 64], F32, ph)
    nselT = kb.sb("nselT", [64, 128], BF16, ph)
    oacc = kb.sb("oacc", [128, 4, 64], F32, ph)
    kaug_keys_s = ["kT_sel_aug"] + [("kT_sel", m) for m in range(8)]
    kaug_keys_w = ["kT_win_aug"] + [("kT_win", m) for m in range(8)]
    st = {"s": 0, "e": 0}

    def emit_score(job):
        mm_list, nrow = job["mm"], job.get("nrow", 128)
        sb_ = st["s"] % NS
        st["s"] += 1
        eb = st["e"] % NE
        st["e"] += 1
        n = len(mm_list)
        for i, (lh, rh, rk) in enumerate(mm_list):
            op("pe", lambda e, lh=lh, rh=rh, i=i: e.matmul(psS[sb_][0:nrow, :] if len(rh.shape) == 2 else
                                                           psS[sb_][0:nrow, :].rearrange("p (a b) -> p a b", a=4),
                                                           lh, rh, start=(i == 0), stop=(i == n - 1)),
               R=rk, W=[("psS", sb_)])
        op("act", lambda e: e.activation(out=Eb[eb][0:nrow, :], in_=psS[sb_][0:nrow, :], func=AF.Exp, scale=0.125),
           R=[("psS", sb_)], W=[("Eb", eb)])
        return (job, eb)

    def run_jobs(jobs):
        pending = []
        for job in jobs:
            pending.append(emit_score(job))
            if len(pending) > 2:
                j_, eb_ = pending.pop(0)
                j_["pv"](eb_)
        for j_, eb_ in pending:
            j_["pv"](eb_)

    for m in range(8):
        J = 4 * m + 3
        qrhs = qT[0:69, m, :, :].rearrange("p a b -> p (a b)")
        qk = [("qT", m), "qT_aug"]
        vkeys = ["V_ones"] + [("V_sw", i) for i in range(8)]
        nch = 1 if 8 * J + 6 < 128 else 2
        jobs = []
        for ch in range(nch):
            nr = 128 if ch == 0 else 127

            def pv_c(eb, ch=ch, nr=nr):
                for hh in range(4):
                    op("pe", lambda e, hh=hh: e.matmul(pc[:, hh, 0:129], Eb[eb][0:nr, hh * 128:(hh + 1) * 128], vcx[0:nr, ch, :],
                                                       start=(ch == 0 and hh % 2 == 0), stop=(ch == nch - 1), skip_group_check=True),
                       R=[("Eb", eb), "vcx0", "vcx1", "vcx_c", "vcx_pad"], W=["pc"])
            jobs.append(dict(mm=[
                (kc_aug[0:69, ch * 128:ch * 128 + nr], qrhs, qk + ["kc_aug", "kc_aug_aug", "kc_pad"]),
                (identb[0:nr, 0:nr], bc(maskc[0:nr, m, ch, :], 4), ["identb", "maskc"]),
            ], nrow=nr, pv=pv_c))
        run_jobs(jobs)
        op("dve", lambda e: e.tensor_scalar(out=zc[:, 0:4], in0=pc[:, :, 64], scalar1=1e-30, scalar2=None, op0=ALU.max), R=["pc"], W=["zc0"])
        op("dve", lambda e: e.reciprocal(out=rz[:, 0:4], in_=zc[:, 0:4]), R=["zc0"], W=["rz0"])
        op("dve", lambda e: e.tensor_scalar(out=imp[:], in0=pc[:, 0, 65:129], scalar1=rz[:, 0:1], scalar2=None, op0=ALU.mult),
           R=["pc", "rz0"], W=["imp"])
        for hh in range(1, 4):
            op("dve", lambda e, hh=hh: e.scalar_tensor_tensor(out=imp[:], in0=pc[:, hh, 65:129], scalar=rz[:, hh:hh + 1], in1=imp[:],
                                                              op0=ALU.mult, op1=ALU.add), R=["pc", "rz0", "imp"], W=["imp"])
        op("dve", lambda e: e.tensor_tensor(out=impa[:], in0=imp[:], in1=vis[:, m, :], op=ALU.mult), R=["imp", "vis"], W=["impa"])
        op("dve", lambda e: e.tensor_tensor(out=impa[:], in0=impa[:], in1=cstb[:, m, :], op=ALU.add), R=["impa", "cstb"], W=["impa"])
        op("dve", lambda e: e.max(out=m8[:, 0:8], in_=impa[:]), R=["impa"], W=["m8a"])
        op("dve", lambda e: e.match_replace(out=imp2[:], in_to_replace=m8[:, 0:8], in_values=impa[:], imm_value=-1e30),
           R=["impa", "m8a"], W=["imp2"])
        op("dve", lambda e: e.max(out=m8[:, 8:16], in_=imp2[:]), R=["imp2"], W=["m8b"])
        op("dve", lambda e: e.tensor_scalar(out=nsel[:], in0=impa[:], scalar1=m8[:, 15:16], scalar2=None, op0=ALU.is_ge),
           R=["impa", "m8b"], W=["nsel"])
        op("dve", lambda e: e.tensor_scalar(out=nsel[:], in0=nsel[:], scalar1=-NEG, scalar2=NEG, op0=ALU.mult, op1=ALU.add),
           R=["nsel"], W=["nsel"])
        if debug:
            dma("sp", D["d_imp"][:, m, :], impa[:], R=["impa"], W=[("d_imp", m, g)])
            dma("sp", D["d_negsel"][:, m, :], nsel[:], R=["nsel"], W=[("d_negsel", m, g)])
        op("dve", lambda e: e.tensor_tensor(out=coef[:, 0:4], in0=rz[:, 0:4], in1=gates[:, m, 12 * g:12 * g + 12].rearrange("p (b a) -> p a b", a=3)[:, 0, :], op=ALU.mult),
           R=["rz0", ("gates", m, g)], W=["coef0"])
        for hh in range(4):
            op("dve", lambda e, hh=hh: e.tensor_scalar(out=oacc[:, hh, :], in0=pc[:, hh, 0:64], scalar1=coef[:, hh:hh + 1], scalar2=None, op0=ALU.mult),
               R=["pc", "coef0"], W=[("oacc", hh)])
        jobs = []
        wlist = [Dd for Dd in (4, 3, 2, 1, 0) if J - Dd >= 0]
        for Dd in wlist:
            kt = J - Dd
            mm = [(kT_win[0:69, kt * 128:(kt + 1) * 128], qrhs, qk + kaug_keys_w)]
            if m == 0:
                mm.append((identb[:], bc(winmask0[:, Dd, :], 4), ["identb", "winmask0"]))
            elif Dd == 0:
                mm.append((identb[:], bc(tri_lo[:], 4), ["identb", "tri_lo"]))
            elif Dd == 4:
                mm.append((identb[:], bc(tri_up[:], 4), ["identb", "tri_up"]))

            def pv_w(eb, kt=kt, Dd=Dd):
                for hh in range(4):
                    op("pe", lambda e, hh=hh: e.matmul(pwin[:, hh * 65:hh * 65 + 65], Eb[eb][:, hh * 128:(hh + 1) * 128], V_sw[:, kt, 1, :],
                                                       start=(Dd == wlist[0] and hh == 0), stop=(Dd == 0), skip_group_check=True),
                       R=[("Eb", eb)] + vkeys, W=["pwin"])
            jobs.append(dict(mm=mm, pv=pv_w))
        run_jobs(jobs)
        op("pe", lambda e: e.transpose(pT[0:64, 0:128], nsel[:], ident[:]), R=["nsel", "ident"], W=["pT"])
        op("dve", lambda e: e.tensor_copy(out=nselT[:], in_=pT[0:64, 0:128]), R=["pT"], W=["nselT"])
        jobs = []
        for kt in range(J + 1):
            mm = [(kT_sel[0:69, kt * 128:(kt + 1) * 128], qrhs, qk + kaug_keys_s),
                  (expand[:, kt, :], bc(nselT[:], 4), ["expand", "nselT"])]
            if kt == J:
                mm.append((identb[:], bc(tri_lo[:], 4), ["identb", "tri_lo"]))

            def pv_s(eb, kt=kt):
                for hh in range(4):
                    op("pe", lambda e, hh=hh: e.matmul(psel[:, hh * 65:hh * 65 + 65], Eb[eb][:, hh * 128:(hh + 1) * 128], V_sw[:, kt, 0, :],
                                                       start=(kt == 0 and hh == 0), stop=(kt == J), skip_group_check=True),
                       R=[("Eb", eb)] + vkeys, W=["psel"])
            jobs.append(dict(mm=mm, pv=pv_s))
        run_jobs(jobs)
        pselv = psel[:, 0:260].rearrange("p (a b) -> p a b", a=4)
        pwinv = pwin[:, 0:260].rearrange("p (a b) -> p a b", a=4)
        op("dve", lambda e: e.tensor_scalar(out=zc[:, 4:8], in0=pselv[:, :, 64], scalar1=1e-30, scalar2=None, op0=ALU.max), R=["psel"], W=["zc1"])
        op("dve", lambda e: e.tensor_scalar(out=zc[:, 8:12], in0=pwinv[:, :, 64], scalar1=1e-30, scalar2=None, op0=ALU.max), R=["pwin"], W=["zc2"])
        op("dve", lambda e: e.reciprocal(out=rz[:, 4:12], in_=zc[:, 4:12]), R=["zc1", "zc2"], W=["rz1"])
        op("dve", lambda e: e.tensor_tensor(out=coef[:, 4:12].rearrange("p (a b) -> p a b", a=2),
                                            in0=rz[:, 4:12].rearrange("p (a b) -> p a b", a=2),
                                            in1=gates[:, m, 12 * g:12 * g + 12].rearrange("p (b a) -> p a b", a=3)[:, 1:3, :], op=ALU.mult),
           R=["rz1", ("gates", m, g)], W=["coef"])
        for hh in range(4):
            op("dve", lambda e, hh=hh: e.scalar_tensor_tensor(out=oacc[:, hh, :], in0=pselv[:, hh, 0:64], scalar=coef[:, 4 + hh:5 + hh], in1=oacc[:, hh, :],
                                                              op0=ALU.mult, op1=ALU.add), R=["psel", "coef", ("oacc", hh)], W=[("oacc", hh)])
            op("dve", lambda e, hh=hh: e.scalar_tensor_tensor(out=o_nsa[:, m, g * 256 + hh * 64:g * 256 + hh * 64 + 64], in0=pwinv[:, hh, 0:64],
                                                              scalar=coef[:, 8 + hh:9 + hh], in1=oacc[:, hh, :], op0=ALU.mult, op1=ALU.add),
               R=["pwin", "coef", ("oacc", hh)], W=[("o_nsa", m)])


_CACHE = {}


def kernel(**inputs):
    maps = _prep_inputs(inputs)
    if "nc" not in _CACHE:
        _CACHE["nc"] = build(False)
    nc = _CACHE["nc"]
    res = run_bass_kernel_spmd(nc, maps, core_ids=list(range(8)))
    out = np.zeros((2, 4096, 2048), np.float32)
    for c in range(8):
        b, s = c // 4, c % 4
        r = res.results[c]["out"]
        for m in range(8):
            j = 4 * m + s
            out[b, j * 128:(j + 1) * 128, :] = r[m * 128:(m + 1) * 128, :]
    return out
```

```python
from contextlib import ExitStack
import numpy as np
import ml_dtypes
import concourse.bass as bass
import concourse.mybir as mybir
from concourse.bass_utils import run_bass_kernel_spmd

F32 = mybir.dt.float32
BF16 = mybir.dt.bfloat16
U32 = mybir.dt.uint32
AF = mybir.ActivationFunctionType
ALU = mybir.AluOpType
AX = mybir.AxisListType

NEG = -30000.0
LN_EPS = 1e-5
DN_ALPHA = 2.0 ** 0.25


class KB:
    NDMA = 12
    SEM_ROLL = 30000

    def __init__(self, nc):
        self.nc = nc
        self.st = ExitStack()
        self.E = dict(pe=nc.tensor, act=nc.scalar, dve=nc.vector, pool=nc.gpsimd, sp=nc.sync)
        self.esem = {}
        self.ecnt = {}
        self.nsem = 0
        for e in self.E:
            self._new_esem(e)
        self.dsem = {}
        self.dcnt = {}
        for q in ("sp", "pool"):
            self.dsem[q] = [self._sem() for _ in range(self.NDMA)]
            self.dcnt[q] = 0
        self.waited = {e: {} for e in self.E}
        self.lastw = {}
        self.readers = {}
        self.n_wait = 0
        self.n_ops = 0

    def _sem(self):
        self.nsem += 1
        return self.st.enter_context(self.nc.semaphore("ks%d" % self.nsem))

    def _new_esem(self, e):
        self.esem[e] = self._sem()
        self.ecnt[e] = 0

    def sb(self, name, shape, dt, st=None):
        self.nsem += 1
        name = "%s_%d" % (name, self.nsem)
        return (st or self.st).enter_context(self.nc.sbuf_tensor("s_" + name, list(shape), dt))

    def ps(self, name, shape, dt=F32, st=None):
        self.nsem += 1
        name = "%s_%d" % (name, self.nsem)
        return (st or self.st).enter_context(self.nc.psum_tensor("p_" + name, list(shape), dt))

    def _wait(self, eng, tok):
        sem, val, src = tok
        w = self.waited[eng]
        sid = id(sem)
        if w.get(sid, 0) >= val:
            return
        w[sid] = val
        self.E[eng].wait_ge(sem, val)
        self.n_wait += 1

    def _deps(self, eng, R, W, is_dma):
        for r in R:
            t = self.lastw.get(r)
            if t is not None:
                if is_dma or not (t[2] == eng and eng == "pe"):
                    self._wait(eng, t)
        for w_ in W:
            t = self.lastw.get(w_)
            if t is not None:
                if is_dma or not (t[2] == eng and eng == "pe"):
                    self._wait(eng, t)
            for t in self.readers.get(w_, ()):
                if is_dma or t[2] != eng:
                    self._wait(eng, t)

    def _record(self, tok, R, W):
        for r in R:
            lst = self.readers.setdefault(r, [])
            lst[:] = [t for t in lst if not (t[2] == tok[2] and t[0] is tok[0])]
            lst.append(tok)
        for w_ in W:
            self.lastw[w_] = tok
            self.readers[w_] = []

    def op(self, eng, fn, R=(), W=()):
        self._deps(eng, R, W, False)
        if self.ecnt[eng] >= self.SEM_ROLL:
            self._new_esem(eng)
        inst = fn(self.E[eng])
        self.ecnt[eng] += 1
        inst.then_inc(self.esem[eng], 1)
        tok = (self.esem[eng], self.ecnt[eng], eng)
        self._record(tok, R, W)
        self.n_ops += 1
        return tok

    def dma(self, q, out, in_, R=(), W=(), **kw):
        if out.dtype != in_.dtype:
            q = "pool"
        self._deps(q, R, W, True)
        k = self.dcnt[q]
        slot = k % self.NDMA
        sem = self.dsem[q][slot]
        val = 16 * (k // self.NDMA + 1)
        if val > 16:
            self._wait(q, (sem, val - 16, "dma_" + q))
        self.E[q].dma_start(out=out, in_=in_, **kw).then_inc(sem, 16)
        self.dcnt[q] += 1
        tok = (sem, val, "dma_" + q)
        self._record(tok, R, W)
        return tok

    def wait_keys(self, eng, keys):
        for k_ in keys:
            t = self.lastw.get(k_)
            if t is not None:
                self._wait(eng, t)

    def barrier(self):
        toks = []
        for e in self.E:
            if self.ecnt[e] > 0:
                toks.append((self.esem[e], self.ecnt[e], e))
        for q in self.dsem:
            k = self.dcnt[q]
            for slot in range(self.NDMA):
                n = (k - slot + self.NDMA - 1) // self.NDMA
                if n > 0:
                    toks.append((self.dsem[q][slot], 16 * n, "dma_" + q))
        for e in self.E:
            for t in toks:
                if t[2] == e and e == "pe":
                    continue
                self._wait(e, t)
        self.lastw.clear()
        self.readers.clear()


def bc(ap, n, axis=1):
    shp = list(ap.shape)
    shp.insert(axis, n)
    return ap.unsqueeze(axis).to_broadcast(shp)


def _bf16_round(a):
    return np.asarray(a, np.float32).astype(ml_dtypes.bfloat16).astype(np.float32)


def _common_consts():
    c = {}
    c["ident"] = np.eye(128, dtype=np.float32)
    h = np.arange(16)
    sl = (2.0 ** (-8.0 * (h + 1) / 16)).astype(np.float64)
    s_hi = _bf16_round(sl).astype(np.float64)
    s_lo = _bf16_round(sl - s_hi).astype(np.float64)
    qaug = np.zeros((4, 5, 8, 4, 128), np.float32)
    for g in range(4):
        for hh in range(4):
            hd = 4 * g + hh
            for m in range(8):
                ref = 128 * (4 * m + 3) + 64 - 2048
                qaug[g, 0, m, hh, :] = 8 * s_hi[hd]
                qaug[g, 1, m, hh, :] = 8 * s_lo[hd]
                qaug[g, 2, m, hh, :] = 8 * s_hi[hd]
                qaug[g, 3, m, hh, :] = 8 * s_lo[hd]
                qaug[g, 4, m, hh, :] = -8 * sl[hd] * ref
    c["qaug"] = qaug.reshape(4, 5, 8 * 4 * 128)

    def aug_rows(P):
        P = np.asarray(P, np.int64)
        hi = 128 * np.floor_divide(P, 128)
        lo = P - hi
        return np.stack([hi, hi, lo, lo, np.ones_like(P)]).astype(np.float32)

    c["kaug"] = aug_rows(np.arange(4096) - 2048)
    pc = np.zeros(256, np.int64)
    pc[:255] = 16 * np.arange(255) + 31 - 2048
    c["kcaug"] = aug_rows(pc)
    ex = np.zeros((64, 32, 128), np.float32)
    for kt in range(32):
        ex[2 * kt, kt, :64] = 1
        ex[2 * kt + 1, kt, 64:] = 1
    c["expand"] = ex.reshape(64, 32 * 128)
    n = np.arange(256)[:, None]
    jb = np.arange(64)[None, :]
    ov = ((16 * n < 64 * jb + 64) & (16 * n + 32 > 64 * jb) & (n < 255)).astype(np.float32)
    ovc = np.zeros((256, 65), np.float32)
    ovc[:255, 0] = 1.0
    ovc[:, 1:] = ov
    c["ovc"] = ovc.reshape(2, 128, 65).transpose(1, 0, 2).copy()
    k = np.arange(128)[:, None]
    q = np.arange(128)[None, :]
    c["tri_lo"] = np.where(k > q, NEG, 0.0).astype(np.float32)
    c["tri_up"] = np.where(k <= q, NEG, 0.0).astype(np.float32)
    return c


def _core_consts(s):
    c = {}
    npre = 3 - s
    nn = np.arange(128)[:, None, None, None]
    m = np.arange(8)[None, :, None, None]
    ch = np.arange(2)[None, None, :, None]
    q = np.arange(128)[None, None, None, :]
    n_ = ch * 128 + nn
    valid = (n_ >= 8 * npre) & (n_ <= 254) & (16 * n_ + 31 <= 128 * (4 * m + 3) + q)
    c["maskc"] = np.where(valid, 0.0, NEG).astype(np.float32).reshape(128, 8 * 2 * 128)
    k = np.arange(128)[:, None]
    qq = np.arange(128)[None, :]
    wm = np.zeros((128, 5, 128), np.float32)
    for D in range(5):
        if s - D < 0:
            wm[:, D, :] = NEG
        elif D == 0:
            wm[:, D, :] = np.where(k > qq, NEG, 0.0)
    c["winmask0"] = wm.reshape(128, 5 * 128)
    qv = np.arange(128)[:, None, None]
    mv = np.arange(8)[None, :, None]
    jb = np.arange(64)[None, None, :]
    jt = jb - 2 * npre
    tq = (4 * mv + s) * 128 + qv
    cur = tq // 64
    real = jt >= 0
    vis = real & (64 * jt <= tq)
    forced = (jt == 0) | (jt == cur) | (jt == cur - 1)
    c["vis"] = vis.astype(np.float32).reshape(128, 8 * 64)
    c["cstb"] = np.where(vis, 1.0e4 * forced, np.where(real, -1.0, -2.0)).astype(np.float32).reshape(128, 8 * 64)
    return c


def _prep_inputs(inp):
    x = np.asarray(inp["x"], np.float32)
    w_in = np.asarray(inp["w_in"], np.float32)[0]
    com = _common_consts()
    wg = np.zeros((4, 2048, 652), np.float32)
    for g in range(4):
        cols = list(range(g * 256, g * 256 + 256))
        for br in (0, 1, 2, 4, 3, 5):
            base = 1024 + (br * 4 + g) * 64
            cols += list(range(base, base + 64))
        cols += list(range(2560 + 12 * g, 2560 + 12 * g + 12))
        wg[g] = w_in[:, cols]
    com["wg"] = wg
    com["wconv"] = np.ascontiguousarray(w_in[:, 2608:4656])
    for nm in ("k", "v"):
        com["w1" + nm] = np.asarray(inp["cmp_w1_" + nm], np.float32)[0]
        com["w2" + nm] = np.asarray(inp["cmp_w2_" + nm], np.float32)[0]
        com["posT" + nm] = np.ascontiguousarray(np.asarray(inp["cmp_pos_" + nm], np.float32)[0].T)
    com["dww"] = np.ascontiguousarray(np.asarray(inp["dw_w"], np.float32)[0].T)
    com["dwb"] = np.ascontiguousarray(np.asarray(inp["dw_b"], np.float32)[0].reshape(8, 128).T)
    com["clg"] = np.ascontiguousarray(np.asarray(inp["conv_ln_g"], np.float32)[0].reshape(8, 128).T)
    com["clb"] = np.ascontiguousarray(np.asarray(inp["conv_ln_b"], np.float32)[0].reshape(8, 128).T)
    com["wout"] = np.asarray(inp["w_out"], np.float32)[0]
    com["ln1g"] = np.broadcast_to(np.asarray(inp["ln1_g"], np.float32)[0][None, :], (128, 2048)).copy()
    com["ln1b"] = np.broadcast_to(np.asarray(inp["ln1_b"], np.float32)[0][None, :], (128, 2048)).copy()
    wq = np.asarray(inp["peer_wq"], np.float32)[0]
    com["wq_l"] = np.ascontiguousarray(wq.reshape(16, 128, 16, 128).transpose(2, 1, 0, 3)).reshape(16, 128, 2048)
    keys = np.asarray(inp["peer_keys"], np.float32)[0]
    com["keysT"] = np.ascontiguousarray(keys.transpose(3, 0, 1, 2)).reshape(128, 2048)
    U = np.asarray(inp["peer_u"], np.float32)[0]
    com["UT_l"] = np.ascontiguousarray(U.reshape(128, 128, 16, 128).transpose(0, 3, 2, 1)).reshape(128, 128, 2048)
    com["V"] = np.asarray(inp["peer_v"], np.float32)[0]
    com["ln2g"] = np.broadcast_to(np.asarray(inp["ln2_g"], np.float32)[0][None, :], (128, 2048)).copy()
    com["ln2b"] = np.broadcast_to(np.asarray(inp["ln2_b"], np.float32)[0][None, :], (128, 2048)).copy()
    maps = []
    for c in range(8):
        b, s = c // 4, c % 4
        sh = (3 - s) * 128
        d = dict(com)
        xT = np.zeros((2048, 4096), np.float32)
        xT[:, sh:] = x[b, :4096 - sh].T
        d["xTs"] = xT
        own = np.concatenate([np.arange((4 * m + s) * 128, (4 * m + s + 1) * 128) for m in range(8)])
        d["x_own"] = np.ascontiguousarray(x[b, own])
        d.update(_core_consts(s))
        maps.append(d)
    return maps


def build(debug=False, stop=None, nch=128, peer_only=False):
    nc = bass.Bass("TRN2", target_bir_lowering=False)
    D = {}

    def din(name, shape, dt=F32):
        D[name] = nc.dram_tensor(name, list(shape), dt, kind="ExternalInput").ap()
        return D[name]

    def dout(name, shape, dt=F32):
        D[name] = nc.dram_tensor(name, list(shape), dt, kind="ExternalOutput").ap()
        return D[name]

    if not peer_only:
        din("xTs", [2048, 4096]); din("x_own", [1024, 2048])
        din("qaug", [4, 5, 4096]); din("kaug", [5, 4096]); din("kcaug", [5, 256])
        din("expand", [64, 4096]); din("ovc", [128, 2, 65]); din("tri_lo", [128, 128]); din("tri_up", [128, 128])
        din("maskc", [128, 2048]); din("winmask0", [128, 640]); din("vis", [128, 512]); din("cstb", [128, 512])
        din("wg", [4, 2048, 652]); din("wconv", [2048, 2048])
        for nm in ("k", "v"):
            din("w1" + nm, [2048, 256]); din("w2" + nm, [256, 64]); din("posT" + nm, [64, 32])
        din("dww", [1024, 31]); din("dwb", [128, 8]); din("clg", [128, 8]); din("clb", [128, 8])
        din("wout", [2048, 2048]); din("ln1g", [128, 2048]); din("ln1b", [128, 2048])
    else:
        din("y_in", [1024, 2048])
    din("ident", [128, 128])
    din("wq_l", [16, 128, 2048]); din("keysT", [128, 2048]); din("UT_l", [nch, 128, 2048]); din("V", [nch * 128, 2048])
    din("ln2g", [128, 2048]); din("ln2b", [128, 2048])
    dout("out", [1024, 2048])
    if debug:
        dout("d_acc", [1024, 2048])
    if debug and not peer_only:
        dout("d_onsa", [128, 8, 1024]); dout("d_oconv", [128, 8, 1024])
        dout("d_kc", [64, 4, 256]); dout("d_vc", [128, 4, 2, 64]); dout("d_q", [64, 4096])
        dout("d_ksel", [64, 4096]); dout("d_vsw", [128, 32 * 2 * 65]); dout("d_gates", [128, 8, 48])
        dout("d_imp", [128, 8, 64]); dout("d_negsel", [128, 8, 64])
    if debug:
        dout("d_S1", [128, 8192]); dout("d_S2", [128, 8192]); dout("d_c3", [128, 64])

    kb = KB(nc)
    with kb.st:
        if peer_only:
            _peer_only(nc, kb, D, debug, stop, nch)
        else:
            _program(nc, kb, D, debug, stop, nch)
        kb.barrier()
    return nc


def _program(nc, kb, D, debug, stop=None, nch=128):
    op, dma = kb.op, kb.dma
    ident = kb.sb("ident", [128, 128], F32)
    identb = kb.sb("identb", [128, 128], BF16)
    epsc = kb.sb("epsc", [128, 1], F32)
    y = kb.sb("y", [128, 8, 2048], F32)
    pA = ExitStack()
    kb.st.enter_context(pA)
    ones_f = kb.sb("ones_f", [128, 128], F32, pA)
    conv_scr = nc.dram_tensor("conv_scr", [128, 8192], BF16, kind="Internal").ap()
    dma("sp", ident[:], D["ident"], W=["ident"])
    dma("pool", identb[:], D["ident"], W=["identb"])
    op("dve", lambda e: e.memset(ones_f[:], 1.0), W=["ones_f"])
    op("dve", lambda e: e.memset(epsc[:], LN_EPS), W=["epsc"])

    xT_d = D["xTs"].rearrange("(dc p) t -> p dc t", p=128)

    with ExitStack() as ph:
        xoh = kb.sb("xoh", [128, 16, 8, 160], BF16, ph)
        o_convT = kb.sb("o_convT", [128, 8, 1024], BF16, ph)
        dww = kb.sb("dww", [128, 8, 31], F32, ph)
        dwb = kb.sb("dwb", [128, 8], F32, ph)
        clg = kb.sb("clg", [128, 8], F32, ph)
        clb = kb.sb("clb", [128, 8], F32, ph)
        call = kb.sb("call", [128, 8, 1024], F32, ph)
        wca = [kb.sb("wca%d" % i, [128, 16, 128], BF16, ph) for i in range(2)]
        wcg = [kb.sb("wcg%d" % i, [128, 16, 128], BF16, ph) for i in range(2)]
        sg = [kb.sb("sg%d" % i, [128, 480], F32, ph) for i in range(2)]
        u = [kb.sb("u%d" % i, [128, 8, 160], BF16, ph) for i in range(2)]
        dgw = [kb.sb("dgw%d" % i, [128, 31, 128], BF16, ph) for i in range(2)]
        psC = [kb.ps("psC%d" % i, [128, 512], F32, ph) for i in range(2)]
        psA = [kb.ps("psA%d" % i, [128, 512], F32, ph) for i in range(2)]
        psG = [kb.ps("psG%d" % i, [128, 512], F32, ph) for i in range(2)]
        psL = [kb.ps("psL%d" % i, [128, 512], F32, ph) for i in range(2)]

        for m in range(8):
            t0 = m * 512 + 352
            dma("pool", xoh[:, :, m, :], xT_d[:, :, t0:t0 + 160], W=[("xoh", m)])
        dma("sp", dww[:], D["dww"].rearrange("(ct p) w -> p ct w", p=128), W=["dww"])
        for nm, t in (("dwb", dwb), ("clg", clg), ("clb", clb)):
            dma("sp", t[:], D[nm], W=[nm])
        wc_d = D["wconv"].rearrange("(dc p) c -> p dc c", p=128)
        xoh_keys = [("xoh", m) for m in range(8)]
        chunks = [(0, 3), (3, 3), (6, 2)]
        k = 0
        for ct in range(8):
            wb = ct % 2
            dma("pool", wca[wb][:], wc_d[:, :, ct * 128:(ct + 1) * 128], W=[("wca", wb)])
            dma("pool", wcg[wb][:], wc_d[:, :, 1024 + ct * 128:1024 + (ct + 1) * 128], W=[("wcg", wb)])
            ub = ct % 2
            for (m0, nm_) in chunks:
                pb = k % 2
                k += 1
                n = nm_ * 160
                for dc in range(16):
                    op("pe", lambda e, dc=dc: e.matmul(psA[pb][:, 0:n].rearrange("p (a b) -> p a b", a=nm_), wca[wb][:, dc, :],
                                                       xoh[:, dc, m0:m0 + nm_, :], start=(dc == 0), stop=(dc == 15)),
                       R=[("wca", wb)] + xoh_keys, W=[("psA", pb)])
                for dc in range(16):
                    op("pe", lambda e, dc=dc: e.matmul(psG[pb][:, 0:n].rearrange("p (a b) -> p a b", a=nm_), wcg[wb][:, dc, :],
                                                       xoh[:, dc, m0:m0 + nm_, :], start=(dc == 0), stop=(dc == 15)),
                       R=[("wcg", wb)] + xoh_keys, W=[("psG", pb)])
                op("act", lambda e: e.activation(out=sg[pb][:, 0:n], in_=psG[pb][:, 0:n], func=AF.Sigmoid),
                   R=[("psG", pb)], W=[("sg", pb)])
                op("dve", lambda e: e.tensor_tensor(out=u[ub][:, m0:m0 + nm_, :].rearrange("p a b -> p (a b)"),
                                                    in0=psA[pb][:, 0:n], in1=sg[pb][:, 0:n], op=ALU.mult),
                   R=[("psA", pb), ("sg", pb)], W=[("u", ub)])
            op("dve", lambda e: e.tensor_tensor(out=dgw[ub][:], in0=bc(identb[:], 31, axis=1), in1=bc(dww[:, ct, :], 128, axis=2), op=ALU.mult),
               R=["identb", "dww"], W=[("dgw", ub)])
            for hf in range(2):
                for w in range(31):
                    op("pe", lambda e, w=w: e.matmul(psC[hf][:].rearrange("p (a b) -> p a b", a=4), dgw[ub][:, w, :],
                                                     u[ub][:, 4 * hf:4 * hf + 4, 2 + w:130 + w], start=(w == 0), stop=(w == 30)),
                       R=[("dgw", ub), ("u", ub)], W=[("psC", hf)])
                op("act", lambda e: e.activation(out=call[:, ct, hf * 512:(hf + 1) * 512], in_=psC[hf][:], func=AF.Identity, bias=dwb[:, ct:ct + 1]),
                   R=[("psC", hf), "dwb"], W=[("call", ct)])
        with ExitStack() as ph2:
            csq = kb.sb("csq", [128, 512], F32, ph2)
            mean = kb.sb("cmean", [128, 512], F32, ph2)
            rstd = kb.sb("crstd", [128, 512], F32, ph2)
            tmp = kb.sb("ctmp", [128, 512], F32, ph2)
            for hf in range(2):
                tsl = slice(hf * 512, (hf + 1) * 512)
                for ct in range(8):
                    op("pe", lambda e, ct=ct: e.matmul(psL[0][:], ones_f[:], call[:, ct, tsl], start=(ct == 0), stop=(ct == 7)),
                       R=["ones_f", ("call", ct)], W=["psL0"])
                for ct in range(8):
                    op("act", lambda e, ct=ct: e.activation(out=csq[:], in_=call[:, ct, tsl], func=AF.Square),
                       R=[("call", ct)], W=["csq"])
                    op("pe", lambda e, ct=ct: e.matmul(psL[1][:], ones_f[:], csq[:], start=(ct == 0), stop=(ct == 7)),
                       R=["ones_f", "csq"], W=["psL1"])
                op("dve", lambda e: e.tensor_scalar(out=mean[:], in0=psL[0][:], scalar1=1.0 / 1024, scalar2=None, op0=ALU.mult),
                   R=["psL0"], W=["cmean"])
                op("dve", lambda e: e.tensor_tensor(out=tmp[:], in0=mean[:], in1=mean[:], op=ALU.mult), R=["cmean"], W=["ctmp"])
                op("dve", lambda e: e.scalar_tensor_tensor(out=rstd[:], in0=psL[1][:], scalar=1.0 / 1024, in1=tmp[:],
                                                           op0=ALU.mult, op1=ALU.subtract),
                   R=["psL1", "ctmp"], W=["crstd"])
                op("act", lambda e: e.activation(out=rstd[:], in_=rstd[:], func=AF.Sqrt, bias=epsc[:, 0:1]), R=["crstd", "epsc"], W=["crstd"])
                op("dve", lambda e: e.reciprocal(out=rstd[:], in_=rstd[:]), R=["crstd"], W=["crstd"])
                for ct in range(8):
                    op("dve", lambda e, ct=ct: e.tensor_tensor(out=tmp[:], in0=call[:, ct, tsl], in1=mean[:], op=ALU.subtract),
                       R=[("call", ct), "cmean"], W=["ctmp"])
                    op("dve", lambda e: e.tensor_tensor(out=tmp[:], in0=tmp[:], in1=rstd[:], op=ALU.mult),
                       R=["ctmp", "crstd"], W=["ctmp"])
                    op("act", lambda e, ct=ct: e.activation(out=o_convT[:, ct, tsl], in_=tmp[:], func=AF.Silu,
                                                            bias=clb[:, ct:ct + 1], scale=clg[:, ct:ct + 1]),
                       R=["ctmp", "clg", "clb"], W=[("o_convT", ct)])
        if debug:
            dma("sp", D["d_oconv"], o_convT[:], R=[("o_convT", ct) for ct in range(8)], W=["d_oconv"])
        dma("sp", conv_scr, o_convT[:].rearrange("p a b -> p (a b)"), R=[("o_convT", ct) for ct in range(8)], W=["conv_scr"])
        kb.barrier()
        if stop == "C":
            return

    expand = kb.sb("expand", [64, 32, 128], BF16, pA)
    tri_lo = kb.sb("tri_lo", [128, 128], BF16, pA)
    tri_up = kb.sb("tri_up", [128, 128], BF16, pA)
    maskc = kb.sb("maskc", [128, 8, 2, 128], BF16, pA)
    winmask0 = kb.sb("winmask0", [128, 5, 128], BF16, pA)
    vis = kb.sb("vis", [128, 8, 64], F32, pA)
    cstb = kb.sb("cstb", [128, 8, 64], F32, pA)
    o_nsa = kb.sb("o_nsa", [128, 8, 1024], BF16, pA)
    gates = kb.sb("gates", [128, 8, 48], F32, pA)
    dma("pool", expand[:].rearrange("p a b -> p (a b)"), D["expand"], W=["expand"])
    dma("pool", tri_lo[:], D["tri_lo"], W=["tri_lo"])
    dma("pool", tri_up[:], D["tri_up"], W=["tri_up"])
    dma("pool", maskc[:].rearrange("p a b c -> p (a b c)"), D["maskc"], W=["maskc"])
    dma("pool", winmask0[:].rearrange("p a b -> p (a b)"), D["winmask0"], W=["winmask0"])
    dma("sp", vis[:].rearrange("p a b -> p (a b)"), D["vis"], W=["vis"])
    dma("sp", cstb[:].rearrange("p a b -> p (a b)"), D["cstb"], W=["cstb"])
    if debug:
        op("pool", lambda e: e.memset(o_nsa[:], 0.0), W=["o_nsa_init"])
        op("pool", lambda e: e.memset(gates[:], 0.0), W=["gates_init"])

    with ExitStack() as phg:
        kT_sel = kb.sb("kT_sel", [69, 4096], BF16, phg)
        kT_win = kb.sb("kT_win", [69, 4096], BF16, phg)
        kvc = kb.sb("kvc", [128, 4096], BF16, phg)
        stg = kb.sb("stg", [128, 4096], BF16, phg)
        V_sw = kb.sb("V_sw", [128, 32, 2, 65], BF16, phg)
        qT = kb.sb("qT", [69, 8, 4, 128], BF16, phg)
        kc_aug = kb.sb("kc_aug", [69, 256], BF16, phg)
        vcx = kb.sb("vcx", [128, 2, 129], BF16, phg)
        cbias = kb.sb("cbias", [128, 2, 2], F32, phg)
        dma("pool", kT_sel[64:69, :], D["kaug"], W=["kT_sel_aug"])
        dma("pool", kT_win[64:69, :], D["kaug"], W=["kT_win_aug"])
        dma("pool", kc_aug[64:69, :], D["kcaug"], W=["kc_aug_aug"])
        dma("pool", vcx[:, :, 64:129], D["ovc"], W=["vcx_c"])
        op("pool", lambda e: e.memset(V_sw[:, :, :, 64:65], 1.0), W=["V_ones"])
        op("pool", lambda e: e.memset(kc_aug[0:64, 255:256], 0.0), W=["kc_pad"])

        for g in range(4):
            with ExitStack() as ph:
                wgs = kb.sb("wgs", [128, 16, 652], BF16, ph)
                xt = [kb.sb("xt%d" % i, [128, 16, 512], BF16, ph) for i in range(2)]
                psP = [kb.ps("psP%d" % i, [128, 512], F32, ph) for i in range(2)]
                psQ = kb.ps("psQ", [64, 512], F32, ph)
                psV = kb.ps("psV", [128, 512], F32, ph)
                psGt = kb.ps("psGt", [128, 512], F32, ph)
                dma("pool", wgs[:], D["wg"][g].rearrange("(dc p) c -> p dc c", p=128), W=["wgs"])
                dma("pool", qT[64:69, :, :, :].rearrange("p a b c -> p (a b c)"), D["qaug"][g], W=["qT_aug"])
                pk = 0
                for m in range(8):
                    xb_ = m % 2
                    dma("pool", xt[xb_][:], xT_d[:, :, m * 512:(m + 1) * 512], W=[("xt", xb_)])
                    pairs = [(256, kvc, "kvc_lo", kvc, "kvc_hi"), (384, kT_sel, "kT_sel", stg, "stg")]
                    for (off, dlo, nlo, dhi, nhi) in pairs:
                        pb = pk % 2
                        pk += 1
                        for dc in range(16):
                            op("pe", lambda e, dc=dc: e.matmul(psP[pb][:], wgs[:, dc, off:off + 128], xt[xb_][:, dc, :],
                                                               start=(dc == 0), stop=(dc == 15)),
                               R=["wgs", ("xt", xb_)], W=[("psP", pb)])
                        op("act", lambda e: e.activation(out=dlo[0:64, m * 512:(m + 1) * 512], in_=psP[pb][0:64, :], func=AF.Identity),
                           R=[("psP", pb)], W=[(nlo, m)])
                        op("dve", lambda e: e.tensor_copy(out=dhi[64:128, m * 512:(m + 1) * 512], in_=psP[pb][64:128, :]),
                           R=[("psP", pb)], W=[(nhi, m), ("psP", pb)])
                    for dc in range(16):
                        for hh in range(4):
                            op("pe", lambda e, dc=dc, hh=hh: e.matmul(psQ[:, hh * 128:(hh + 1) * 128],
                                                                      wgs[:, dc, hh * 64:(hh + 1) * 64], xt[xb_][:, dc, 384:512],
                                                                      start=(dc == 0 and hh == 0), stop=(dc == 15),
                                                                      skip_group_check=True),
                               R=["wgs", ("xt", xb_)], W=["psQ"])
                    op("act", lambda e: e.activation(out=qT[0:64, m, :, :].rearrange("p a b -> p (a b)"), in_=psQ[:], func=AF.Identity),
                       R=["psQ"], W=[("qT", m)])
                    for dc in range(16):
                        for sub in range(4):
                            op("pe", lambda e, dc=dc, sub=sub: e.matmul(psV[:, sub * 128:(sub + 1) * 128],
                                                                        xt[xb_][:, dc, sub * 128:(sub + 1) * 128], wgs[:, dc, 512:640],
                                                                        start=(dc == 0 and sub == 0), stop=(dc == 15),
                                                                        skip_group_check=True),
                               R=["wgs", ("xt", xb_)], W=["psV"])
                    op("dve", lambda e: e.tensor_copy(out=V_sw[:, 4 * m:4 * m + 4, :, 0:64],
                                                      in_=psV[:].rearrange("p (a b c) -> p a b c", a=4, b=2)),
                       R=["psV", "V_ones"], W=[("V_sw", m)])
                    for dc in range(16):
                        op("pe", lambda e, dc=dc: e.matmul(psGt[:, 0:12], xt[xb_][:, dc, 384:512], wgs[:, dc, 640:652],
                                                           start=(dc == 0), stop=(dc == 15)),
                           R=["wgs", ("xt", xb_)], W=["psGt"])
                    op("act", lambda e: e.activation(out=gates[:, m, 12 * g:12 * g + 12], in_=psGt[:, 0:12], func=AF.Sigmoid),
                       R=["psGt"], W=[("gates", m, g)])
                dma("sp", kT_win[0:64, :], stg[64:128, :], R=[("stg", m) for m in range(8)], W=[("kT_win", m) for m in range(8)])
                if debug and g == 0:
                    dma("sp", D["d_q"], qT[0:64].rearrange("p a b c -> p (a b c)"), R=[("qT", m) for m in range(8)], W=["d_q"])
                    dma("sp", D["d_ksel"], kT_sel[0:64, :], R=[("kT_sel", m) for m in range(8)], W=["d_ksel"])
                    dma("sp", D["d_vsw"], V_sw[:].rearrange("p a b c -> p (a b c)"), R=[("V_sw", m) for m in range(8)] + ["V_ones"], W=["d_vsw"])
                kb.barrier()
                if stop == "G1":
                    return

            with ExitStack() as ph:
                w1kv = kb.sb("w1kv", [128, 32, 256], BF16, ph)
                w2 = [kb.sb("w2_%d" % i, [128, 2, 64], BF16, ph) for i in range(2)]
                posT = kb.sb("posTkv", [128, 32], BF16, ph)
                hid = [kb.sb("hid%d" % i, [128, 2, 256], BF16, ph) for i in range(2)]
                psH = [kb.ps("psH%d" % i, [128, 512], F32, ph) for i in range(2)]
                psB = kb.ps("psB", [128, 512], F32, ph)
                psO = kb.ps("psO", [128, 512], F32, ph)
                for wi, nm in enumerate(("k", "v")):
                    dma("pool", w1kv[wi * 64:wi * 64 + 64], D["w1" + nm].rearrange("(l d) c -> d l c", d=64), W=[("w1", wi)])
                    dma("pool", w2[wi][:], D["w2" + nm].rearrange("(cc p) d -> p cc d", p=128), W=[("w2", wi)])
                    dma("pool", posT[wi * 64:wi * 64 + 64, :], D["posT" + nm], W=[("posT", wi)])
                for wi in range(2):
                    p0 = wi * 64
                    skeys = [("kvc_lo" if wi == 0 else "kvc_hi", m) for m in range(8)]
                    for cc in range(2):
                        for l in range(32):
                            op("pe", lambda e, l=l, cc=cc: e.matmul(psB[:, wi * 2 + cc:wi * 2 + cc + 1], w1kv[p0:p0 + 64, l, cc * 128:(cc + 1) * 128],
                                                                    posT[p0:p0 + 64, l:l + 1], start=(l == 0 and cc == 0 and wi == 0), stop=(l == 31),
                                                                    skip_group_check=True),
                               R=[("w1", wi), ("posT", wi)], W=["psB"])
                    op("dve", lambda e: e.tensor_copy(out=cbias[:, wi, :], in_=psB[:, wi * 2:wi * 2 + 2]), R=["psB"], W=[("cbias", wi)])
                    for cc in range(2):
                        for l in range(32):
                            op("pe", lambda e, l=l, cc=cc: e.matmul(psH[cc][:, 0:255], w1kv[p0:p0 + 64, l, cc * 128:(cc + 1) * 128],
                                                                    kvc[p0:p0 + 64, l:l + 16 * 254 + 1:16], start=(l == 0), stop=(l == 31)),
                               R=[("w1", wi)] + skeys, W=[("psH", cc)])
                        op("act", lambda e, cc=cc: e.activation(out=hid[wi][:, cc, 0:255], in_=psH[cc][:, 0:255], func=AF.Gelu_apprx_tanh,
                                                                bias=cbias[:, wi, cc:cc + 1]),
                           R=[("psH", cc), ("cbias", wi)], W=[("hid", wi, cc)])
                for cc in range(2):
                    op("pe", lambda e, cc=cc: e.matmul(psO[0:64, 0:255], w2[0][:, cc, :], hid[0][:, cc, 0:255], start=(cc == 0), stop=(cc == 1)),
                       R=[("w2", 0), ("hid", 0, cc)], W=["psO"])
                op("dve", lambda e: e.tensor_copy(out=kc_aug[0:64, 0:255], in_=psO[0:64, 0:255]), R=["psO", "kc_pad"], W=["kc_aug"])
                for ch in range(2):
                    ncol = 128 if ch == 0 else 127
                    for cc in range(2):
                        op("pe", lambda e, cc=cc: e.matmul(psO[0:ncol, 256 + ch * 64:256 + ch * 64 + 64], hid[1][:, cc, ch * 128:ch * 128 + ncol],
                                                           w2[1][:, cc, :], start=(cc == 0), stop=(cc == 1), skip_group_check=True),
                           R=[("w2", 1), ("hid", 1, cc)], W=["psO"])
                op("pool", lambda e: e.memset(vcx[:, 1, 0:64], 0.0), W=["vcx_pad"])
                op("dve", lambda e: e.tensor_copy(out=vcx[:, 0, 0:64], in_=psO[:, 256:320]), R=["psO", "vcx_c"], W=["vcx0"])
                op("dve", lambda e: e.tensor_copy(out=vcx[0:127, 1, 0:64], in_=psO[0:127, 320:384]), R=["psO", "vcx_pad", "vcx_c"], W=["vcx1"])
                if debug:
                    dma("sp", D["d_kc"][:, g, :], kc_aug[0:64, :], R=["kc_aug", "kc_pad"], W=[("d_kc", g)])
                    dma("sp", D["d_vc"][:, g, :, :], vcx[:, :, 0:64], R=["vcx0", "vcx1"], W=[("d_vc", g)])
                kb.barrier()
                if stop == "G2":
                    return

            with ExitStack() as ph:
                _attention(nc, kb, D, debug, ph, g, locals())
                kb.barrier()
                if stop == "G3":
                    dma("sp", D["d_onsa"], o_nsa[:], R=[], W=["d_onsa"])
                    dma("sp", D["d_gates"], gates[:], R=[], W=["d_gates"])
                    return
        if debug:
            dma("sp", D["d_onsa"], o_nsa[:], R=[], W=["d_onsa"])
            dma("sp", D["d_gates"], gates[:], R=[], W=["d_gates"])
            kb.barrier()

    with ExitStack() as ph:
        o_nsaT = kb.sb("o_nsaT", [128, 8, 1024], BF16, ph)
        o_convT = kb.sb("o_convT2", [128, 8, 1024], BF16, ph)
        dma("sp", o_convT[:].rearrange("p a b -> p (a b)"), conv_scr, W=["o_convT2"])
        woutc = [kb.sb("woutc%d" % i, [128, 16, 512], BF16, ph) for i in range(2)]
        g1 = kb.sb("ln1g", [128, 2048], F32, ph)
        b1 = kb.sb("ln1b", [128, 2048], F32, ph)
        xo = [kb.sb("xo%d" % i, [128, 512], F32, ph) for i in range(2)]
        stats = kb.sb("stats", [128, 4, 6], F32, ph)
        mv = kb.sb("mv", [128, 2], F32, ph)
        rs = kb.sb("rs", [128, 1], F32, ph)
        psT = [kb.ps("psT%d" % i, [128, 512], BF16, ph) for i in range(2)]
        psM = [kb.ps("psM%d" % i, [128, 512], F32, ph) for i in range(2)]
        wo_d = D["wout"].rearrange("(fc p) c -> p fc c", p=128)
        dma("sp", g1[:], D["ln1g"], W=["ln1g"])
        dma("sp", b1[:], D["ln1b"], W=["ln1b"])
        k = 0
        for m in range(8):
            for fc in range(8):
                pb = k % 2
                k += 1
                op("pe", lambda e, fc=fc: e.transpose(psT[pb][:, 0:128], o_nsa[:, m, fc * 128:(fc + 1) * 128], identb[:]),
                   R=[("o_nsa", m), "identb"], W=[("psT", pb)])
                if fc % 2 == 0:
                    op("act", lambda e, fc=fc: e.activation(out=o_nsaT[:, fc, m * 128:(m + 1) * 128], in_=psT[pb][:, 0:128], func=AF.Identity),
                       R=[("psT", pb)], W=[("o_nsaT", m)])
                else:
                    op("dve", lambda e, fc=fc: e.tensor_copy(out=o_nsaT[:, fc, m * 128:(m + 1) * 128], in_=psT[pb][:, 0:128]),
                       R=[("psT", pb)], W=[("o_nsaT", m)])
        k = 0
        for cc in range(4):
            wb = cc % 2
            dma("pool", woutc[wb][:], wo_d[:, :, cc * 512:(cc + 1) * 512], W=[("woutc", wb)])
            for m in range(8):
                pb = k % 2
                k += 1
                dma("sp", xo[pb][:], D["x_own"][m * 128:(m + 1) * 128, cc * 512:(cc + 1) * 512], W=[("xo", pb)])
                for fc in range(16):
                    src = o_nsaT[:, fc, m * 128:(m + 1) * 128] if fc < 8 else o_convT[:, fc - 8, m * 128:(m + 1) * 128]
                    op("pe", lambda e, fc=fc, src=src: e.matmul(psM[pb][:], src, woutc[wb][:, fc, :], start=(fc == 0), stop=(fc == 15)),
                       R=[("o_nsaT", m), ("woutc", wb), "o_convT2"], W=[("psM", pb)])
                op("dve", lambda e: e.scalar_tensor_tensor(out=y[:, m, cc * 512:(cc + 1) * 512], in0=xo[pb][:], scalar=DN_ALPHA, in1=psM[pb][:],
                                                           op0=ALU.mult, op1=ALU.add),
                   R=[("xo", pb), ("psM", pb)], W=[("y", m)])
                if cc == 3:
                    _layer_norm(kb, y[:, m, :], y[:, m, :], g1, b1, stats, mv, rs, [("y", m)], [("y", m)], ["ln1g", "ln1b"], epsc)
        kb.barrier()

    pA.close()
    if stop == "O":
        for m in range(8):
            dma("sp", D["out"][m * 128:(m + 1) * 128, :], y[:, m, :], R=[("y", m)], W=[("out", m)])
        return
    _peer(nc, kb, D, debug, stop, y, ident, identb, epsc, nch)


def _peer_only(nc, kb, D, debug, stop, nch):
    op, dma = kb.op, kb.dma
    ident = kb.sb("ident", [128, 128], F32)
    identb = kb.sb("identb", [128, 128], BF16)
    epsc = kb.sb("epsc", [128, 1], F32)
    y = kb.sb("y", [128, 8, 2048], F32)
    dma("sp", ident[:], D["ident"], W=["ident"])
    dma("pool", identb[:], D["ident"], W=["identb"])
    op("dve", lambda e: e.memset(epsc[:], LN_EPS), W=["epsc"])
    for m in range(8):
        dma("sp", y[:, m, :], D["y_in"][m * 128:(m + 1) * 128, :], W=[("y", m)])
    _peer(nc, kb, D, debug, stop, y, ident, identb, epsc, nch)


def _peer(nc, kb, D, debug, stop, y, ident, identb, epsc, nch):
    op, dma = kb.op, kb.dma
    with ExitStack() as pp:
        yT = kb.sb("yT", [128, 16, 1024], BF16, pp)
        S2 = kb.sb("S2", [128, 8, 8, 128], F32, pp)
        c3 = kb.sb("c3", [128, 8, 8], F32, pp)
        ec3 = kb.sb("ec3", [128, 8, 8], F32, pp)
        S1d = nc.dram_tensor("S1d", [128, 128, 64], F32, kind="Internal").ap()
        ykeys = [("y", m) for m in range(8)]
        with ExitStack() as ph:
            S1 = kb.sb("S1", [128, 8, 8, 128], F32, ph)
            s1stg = kb.sb("s1stg", [128, 16, 64], F32, ph)
            qc = [kb.sb("qc%d" % i, [128, 1024], BF16, ph) for i in range(2)]
            wqc = [kb.sb("wqc%d" % i, [128, 16, 128], BF16, ph) for i in range(2)]
            keysT = kb.sb("keysT", [128, 16, 128], BF16, ph)
            v32 = kb.sb("v32", [128, 8, 32], F32, ph)
            tmpab = [kb.sb("tmpab%d" % i, [128, 256], F32, ph) for i in range(8)]
            tmpa = [t[:, 0:128] for t in tmpab]
            tmpb = [t[:, 128:256] for t in tmpab]
            cand = [kb.sb("cand%d" % i, [128, 256], F32, ph) for i in range(8)]
            cand2 = tmpab
            t24 = kb.sb("t24", [128, 8, 8, 24], F32, ph)
            d16 = kb.sb("d16", [128, 8, 16], F32, ph)
            sm = kb.sb("sm", [128, 8, 8], F32, ph)
            psA = [kb.ps("ppA%d" % i, [128, 512], F32, ph) for i in range(2)]
            psS = [kb.ps("ppS%d" % i, [128, 1024], F32, ph) for i in range(2)]
            dma("pool", keysT[:].rearrange("p a b -> p (a b)"), D["keysT"], W=["keysT"])
            k = 0
            for m in range(8):
                for dc4 in range(4):
                    pb = k % 2
                    k += 1
                    for i in range(4):
                        dc = dc4 * 4 + i
                        op("pe", lambda e, i=i, dc=dc: e.transpose(psA[pb][:, i * 128:(i + 1) * 128], y[:, m, dc * 128:(dc + 1) * 128], ident[:]),
                           R=[("y", m), "ident"], W=[("ppA", pb)])
                    dst = yT[:, dc4 * 4:dc4 * 4 + 4, m * 128:(m + 1) * 128]
                    if k % 2 == 0:
                        op("act", lambda e: e.activation(out=dst, in_=psA[pb][:].rearrange("p (a b) -> p a b", a=4), func=AF.Identity),
                           R=[("ppA", pb)], W=[("yT", m)])
                    else:
                        op("dve", lambda e: e.tensor_copy(out=dst, in_=psA[pb][:].rearrange("p (a b) -> p a b", a=4)),
                           R=[("ppA", pb)], W=[("yT", m)])
            yTk = [("yT", m) for m in range(8)]
            for m in range(8):
                op("act", lambda e: e.activation(out=y[:, m, :], in_=y[:, m, :], func=AF.Identity, scale=DN_ALPHA),
                   R=[("y", m)], W=[("y", m)])
            k = 0
            for c16 in range(16):
                wb = c16 % 2
                dma("pool", wqc[wb][:].rearrange("p a b -> p (a b)"), D["wq_l"][c16], W=[("wqc", wb)])
                for hf in range(2):
                    pb = k % 2
                    k += 1
                    for dc in range(16):
                        op("pe", lambda e, dc=dc: e.matmul(psA[pb][:], wqc[wb][:, dc, :], yT[:, dc, hf * 512:(hf + 1) * 512],
                                                           start=(dc == 0), stop=(dc == 15)),
                           R=[("wqc", wb)] + yTk, W=[("ppA", pb)])
                    op("act", lambda e: e.activation(out=qc[wb][:, hf * 512:(hf + 1) * 512], in_=psA[pb][:], func=AF.Identity),
                       R=[("ppA", pb)], W=[("qc", wb)])
                for m in range(8):
                    op("pe", lambda e, m=m: e.matmul(psS[wb][:, m * 128:(m + 1) * 128], qc[wb][:, m * 128:(m + 1) * 128], keysT[:, c16, :],
                                                     start=(m % 4 == 0), stop=True, skip_group_check=True),
                       R=[("qc", wb), "keysT"], W=[("ppS", wb)])
                S, sn = (S1, "S1") if c16 % 2 == 0 else (S2, "S2")
                hh = c16 // 2
                op("act", lambda e: e.activation(out=S[:, :, hh, :], in_=psS[wb][:].rearrange("p (a b) -> p a b", a=8), func=AF.Identity),
                   R=[("ppS", wb)], W=[(sn, hh)])
                if c16 % 2 == 1:
                    h = hh
                    chains = []
                    for m in range(8):
                        c = m
                        ta, tb, ca, cb = tmpa[c], tmpb[c], cand[c], cand2[c]
                        vv = v32[:, m, :]
                        tt = t24[:, m, h, :]
                        steps = [
                            lambda m=m, vv=vv, c=c: op("dve", lambda e: e.max(out=vv[:, 0:8], in_=S1[:, m, h, :]), R=[("S1", h)], W=[("v32a", c)]),
                            lambda m=m, vv=vv, c=c: op("dve", lambda e: e.max(out=vv[:, 16:24], in_=S2[:, m, h, :]), R=[("S2", h)], W=[("v32b", c)]),
                            lambda m=m, vv=vv, ta=ta, c=c: op("dve", lambda e: e.match_replace(out=ta, in_to_replace=vv[:, 0:8], in_values=S1[:, m, h, :], imm_value=-1e30),
                                                              R=[("S1", h), ("v32a", c)], W=[("tmpa", c)]),
                            lambda m=m, vv=vv, tb=tb, c=c: op("dve", lambda e: e.match_replace(out=tb, in_to_replace=vv[:, 16:24], in_values=S2[:, m, h, :], imm_value=-1e30),
                                                              R=[("S2", h), ("v32b", c)], W=[("tmpb", c)]),
                            lambda vv=vv, ta=ta, c=c: op("dve", lambda e: e.max(out=vv[:, 8:16], in_=ta), R=[("tmpa", c)], W=[("v32c", c)]),
                            lambda vv=vv, tb=tb, c=c: op("dve", lambda e: e.max(out=vv[:, 24:32], in_=tb), R=[("tmpb", c)], W=[("v32d", c)]),
                            lambda vv=vv, ca=ca, c=c: op("dve", lambda e: e.tensor_tensor(out=ca[:].rearrange("p (a b) -> p a b", a=16), in0=bc(vv[:, 0:16], 16, axis=2),
                                                                                          in1=bc(vv[:, 16:32], 16, axis=1), op=ALU.add),
                                                         R=[("v32a", c), ("v32b", c), ("v32c", c), ("v32d", c)], W=[("cand", c)]),
                            lambda tt=tt, ca=ca, c=c, m=m: op("dve", lambda e: e.max(out=tt[:, 0:8], in_=ca[:]), R=[("cand", c)], W=[("t24a", m, h)]),
                            lambda tt=tt, ca=ca, cb=cb, c=c, m=m: op("dve", lambda e: e.match_replace(out=cb[:], in_to_replace=tt[:, 0:8], in_values=ca[:], imm_value=-1e30),
                                                                     R=[("cand", c), ("t24a", m, h)], W=[("cand2", c)]),
                            lambda tt=tt, cb=cb, c=c, m=m: op("dve", lambda e: e.max(out=tt[:, 8:16], in_=cb[:]), R=[("cand2", c)], W=[("t24b", m, h)]),
                            lambda tt=tt, ca=ca, cb=cb, c=c, m=m: op("dve", lambda e: e.match_replace(out=ca[:], in_to_replace=tt[:, 8:16], in_values=cb[:], imm_value=-1e30),
                                                                     R=[("cand2", c), ("t24b", m, h)], W=[("cand", c)]),
                            lambda tt=tt, ca=ca, c=c, m=m: op("dve", lambda e: e.max(out=tt[:, 16:24], in_=ca[:]), R=[("cand", c)], W=[("t24c", m, h)]),
                        ]
                        chains.append(steps)
                    for si in range(len(chains[0])):
                        for c in range(8):
                            chains[c][si]()
            S1k = [("S1", h) for h in range(8)]
            S2k = [("S2", h) for h in range(8)]
            for m in range(8):
                t24k = [(k_, m, h) for k_ in ("t24a", "t24b", "t24c") for h in range(8)]
                op("dve", lambda e: e.tensor_copy(out=sm[:, 0, :], in_=t24[:, m, :, 0]), R=t24k, W=["sm0"])
                op("dve", lambda e: e.tensor_tensor(out=d16[:], in0=t24[:, m, :, 0:16], in1=bc(sm[:, 0, :], 16, axis=2), op=ALU.subtract),
                   R=t24k + ["sm0"], W=["d16"])
                op("act", lambda e: e.activation(out=d16[:], in_=d16[:], func=AF.Exp), R=["d16"], W=["d16"])
                op("dve", lambda e: e.tensor_reduce(out=sm[:, 1, :], in_=d16[:], axis=AX.X, op=ALU.add), R=["d16"], W=["sm1"])
                op("act", lambda e: e.activation(out=sm[:, 2, :], in_=sm[:, 1, :], func=AF.Ln), R=["sm1"], W=["sm2"])
                op("dve", lambda e: e.tensor_tensor(out=sm[:, 3, :], in0=sm[:, 0, :], in1=sm[:, 2, :], op=ALU.add), R=["sm0", "sm2"], W=["sm3"])
                op("dve", lambda e: e.tensor_tensor(out=sm[:, 4, :], in0=t24[:, m, :, 15], in1=t24[:, m, :, 16], op=ALU.add), R=t24k, W=["sm4"])
                op("dve", lambda e: e.scalar_tensor_tensor(out=c3[:, m, :], in0=sm[:, 4, :], scalar=0.5, in1=sm[:, 3, :], op0=ALU.mult, op1=ALU.subtract),
                   R=["sm4", "sm3"], W=[("c3", m)])
                op("dve", lambda e: e.tensor_scalar(out=sm[:, 5, :], in0=sm[:, 4, :], scalar1=0.5, scalar2=None, op0=ALU.mult), R=["sm4"], W=["sm5"])
                op("dve", lambda e: e.tensor_tensor(out=S2[:, m, :, :], in0=S2[:, m, :, :], in1=bc(sm[:, 5, :], 128, axis=2), op=ALU.subtract),
                   R=S2k + ["sm5"], W=[("S2f", m)])
                op("act", lambda e: e.activation(out=S2[:, m, :, :].rearrange("p a b -> p (a b)"), in_=S2[:, m, :, :].rearrange("p a b -> p (a b)"), func=AF.Exp),
                   R=[("S2f", m)], W=[("S2f", m)])
            op("act", lambda e: e.activation(out=ec3[:], in_=c3[:], func=AF.Exp), R=[("c3", m) for m in range(8)], W=["ec3"])
            if debug:
                dma("sp", D["d_S1"], S1[:].rearrange("p a b c -> p (a b c)"), R=S1k, W=["d_S1"])
            for m in range(8):
                op("act", lambda e: e.activation(out=S1[:, m, :, :].rearrange("p a b -> p (a b)"), in_=S1[:, m, :, :].rearrange("p a b -> p (a b)"), func=AF.Exp),
                   R=S1k + ["d_S1"], W=S1k)
            S1v = S1[:].rearrange("p m h i -> p i (m h)")
            for ib in range(8):
                op("act", lambda e: e.activation(out=s1stg[:], in_=S1v[:, ib * 16:(ib + 1) * 16, :], func=AF.Identity), R=S1k, W=["s1stg"])
                dma("sp", S1d[:, ib * 16:(ib + 1) * 16, :], s1stg[:], R=["s1stg"], W=[("S1d", ib)])
            if debug:
                dma("sp", D["d_S2"], S2[:].rearrange("p a b c -> p (a b c)"), R=[("S2f", m) for m in range(8)], W=["d_S2"])
                dma("sp", D["d_c3"], c3[:].rearrange("p a b -> p (a b)"), R=[("c3", m) for m in range(8)], W=["d_c3"])
            kb.barrier()
        if stop == "P4":
            return
        NCH = nch
        GC = 2
        with ExitStack() as ph:
            UT = [kb.sb("UT%d" % i, [128, 16, 128], BF16, ph) for i in range(2)]
            Vg = [kb.sb("Vg%d" % i, [128, GC, 2048], BF16, ph) for i in range(2)]
            GH = [kb.sb("GH%d" % i, [128, GC, 1024], BF16, ph) for i in range(2)]
            NRB = 3
            POOL_TILES = (2, 6)
            EtB = [kb.sb("EtB%d" % i, [128, 8, 128], F32, ph) for i in range(NRB)]
            NRM = 3
            mkB = [kb.sb("mkB%d" % i, [128, 8, 128], BF16, ph) for i in range(NRM)]
            s1c = [kb.sb("s1c%d" % i, [128, 64], F32, ph) for i in range(3)]
            Dg = kb.sb("Dg", [128, 8, 8, 128], BF16, ph)
            for m in range(8):
                for h in range(8):
                    op("dve", lambda e: e.tensor_scalar(out=Dg[:, m, h, :], in0=identb[:], scalar1=ec3[:, m, h:h + 1], scalar2=None, op0=ALU.mult),
                       R=[], W=[("Dg", m)])
            psH = kb.ps("psH", [128, 1024], F32, ph)
            psG = kb.ps("psG", [128, 1024], F32, ph)
            psY = [kb.ps("psY%d" % i, [128, 1024], F32, ph) for i in range(2)]
            yTk = [("yT", m) for m in range(8)]
            skeys = [("S1", m) for m in range(8)] + [("S2", m) for m in range(8)] + [("c3", m) for m in range(8)]

            def load_u(i):
                dma("pool", UT[i % 2][:].rearrange("p a b -> p (a b)"), D["UT_l"][i], W=[("UT", i % 2)])

            def load_v(i):
                gi = i // GC
                dma("pool", Vg[gi % 2][:, i % GC, :], D["V"][i * 128:(i + 1) * 128, :], W=[("Vg", gi % 2, i % GC)])

            def a_part(i, p):
                hf = p // 4
                for dc in range((p % 4) * 4, (p % 4) * 4 + 4):
                    op("pe", lambda e, dc=dc: e.matmul(psH[:, hf * 512:(hf + 1) * 512], UT[i % 2][:, dc, :], yT[:, dc, hf * 512:(hf + 1) * 512],
                                                       start=(dc == 0), stop=(dc == 15)),
                       R=[("UT", i % 2)] + yTk, W=["psH"])
                if p == 7:
                    op("act", lambda e: e.activation(out=GH[(i // GC) % 2][:, i % GC, :], in_=psH[:], func=AF.Gelu_apprx_tanh),
                       R=["psH"], W=[("GH", (i // GC) % 2, i % GC)])

            cnt = {"r": 0, "y": 0}

            def load_s1(i):
                dma("sp", s1c[i % 3][:], S1d[:, i, :], R=[("S1d", ib) for ib in range(8)], W=[("s1c", i % 3)])

            def b_front(i, m):
                re = cnt["r"] % NRB
                r = cnt["r"] % NRM
                cnt["r"] += 1
                ek = [("EtB", re, h) for h in range(8)]
                if m in POOL_TILES:
                    op("pool", lambda e: e.tensor_tensor(out=EtB[re][:], in0=S2[:, m, :, :], in1=bc(s1c[i % 3][:, m * 8:(m + 1) * 8], 128, axis=2), op=ALU.mult),
                       R=[("s1c", i % 3)], W=ek)
                else:
                    for h in range(8):
                        op("act", lambda e: e.activation(out=EtB[re][:, h, :], in_=S2[:, m, h, :], func=AF.Identity, scale=s1c[i % 3][:, m * 8 + h:m * 8 + h + 1]),
                           R=[("s1c", i % 3)], W=[("EtB", re, h)])
                ef = EtB[re][:].rearrange("p a b -> p (a b)")
                op("dve", lambda e: e.scalar_tensor_tensor(out=mkB[r][:].rearrange("p a b -> p (a b)"), in0=ef, scalar=1.0, in1=ef,
                                                           op0=ALU.is_ge, op1=ALU.mult),
                   R=ek, W=[("mkB", r)] + ek)
                return r

            def b_back(i, m, r):
                for h in range(8):
                    op("pe", lambda e, h=h: e.matmul(psG[:, m * 128:(m + 1) * 128], mkB[r][:, h, :], Dg[:, m, h, :],
                                                     start=(h == 0 and m % 4 == 0), stop=(h == 7), skip_group_check=True),
                       R=[("mkB", r), ("Dg", m)], W=["psG"])

            def gh(i):
                gi = i // GC
                for hf in range(2):
                    dst = GH[gi % 2][:, i % GC, hf * 512:(hf + 1) * 512]
                    op("dve", lambda e: e.tensor_tensor(out=dst, in0=psG[:, hf * 512:(hf + 1) * 512], in1=dst, op=ALU.mult),
                       R=["psG", ("GH", gi % 2, i % GC)], W=[("GH", gi % 2, i % GC)])

            def c_unit(gi, u):
                m, hf = u // 2, u % 2
                yb = cnt["y"] % 2
                cnt["y"] += 1
                for ci in range(GC):
                    for c2 in range(2):
                        col = hf * 1024 + c2 * 512
                        op("pe", lambda e, ci=ci, c2=c2, col=col: e.matmul(psY[yb][:, c2 * 512:(c2 + 1) * 512], GH[gi % 2][:, ci, m * 128:(m + 1) * 128],
                                                                           Vg[gi % 2][:, ci, col:col + 512], start=(ci == 0), stop=(ci == GC - 1)),
                           R=[("GH", gi % 2, c) for c in range(GC)] + [("Vg", gi % 2, c) for c in range(GC)], W=[("psY", yb)])
                op("dve", lambda e: e.tensor_tensor(out=y[:, m, hf * 1024:(hf + 1) * 1024], in0=psY[yb][:], in1=y[:, m, hf * 1024:(hf + 1) * 1024], op=ALU.add),
                   R=[("psY", yb), ("y", m)], W=[("y", m)])

            load_u(0); load_u(1)
            for i in range(2 * GC):
                load_v(i)
            load_s1(0)
            if NCH > 1:
                load_s1(1)
            for p in range(8):
                a_part(0, p)
            queue = []
            for i in range(NCH):
                if i + 2 < NCH:
                    load_u(i + 2)
                    load_s1(i + 2)
                plan = [1] * 8 if i % 2 == 0 else [2, 1, 1, 1, 1, 1, 1, 0]
                for m in range(8):
                    r = b_front(i, m)
                    if i + 1 < NCH:
                        a_part(i + 1, m)
                    for _ in range(plan[m]):
                        if queue:
                            c_unit(*queue.pop(0))
                    b_back(i, m, r)
                gh(i)
                if i % GC == GC - 1:
                    queue += [(i // GC, u) for u in range(16)]
                    g_next = i // GC + 1
                    if g_next >= 2:
                        for ii in range(g_next * GC, (g_next + 1) * GC):
                            if ii < NCH:
                                load_v(ii)
            while queue:
                c_unit(*queue.pop(0))
            kb.barrier()
        if debug:
            for m in range(8):
                dma("sp", D["d_acc"][m * 128:(m + 1) * 128, :], y[:, m, :], R=[("y", m)], W=[("d_acc", m)])
        with ExitStack() as ph:
            g2 = kb.sb("ln2g", [128, 2048], F32, ph)
            b2 = kb.sb("ln2b", [128, 2048], F32, ph)
            stats = kb.sb("stats2", [128, 4, 6], F32, ph)
            mv = kb.sb("mv2", [128, 2], F32, ph)
            rs = kb.sb("rs2", [128, 1], F32, ph)
            dma("sp", g2[:], D["ln2g"], W=["ln2g"])
            dma("sp", b2[:], D["ln2b"], W=["ln2b"])
            for m in range(8):
                _layer_norm(kb, y[:, m, :], y[:, m, :], g2, b2, stats, mv, rs, [("y", m)], [("y", m)], ["ln2g", "ln2b"], epsc)
                dma("sp", D["out"][m * 128:(m + 1) * 128, :], y[:, m, :], R=[("y", m)], W=[("out", m)])
            kb.wait_keys("sp", [("out", m) for m in range(8)])
            kb.barrier()


def _layer_norm(kb, z, out_ap, g_t, b_t, stats, mv, rs, zkeys, okeys, gkeys, epsc):
    op = kb.op
    for c4 in range(4):
        op("dve", lambda e, c4=c4: e.bn_stats(out=stats[:, c4, :], in_=z[:, c4 * 512:(c4 + 1) * 512]), R=zkeys, W=["stats"])
    op("dve", lambda e: e.bn_aggr(out=mv[:], in_=stats[:].rearrange("p a b -> p (a b)")), R=["stats"], W=["mv"])
    op("act", lambda e: e.activation(out=rs[:], in_=mv[:, 1:2], func=AF.Sqrt, bias=epsc[:, 0:1]), R=["mv", "epsc"], W=["rs"])
    op("dve", lambda e: e.reciprocal(out=rs[:], in_=rs[:]), R=["rs"], W=["rs"])
    op("dve", lambda e: e.tensor_scalar(out=mv[:, 1:2], in0=mv[:, 0:1], scalar1=rs[:, 0:1], scalar2=-1.0, op0=ALU.mult, op1=ALU.mult),
       R=["mv", "rs"], W=["mv"])
    op("act", lambda e: e.activation(out=z, in_=z, func=AF.Identity, scale=rs[:, 0:1], bias=mv[:, 1:2]), R=zkeys + ["mv", "rs"], W=zkeys)
    op("dve", lambda e: e.tensor_tensor(out=z, in0=z, in1=g_t[:], op=ALU.mult), R=zkeys + gkeys, W=zkeys)
    op("pool", lambda e: e.tensor_tensor(out=out_ap, in0=z, in1=b_t[:], op=ALU.add), R=zkeys + gkeys, W=okeys)


def _attention(nc, kb, D, debug, ph, g, L):
    op, dma = kb.op, kb.dma
    kT_sel, kT_win, V_sw, qT, kc_aug, vcx = L["kT_sel"], L["kT_win"], L["V_sw"], L["qT"], L["kc_aug"], L["vcx"]
    identb, ident, expand, tri_lo, tri_up = L["identb"], L["ident"], L["expand"], L["tri_lo"], L["tri_up"]
    maskc, winmask0, vis, cstb, o_nsa, gates = L["maskc"], L["winmask0"], L["vis"], L["cstb"], L["o_nsa"], L["gates"]
    NE = 4
    Eb = [kb.sb("Eb%d" % i, [128, 512], BF16, ph) for i in range(NE)]
    NS = 3
    psS = [kb.ps("psS%d" % i, [128, 512], F32, ph) for i in range(NS)]
    pc = kb.ps("pc", [128, 4, 256], F32, ph)
    psel = kb.ps("psel", [128, 512], F32, ph)
    pwin = kb.ps("pwin", [128, 512], F32, ph)
    pT = kb.ps("pT", [128, 512], F32, ph)
    zc = kb.sb("zc", [128, 12], F32, ph)
    rz = kb.sb("rz", [128, 12], F32, ph)
    coef = kb.sb("coef", [128, 12], F32, ph)
    imp = kb.sb("imp", [128, 64], F32, ph)
    impa = kb.sb("impa", [128, 64], F32, ph)
    imp2 = kb.sb("imp2", [128, 64], F32, ph)
    m8 = kb.sb("m8", [128, 16], F32, ph)
    nsel = kb.sb("nsel", [128, 64], F32, ph)
    nselT = kb.sb("nselT", [64, 128], BF16, ph)
    oacc = kb.sb("oacc", [128, 4, 64], F32, ph)
    kaug_keys_s = ["kT_sel_aug"] + [("kT_sel", m) for m in range(8)]
    kaug_keys_w = ["kT_win_aug"] + [("kT_win", m) for m in range(8)]
    st = {"s": 0, "e": 0}

    def emit_score(job):
        mm_list, nrow = job["mm"], job.get("nrow", 128)
        sb_ = st["s"] % NS
        st["s"] += 1
        eb = st["e"] % NE
        st["e"] += 1
        n = len(mm_list)
        for i, (lh, rh, rk) in enumerate(mm_list):
            op("pe", lambda e, lh=lh, rh=rh, i=i: e.matmul(psS[sb_][0:nrow, :] if len(rh.shape) == 2 else
                                                           psS[sb_][0:nrow, :].rearrange("p (a b) -> p a b", a=4),
                                                           lh, rh, start=(i == 0), stop=(i == n - 1)),
               R=rk, W=[("psS", sb_)])
        op("act", lambda e: e.activation(out=Eb[eb][0:nrow, :], in_=psS[sb_][0:nrow, :], func=AF.Exp, scale=0.125),
           R=[("psS", sb_)], W=[("Eb", eb)])
        return (job, eb)

    def run_jobs(jobs):
        pending = []
        for job in jobs:
            pending.append(emit_score(job))
            if len(pending) > 2:
                j_, eb_ = pending.pop(0)
                j_["pv"](eb_)
        for j_, eb_ in pending:
            j_["pv"](eb_)

    for m in range(8):
        J = 4 * m + 3
        qrhs = qT[0:69, m, :, :].rearrange("p a b -> p (a b)")
        qk = [("qT", m), "qT_aug"]
        vkeys = ["V_ones"] + [("V_sw", i) for i in range(8)]
        nch = 1 if 8 * J + 6 < 128 else 2
        jobs = []
        for ch in range(nch):
            nr = 128 if ch == 0 else 127

            def pv_c(eb, ch=ch, nr=nr):
                for hh in range(4):
                    op("pe", lambda e, hh=hh: e.matmul(pc[:, hh, 0:129], Eb[eb][0:nr, hh * 128:(hh + 1) * 128], vcx[0:nr, ch, :],
                                                       start=(ch == 0 and hh % 2 == 0), stop=(ch == nch - 1), skip_group_check=True),
                       R=[("Eb", eb), "vcx0", "vcx1", "vcx_c", "vcx_pad"], W=["pc"])
            jobs.append(dict(mm=[
                (kc_aug[0:69, ch * 128:ch * 128 + nr], qrhs, qk + ["kc_aug", "kc_aug_aug", "kc_pad"]),
                (identb[0:nr, 0:nr], bc(maskc[0:nr, m, ch, :], 4), ["identb", "maskc"]),
            ], nrow=nr, pv=pv_c))
        run_jobs(jobs)
        op("dve", lambda e: e.tensor_scalar(out=zc[:, 0:4], in0=pc[:, :, 64], scalar1=1e-30, scalar2=None, op0=ALU.max), R=["pc"], W=["zc0"])
        op("dve", lambda e: e.reciprocal(out=rz[:, 0:4], in_=zc[:, 0:4]), R=["zc0"], W=["rz0"])
        op("dve", lambda e: e.tensor_scalar(out=imp[:], in0=pc[:, 0, 65:129], scalar1=rz[:, 0:1], scalar2=None, op0=ALU.mult),
           R=["pc", "rz0"], W=["imp"])
        for hh in range(1, 4):
            op("dve", lambda e, hh=hh: e.scalar_tensor_tensor(out=imp[:], in0=pc[:, hh, 65:129], scalar=rz[:, hh:hh + 1], in1=imp[:],
                                                              op0=ALU.mult, op1=ALU.add), R=["pc", "rz0", "imp"], W=["imp"])
        op("dve", lambda e: e.tensor_tensor(out=impa[:], in0=imp[:], in1=vis[:, m, :], op=ALU.mult), R=["imp", "vis"], W=["impa"])
        op("dve", lambda e: e.tensor_tensor(out=impa[:], in0=impa[:], in1=cstb[:, m, :], op=ALU.add), R=["impa", "cstb"], W=["impa"])
        op("dve", lambda e: e.max(out=m8[:, 0:8], in_=impa[:]), R=["impa"], W=["m8a"])
        op("dve", lambda e: e.match_replace(out=imp2[:], in_to_replace=m8[:, 0:8], in_values=impa[:], imm_value=-1e30),
           R=["impa", "m8a"], W=["imp2"])
        op("dve", lambda e: e.max(out=m8[:, 8:16], in_=imp2[:]), R=["imp2"], W=["m8b"])
        op("dve", lambda e: e.tensor_scalar(out=nsel[:], in0=impa[:], scalar1=m8[:, 15:16], scalar2=None, op0=ALU.is_ge),
           R=["impa", "m8b"], W=["nsel"])
        op("dve", lambda e: e.tensor_scalar(out=nsel[:], in0=nsel[:], scalar1=-NEG, scalar2=NEG, op0=ALU.mult, op1=ALU.add),
           R=["nsel"], W=["nsel"])
        if debug:
            dma("sp", D["d_imp"][:, m, :], impa[:], R=["impa"], W=[("d_imp", m, g)])
            dma("sp", D["d_negsel"][:, m, :], nsel[:], R=["nsel"], W=[("d_negsel", m, g)])
        op("dve", lambda e: e.tensor_tensor(out=coef[:, 0:4], in0=rz[:, 0:4], in1=gates[:, m, 12 * g:12 * g + 12].rearrange("p (b a) -> p a b", a=3)[:, 0, :], op=ALU.mult),
           R=["rz0", ("gates", m, g)], W=["coef0"])
        for hh in range(4):
            op("dve", lambda e, hh=hh: e.tensor_scalar(out=oacc[:, hh, :], in0=pc[:, hh, 0:64], scalar1=coef[:, hh:hh + 1], scalar2=None, op0=ALU.mult),
               R=["pc", "coef0"], W=[("oacc", hh)])
        jobs = []
        wlist = [Dd for Dd in (4, 3, 2, 1, 0) if J - Dd >= 0]
        for Dd in wlist:
            kt = J - Dd
            mm = [(kT_win[0:69, kt * 128:(kt + 1) * 128], qrhs, qk + kaug_keys_w)]
            if m == 0:
                mm.append((identb[:], bc(winmask0[:, Dd, :], 4), ["identb", "winmask0"]))
            elif Dd == 0:
                mm.append((identb[:], bc(tri_lo[:], 4), ["identb", "tri_lo"]))
            elif Dd == 4:
                mm.append((identb[:], bc(tri_up[:], 4), ["identb", "tri_up"]))

            def pv_w(eb, kt=kt, Dd=Dd):
                for hh in range(4):
                    op("pe", lambda e, hh=hh: e.matmul(pwin[:, hh * 65:hh * 65 + 65], Eb[eb][:, hh * 128:(hh + 1) * 128], V_sw[:, kt, 1, :],
                                                       start=(Dd == wlist[0] and hh == 0), stop=(Dd == 0), skip_group_check=True),
                       R=[("Eb", eb)] + vkeys, W=["pwin"])
            jobs.append(dict(mm=mm, pv=pv_w))
        run_jobs(jobs)
        op("pe", lambda e: e.transpose(pT[0:64, 0:128], nsel[:], ident[:]), R=["nsel", "ident"], W=["pT"])
        op("dve", lambda e: e.tensor_copy(out=nselT[:], in_=pT[0:64, 0:128]), R=["pT"], W=["nselT"])
        jobs = []
        for kt in range(J + 1):
            mm = [(kT_sel[0:69, kt * 128:(kt + 1) * 128], qrhs, qk + kaug_keys_s),
                  (expand[:, kt, :], bc(nselT[:], 4), ["expand", "nselT"])]
            if kt == J:
                mm.append((identb[:], bc(tri_lo[:], 4), ["identb", "tri_lo"]))

            def pv_s(eb, kt=kt):
                for hh in range(4):
                    op("pe", lambda e, hh=hh: e.matmul(psel[:, hh * 65:hh * 65 + 65], Eb[eb][:, hh * 128:(hh + 1) * 128], V_sw[:, kt, 0, :],
                                                       start=(kt == 0 and hh == 0), stop=(kt == J), skip_group_check=True),
                       R=[("Eb", eb)] + vkeys, W=["psel"])
            jobs.append(dict(mm=mm, pv=pv_s))
        run_jobs(jobs)
        pselv = psel[:, 0:260].rearrange("p (a b) -> p a b", a=4)
        pwinv = pwin[:, 0:260].rearrange("p (a b) -> p a b", a=4)
        op("dve", lambda e: e.tensor_scalar(out=zc[:, 4:8], in0=pselv[:, :, 64], scalar1=1e-30, scalar2=None, op0=ALU.max), R=["psel"], W=["zc1"])
        op("dve", lambda e: e.tensor_scalar(out=zc[:, 8:12], in0=pwinv[:, :, 64], scalar1=1e-30, scalar2=None, op0=ALU.max), R=["pwin"], W=["zc2"])
        op("dve", lambda e: e.reciprocal(out=rz[:, 4:12], in_=zc[:, 4:12]), R=["zc1", "zc2"], W=["rz1"])
        op("dve", lambda e: e.tensor_tensor(out=coef[:, 4:12].rearrange("p (a b) -> p a b", a=2),
                                            in0=rz[:, 4:12].rearrange("p (a b) -> p a b", a=2),
                                            in1=gates[:, m, 12 * g:12 * g + 12].rearrange("p (b a) -> p a b", a=3)[:, 1:3, :], op=ALU.mult),
           R=["rz1", ("gates", m, g)], W=["coef"])
        for hh in range(4):
            op("dve", lambda e, hh=hh: e.scalar_tensor_tensor(out=oacc[:, hh, :], in0=pselv[:, hh, 0:64], scalar=coef[:, 4 + hh:5 + hh], in1=oacc[:, hh, :],
                                                              op0=ALU.mult, op1=ALU.add), R=["psel", "coef", ("oacc", hh)], W=[("oacc", hh)])
            op("dve", lambda e, hh=hh: e.scalar_tensor_tensor(out=o_nsa[:, m, g * 256 + hh * 64:g * 256 + hh * 64 + 64], in0=pwinv[:, hh, 0:64],
                                                              scalar=coef[:, 8 + hh:9 + hh], in1=oacc[:, hh, :], op0=ALU.mult, op1=ALU.add),
               R=["pwin", "coef", ("oacc", hh)], W=[("o_nsa", m)])


_CACHE = {}


def kernel(**inputs):
    maps = _prep_inputs(inputs)
    if "nc" not in _CACHE:
        _CACHE["nc"] = build(False)
    nc = _CACHE["nc"]
    res = run_bass_kernel_spmd(nc, maps, core_ids=list(range(8)))
    out = np.zeros((2, 4096, 2048), np.float32)
    for c in range(8):
        b, s = c // 4, c % 4
        r = res.results[c]["out"]
        for m in range(8):
            j = 4 * m + s
            out[b, j * 128:(j + 1) * 128, :] = r[m * 128:(m + 1) * 128, :]
    return out
```

```python
from contextlib import ExitStack
import numpy as np
import ml_dtypes
import concourse.bass as bass
import concourse.mybir as mybir
from concourse.bass_utils import run_bass_kernel_spmd

F32 = mybir.dt.float32
BF16 = mybir.dt.bfloat16
U32 = mybir.dt.uint32
AF = mybir.ActivationFunctionType
ALU = mybir.AluOpType
AX = mybir.AxisListType

NEG = -30000.0
LN_EPS = 1e-5
DN_ALPHA = 2.0 ** 0.25


class KB:
    NDMA = 12
    SEM_ROLL = 30000

    def __init__(self, nc):
        self.nc = nc
        self.st = ExitStack()
        self.E = dict(pe=nc.tensor, act=nc.scalar, dve=nc.vector, pool=nc.gpsimd, sp=nc.sync)
        self.esem = {}
        self.ecnt = {}
        self.nsem = 0
        for e in self.E:
            self._new_esem(e)
        self.dsem = {}
        self.dcnt = {}
        for q in ("sp", "pool"):
            self.dsem[q] = [self._sem() for _ in range(self.NDMA)]
            self.dcnt[q] = 0
        self.waited = {e: {} for e in self.E}
        self.lastw = {}
        self.readers = {}
        self.n_wait = 0
        self.n_ops = 0

    def _sem(self):
        self.nsem += 1
        return self.st.enter_context(self.nc.semaphore("ks%d" % self.nsem))

    def _new_esem(self, e):
        self.esem[e] = self._sem()
        self.ecnt[e] = 0

    def sb(self, name, shape, dt, st=None):
        self.nsem += 1
        name = "%s_%d" % (name, self.nsem)
        return (st or self.st).enter_context(self.nc.sbuf_tensor("s_" + name, list(shape), dt))

    def ps(self, name, shape, dt=F32, st=None):
        self.nsem += 1
        name = "%s_%d" % (name, self.nsem)
        return (st or self.st).enter_context(self.nc.psum_tensor("p_" + name, list(shape), dt))

    def _wait(self, eng, tok):
        sem, val, src = tok
        w = self.waited[eng]
        sid = id(sem)
        if w.get(sid, 0) >= val:
            return
        w[sid] = val
        self.E[eng].wait_ge(sem, val)
        self.n_wait += 1

    def _deps(self, eng, R, W, is_dma):
        for r in R:
            t = self.lastw.get(r)
            if t is not None:
                if is_dma or not (t[2] == eng and eng == "pe"):
                    self._wait(eng, t)
        for w_ in W:
            t = self.lastw.get(w_)
            if t is not None:
                if is_dma or not (t[2] == eng and eng == "pe"):
                    self._wait(eng, t)
            for t in self.readers.get(w_, ()):
                if is_dma or t[2] != eng:
                    self._wait(eng, t)

    def _record(self, tok, R, W):
        for r in R:
            lst = self.readers.setdefault(r, [])
            lst[:] = [t for t in lst if not (t[2] == tok[2] and t[0] is tok[0])]
            lst.append(tok)
        for w_ in W:
            self.lastw[w_] = tok
            self.readers[w_] = []

    def op(self, eng, fn, R=(), W=()):
        self._deps(eng, R, W, False)
        if self.ecnt[eng] >= self.SEM_ROLL:
            self._new_esem(eng)
        inst = fn(self.E[eng])
        self.ecnt[eng] += 1
        inst.then_inc(self.esem[eng], 1)
        tok = (self.esem[eng], self.ecnt[eng], eng)
        self._record(tok, R, W)
        self.n_ops += 1
        return tok

    def dma(self, q, out, in_, R=(), W=(), **kw):
        if out.dtype != in_.dtype:
            q = "pool"
        self._deps(q, R, W, True)
        k = self.dcnt[q]
        slot = k % self.NDMA
        sem = self.dsem[q][slot]
        val = 16 * (k // self.NDMA + 1)
        if val > 16:
            self._wait(q, (sem, val - 16, "dma_" + q))
        self.E[q].dma_start(out=out, in_=in_, **kw).then_inc(sem, 16)
        self.dcnt[q] += 1
        tok = (sem, val, "dma_" + q)
        self._record(tok, R, W)
        return tok

    def wait_keys(self, eng, keys):
        for k_ in keys:
            t = self.lastw.get(k_)
            if t is not None:
                self._wait(eng, t)

    def barrier(self):
        toks = []
        for e in self.E:
            if self.ecnt[e] > 0:
                toks.append((self.esem[e], self.ecnt[e], e))
        for q in self.dsem:
            k = self.dcnt[q]
            for slot in range(self.NDMA):
                n = (k - slot + self.NDMA - 1) // self.NDMA
                if n > 0:
                    toks.append((self.dsem[q][slot], 16 * n, "dma_" + q))
        for e in self.E:
            for t in toks:
                if t[2] == e and e == "pe":
                    continue
                self._wait(e, t)
        self.lastw.clear()
        self.readers.clear()


def bc(ap, n, axis=1):
    shp = list(ap.shape)
    shp.insert(axis, n)
    return ap.unsqueeze(axis).to_broadcast(shp)


def _bf16_round(a):
    return np.asarray(a, np.float32).astype(ml_dtypes.bfloat16).astype(np.float32)


def _common_consts():
    c = {}
    c["ident"] = np.eye(128, dtype=np.float32)
    h = np.arange(16)
    sl = (2.0 ** (-8.0 * (h + 1) / 16)).astype(np.float64)
    s_hi = _bf16_round(sl).astype(np.float64)
    s_lo = _bf16_round(sl - s_hi).astype(np.float64)
    qaug = np.zeros((4, 5, 8, 4, 128), np.float32)
    for g in range(4):
        for hh in range(4):
            hd = 4 * g + hh
            for m in range(8):
                ref = 128 * (4 * m + 3) + 64 - 2048
                qaug[g, 0, m, hh, :] = 8 * s_hi[hd]
                qaug[g, 1, m, hh, :] = 8 * s_lo[hd]
                qaug[g, 2, m, hh, :] = 8 * s_hi[hd]
                qaug[g, 3, m, hh, :] = 8 * s_lo[hd]
                qaug[g, 4, m, hh, :] = -8 * sl[hd] * ref
    c["qaug"] = qaug.reshape(4, 5, 8 * 4 * 128)

    def aug_rows(P):
        P = np.asarray(P, np.int64)
        hi = 128 * np.floor_divide(P, 128)
        lo = P - hi
        return np.stack([hi, hi, lo, lo, np.ones_like(P)]).astype(np.float32)

    c["kaug"] = aug_rows(np.arange(4096) - 2048)
    pc = np.zeros(256, np.int64)
    pc[:255] = 16 * np.arange(255) + 31 - 2048
    c["kcaug"] = aug_rows(pc)
    ex = np.zeros((64, 32, 128), np.float32)
    for kt in range(32):
        ex[2 * kt, kt, :64] = 1
        ex[2 * kt + 1, kt, 64:] = 1
    c["expand"] = ex.reshape(64, 32 * 128)
    n = np.arange(256)[:, None]
    jb = np.arange(64)[None, :]
    ov = ((16 * n < 64 * jb + 64) & (16 * n + 32 > 64 * jb) & (n < 255)).astype(np.float32)
    ovc = np.zeros((256, 65), np.float32)
    ovc[:255, 0] = 1.0
    ovc[:, 1:] = ov
    c["ovc"] = ovc.reshape(2, 128, 65).transpose(1, 0, 2).copy()
    k = np.arange(128)[:, None]
    q = np.arange(128)[None, :]
    c["tri_lo"] = np.where(k > q, NEG, 0.0).astype(np.float32)
    c["tri_up"] = np.where(k <= q, NEG, 0.0).astype(np.float32)
    return c


def _core_consts(s):
    c = {}
    npre = 3 - s
    nn = np.arange(128)[:, None, None, None]
    m = np.arange(8)[None, :, None, None]
    ch = np.arange(2)[None, None, :, None]
    q = np.arange(128)[None, None, None, :]
    n_ = ch * 128 + nn
    valid = (n_ >= 8 * npre) & (n_ <= 254) & (16 * n_ + 31 <= 128 * (4 * m + 3) + q)
    c["maskc"] = np.where(valid, 0.0, NEG).astype(np.float32).reshape(128, 8 * 2 * 128)
    k = np.arange(128)[:, None]
    qq = np.arange(128)[None, :]
    wm = np.zeros((128, 5, 128), np.float32)
    for D in range(5):
        if s - D < 0:
            wm[:, D, :] = NEG
        elif D == 0:
            wm[:, D, :] = np.where(k > qq, NEG, 0.0)
    c["winmask0"] = wm.reshape(128, 5 * 128)
    qv = np.arange(128)[:, None, None]
    mv = np.arange(8)[None, :, None]
    jb = np.arange(64)[None, None, :]
    jt = jb - 2 * npre
    tq = (4 * mv + s) * 128 + qv
    cur = tq // 64
    real = jt >= 0
    vis = real & (64 * jt <= tq)
    forced = (jt == 0) | (jt == cur) | (jt == cur - 1)
    c["vis"] = vis.astype(np.float32).reshape(128, 8 * 64)
    c["cstb"] = np.where(vis, 1.0e4 * forced, np.where(real, -1.0, -2.0)).astype(np.float32).reshape(128, 8 * 64)
    return c


def _prep_inputs(inp):
    x = np.asarray(inp["x"], np.float32)
    w_in = np.asarray(inp["w_in"], np.float32)[0]
    com = _common_consts()
    wg = np.zeros((4, 2048, 652), np.float32)
    for g in range(4):
        cols = list(range(g * 256, g * 256 + 256))
        for br in (0, 1, 2, 4, 3, 5):
            base = 1024 + (br * 4 + g) * 64
            cols += list(range(base, base + 64))
        cols += list(range(2560 + 12 * g, 2560 + 12 * g + 12))
        wg[g] = w_in[:, cols]
    com["wg"] = wg
    com["wconv"] = np.ascontiguousarray(w_in[:, 2608:4656])
    for nm in ("k", "v"):
        com["w1" + nm] = np.asarray(inp["cmp_w1_" + nm], np.float32)[0]
        com["w2" + nm] = np.asarray(inp["cmp_w2_" + nm], np.float32)[0]
        com["posT" + nm] = np.ascontiguousarray(np.asarray(inp["cmp_pos_" + nm], np.float32)[0].T)
    com["dww"] = np.ascontiguousarray(np.asarray(inp["dw_w"], np.float32)[0].T)
    com["dwb"] = np.ascontiguousarray(np.asarray(inp["dw_b"], np.float32)[0].reshape(8, 128).T)
    com["clg"] = np.ascontiguousarray(np.asarray(inp["conv_ln_g"], np.float32)[0].reshape(8, 128).T)
    com["clb"] = np.ascontiguousarray(np.asarray(inp["conv_ln_b"], np.float32)[0].reshape(8, 128).T)
    com["wout"] = np.asarray(inp["w_out"], np.float32)[0]
    com["ln1g"] = np.broadcast_to(np.asarray(inp["ln1_g"], np.float32)[0][None, :], (128, 2048)).copy()
    com["ln1b"] = np.broadcast_to(np.asarray(inp["ln1_b"], np.float32)[0][None, :], (128, 2048)).copy()
    wq = np.asarray(inp["peer_wq"], np.float32)[0]
    com["wq_l"] = np.ascontiguousarray(wq.reshape(16, 128, 16, 128).transpose(2, 1, 0, 3)).reshape(16, 128, 2048)
    keys = np.asarray(inp["peer_keys"], np.float32)[0]
    com["keysT"] = np.ascontiguousarray(keys.transpose(3, 0, 1, 2)).reshape(128, 2048)
    U = np.asarray(inp["peer_u"], np.float32)[0]
    com["UT_l"] = np.ascontiguousarray(U.reshape(128, 128, 16, 128).transpose(0, 3, 2, 1)).reshape(128, 128, 2048)
    com["V"] = np.asarray(inp["peer_v"], np.float32)[0]
    com["ln2g"] = np.broadcast_to(np.asarray(inp["ln2_g"], np.float32)[0][None, :], (128, 2048)).copy()
    com["ln2b"] = np.broadcast_to(np.asarray(inp["ln2_b"], np.float32)[0][None, :], (128, 2048)).copy()
    maps = []
    for c in range(8):
        b, s = c // 4, c % 4
        sh = (3 - s) * 128
        d = dict(com)
        xT = np.zeros((2048, 4096), np.float32)
        xT[:, sh:] = x[b, :4096 - sh].T
        d["xTs"] = xT
        own = np.concatenate([np.arange((4 * m + s) * 128, (4 * m + s + 1) * 128) for m in range(8)])
        d["x_own"] = np.ascontiguousarray(x[b, own])
        d.update(_core_consts(s))
        maps.append(d)
    return maps


def build(debug=False, stop=None, nch=128, peer_only=False):
    nc = bass.Bass("TRN2", target_bir_lowering=False)
    D = {}

    def din(name, shape, dt=F32):
        D[name] = nc.dram_tensor(name, list(shape), dt, kind="ExternalInput").ap()
        return D[name]

    def dout(name, shape, dt=F32):
        D[name] = nc.dram_tensor(name, list(shape), dt, kind="ExternalOutput").ap()
        return D[name]

    if not peer_only:
        din("xTs", [2048, 4096]); din("x_own", [1024, 2048])
        din("qaug", [4, 5, 4096]); din("kaug", [5, 4096]); din("kcaug", [5, 256])
        din("expand", [64, 4096]); din("ovc", [128, 2, 65]); din("tri_lo", [128, 128]); din("tri_up", [128, 128])
        din("maskc", [128, 2048]); din("winmask0", [128, 640]); din("vis", [128, 512]); din("cstb", [128, 512])
        din("wg", [4, 2048, 652]); din("wconv", [2048, 2048])
        for nm in ("k", "v"):
            din("w1" + nm, [2048, 256]); din("w2" + nm, [256, 64]); din("posT" + nm, [64, 32])
        din("dww", [1024, 31]); din("dwb", [128, 8]); din("clg", [128, 8]); din("clb", [128, 8])
        din("wout", [2048, 2048]); din("ln1g", [128, 2048]); din("ln1b", [128, 2048])
    else:
        din("y_in", [1024, 2048])
    din("ident", [128, 128])
    din("wq_l", [16, 128, 2048]); din("keysT", [128, 2048]); din("UT_l", [nch, 128, 2048]); din("V", [nch * 128, 2048])
    din("ln2g", [128, 2048]); din("ln2b", [128, 2048])
    dout("out", [1024, 2048])
    if debug:
        dout("d_acc", [1024, 2048])
    if debug and not peer_only:
        dout("d_onsa", [128, 8, 1024]); dout("d_oconv", [128, 8, 1024])
        dout("d_kc", [64, 4, 256]); dout("d_vc", [128, 4, 2, 64]); dout("d_q", [64, 4096])
        dout("d_ksel", [64, 4096]); dout("d_vsw", [128, 32 * 2 * 65]); dout("d_gates", [128, 8, 48])
        dout("d_imp", [128, 8, 64]); dout("d_negsel", [128, 8, 64])
    if debug:
        dout("d_S1", [128, 8192]); dout("d_S2", [128, 8192]); dout("d_c3", [128, 64])

    kb = KB(nc)
    with kb.st:
        if peer_only:
            _peer_only(nc, kb, D, debug, stop, nch)
        else:
            _program(nc, kb, D, debug, stop, nch)
        kb.barrier()
    return nc


def _program(nc, kb, D, debug, stop=None, nch=128):
    op, dma = kb.op, kb.dma
    ident = kb.sb("ident", [128, 128], F32)
    identb = kb.sb("identb", [128, 128], BF16)
    epsc = kb.sb("epsc", [128, 1], F32)
    y = kb.sb("y", [128, 8, 2048], F32)
    pA = ExitStack()
    kb.st.enter_context(pA)
    ones_f = kb.sb("ones_f", [128, 128], F32, pA)
    conv_scr = nc.dram_tensor("conv_scr", [128, 8192], BF16, kind="Internal").ap()
    dma("sp", ident[:], D["ident"], W=["ident"])
    dma("pool", identb[:], D["ident"], W=["identb"])
    op("dve", lambda e: e.memset(ones_f[:], 1.0), W=["ones_f"])
    op("dve", lambda e: e.memset(epsc[:], LN_EPS), W=["epsc"])

    xT_d = D["xTs"].rearrange("(dc p) t -> p dc t", p=128)

    with ExitStack() as ph:
        xoh = kb.sb("xoh", [128, 16, 8, 160], BF16, ph)
        o_convT = kb.sb("o_convT", [128, 8, 1024], BF16, ph)
        dww = kb.sb("dww", [128, 8, 31], F32, ph)
        dwb = kb.sb("dwb", [128, 8], F32, ph)
        clg = kb.sb("clg", [128, 8], F32, ph)
        clb = kb.sb("clb", [128, 8], F32, ph)
        call = kb.sb("call", [128, 8, 1024], F32, ph)
        wca = [kb.sb("wca%d" % i, [128, 16, 128], BF16, ph) for i in range(2)]
        wcg = [kb.sb("wcg%d" % i, [128, 16, 128], BF16, ph) for i in range(2)]
        sg = [kb.sb("sg%d" % i, [128, 480], F32, ph) for i in range(2)]
        u = [kb.sb("u%d" % i, [128, 8, 160], BF16, ph) for i in range(2)]
        dgw = [kb.sb("dgw%d" % i, [128, 31, 128], BF16, ph) for i in range(2)]
        psC = [kb.ps("psC%d" % i, [128, 512], F32, ph) for i in range(2)]
        psA = [kb.ps("psA%d" % i, [128, 512], F32, ph) for i in range(2)]
        psG = [kb.ps("psG%d" % i, [128, 512], F32, ph) for i in range(2)]
        psL = [kb.ps("psL%d" % i, [128, 512], F32, ph) for i in range(2)]

        for m in range(8):
            t0 = m * 512 + 352
            dma("pool", xoh[:, :, m, :], xT_d[:, :, t0:t0 + 160], W=[("xoh", m)])
        dma("sp", dww[:], D["dww"].rearrange("(ct p) w -> p ct w", p=128), W=["dww"])
        for nm, t in (("dwb", dwb), ("clg", clg), ("clb", clb)):
            dma("sp", t[:], D[nm], W=[nm])
        wc_d = D["wconv"].rearrange("(dc p) c -> p dc c", p=128)
        xoh_keys = [("xoh", m) for m in range(8)]
        chunks = [(0, 3), (3, 3), (6, 2)]
        k = 0
        for ct in range(8):
            wb = ct % 2
            dma("pool", wca[wb][:], wc_d[:, :, ct * 128:(ct + 1) * 128], W=[("wca", wb)])
            dma("pool", wcg[wb][:], wc_d[:, :, 1024 + ct * 128:1024 + (ct + 1) * 128], W=[("wcg", wb)])
            ub = ct % 2
            for (m0, nm_) in chunks:
                pb = k % 2
                k += 1
                n = nm_ * 160
                for dc in range(16):
                    op("pe", lambda e, dc=dc: e.matmul(psA[pb][:, 0:n].rearrange("p (a b) -> p a b", a=nm_), wca[wb][:, dc, :],
                                                       xoh[:, dc, m0:m0 + nm_, :], start=(dc == 0), stop=(dc == 15)),
                       R=[("wca", wb)] + xoh_keys, W=[("psA", pb)])
                for dc in range(16):
                    op("pe", lambda e, dc=dc: e.matmul(psG[pb][:, 0:n].rearrange("p (a b) -> p a b", a=nm_), wcg[wb][:, dc, :],
                                                       xoh[:, dc, m0:m0 + nm_, :], start=(dc == 0), stop=(dc == 15)),
                       R=[("wcg", wb)] + xoh_keys, W=[("psG", pb)])
                op("act", lambda e: e.activation(out=sg[pb][:, 0:n], in_=psG[pb][:, 0:n], func=AF.Sigmoid),
                   R=[("psG", pb)], W=[("sg", pb)])
                op("dve", lambda e: e.tensor_tensor(out=u[ub][:, m0:m0 + nm_, :].rearrange("p a b -> p (a b)"),
                                                    in0=psA[pb][:, 0:n], in1=sg[pb][:, 0:n], op=ALU.mult),
                   R=[("psA", pb), ("sg", pb)], W=[("u", ub)])
            op("dve", lambda e: e.tensor_tensor(out=dgw[ub][:], in0=bc(identb[:], 31, axis=1), in1=bc(dww[:, ct, :], 128, axis=2), op=ALU.mult),
               R=["identb", "dww"], W=[("dgw", ub)])
            for hf in range(2):
                for w in range(31):
                    op("pe", lambda e, w=w: e.matmul(psC[hf][:].rearrange("p (a b) -> p a b", a=4), dgw[ub][:, w, :],
                                                     u[ub][:, 4 * hf:4 * hf + 4, 2 + w:130 + w], start=(w == 0), stop=(w == 30)),
                       R=[("dgw", ub), ("u", ub)], W=[("psC", hf)])
                op("act", lambda e: e.activation(out=call[:, ct, hf * 512:(hf + 1) * 512], in_=psC[hf][:], func=AF.Identity, bias=dwb[:, ct:ct + 1]),
                   R=[("psC", hf), "dwb"], W=[("call", ct)])
        with ExitStack() as ph2:
            csq = kb.sb("csq", [128, 512], F32, ph2)
            mean = kb.sb("cmean", [128, 512], F32, ph2)
            rstd = kb.sb("crstd", [128, 512], F32, ph2)
            tmp = kb.sb("ctmp", [128, 512], F32, ph2)
            for hf in range(2):
                tsl = slice(hf * 512, (hf + 1) * 512)
                for ct in range(8):
                    op("pe", lambda e, ct=ct: e.matmul(psL[0][:], ones_f[:], call[:, ct, tsl], start=(ct == 0), stop=(ct == 7)),
                       R=["ones_f", ("call", ct)], W=["psL0"])
                for ct in range(8):
                    op("act", lambda e, ct=ct: e.activation(out=csq[:], in_=call[:, ct, tsl], func=AF.Square),
                       R=[("call", ct)], W=["csq"])
                    op("pe", lambda e, ct=ct: e.matmul(psL[1][:], ones_f[:], csq[:], start=(ct == 0), stop=(ct == 7)),
                       R=["ones_f", "csq"], W=["psL1"])
                op("dve", lambda e: e.tensor_scalar(out=mean[:], in0=psL[0][:], scalar1=1.0 / 1024, scalar2=None, op0=ALU.mult),
                   R=["psL0"], W=["cmean"])
                op("dve", lambda e: e.tensor_tensor(out=tmp[:], in0=mean[:], in1=mean[:], op=ALU.mult), R=["cmean"], W=["ctmp"])
                op("dve", lambda e: e.scalar_tensor_tensor(out=rstd[:], in0=psL[1][:], scalar=1.0 / 1024, in1=tmp[:],
                                                           op0=ALU.mult, op1=ALU.subtract),
                   R=["psL1", "ctmp"], W=["crstd"])
                op("act", lambda e: e.activation(out=rstd[:], in_=rstd[:], func=AF.Sqrt, bias=epsc[:, 0:1]), R=["crstd", "epsc"], W=["crstd"])
                op("dve", lambda e: e.reciprocal(out=rstd[:], in_=rstd[:]), R=["crstd"], W=["crstd"])
                for ct in range(8):
                    op("dve", lambda e, ct=ct: e.tensor_tensor(out=tmp[:], in0=call[:, ct, tsl], in1=mean[:], op=ALU.subtract),
                       R=[("call", ct), "cmean"], W=["ctmp"])
                    op("dve", lambda e: e.tensor_tensor(out=tmp[:], in0=tmp[:], in1=rstd[:], op=ALU.mult),
                       R=["ctmp", "crstd"], W=["ctmp"])
                    op("act", lambda e, ct=ct: e.activation(out=o_convT[:, ct, tsl], in_=tmp[:], func=AF.Silu,
                                                            bias=clb[:, ct:ct + 1], scale=clg[:, ct:ct + 1]),
                       R=["ctmp", "clg", "clb"], W=[("o_convT", ct)])
        if debug:
            dma("sp", D["d_oconv"], o_convT[:], R=[("o_convT", ct) for ct in range(8)], W=["d_oconv"])
        dma("sp", conv_scr, o_convT[:].rearrange("p a b -> p (a b)"), R=[("o_convT", ct) for ct in range(8)], W=["conv_scr"])
        kb.barrier()
        if stop == "C":
            return

    expand = kb.sb("expand", [64, 32, 128], BF16, pA)
    tri_lo = kb.sb("tri_lo", [128, 128], BF16, pA)
    tri_up = kb.sb("tri_up", [128, 128], BF16, pA)
    maskc = kb.sb("maskc", [128, 8, 2, 128], BF16, pA)
    winmask0 = kb.sb("winmask0", [128, 5, 128], BF16, pA)
    vis = kb.sb("vis", [128, 8, 64], F32, pA)
    cstb = kb.sb("cstb", [128, 8, 64], F32, pA)
    o_nsa = kb.sb("o_nsa", [128, 8, 1024], BF16, pA)
    gates = kb.sb("gates", [128, 8, 48], F32, pA)
    dma("pool", expand[:].rearrange("p a b -> p (a b)"), D["expand"], W=["expand"])
    dma("pool", tri_lo[:], D["tri_lo"], W=["tri_lo"])
    dma("pool", tri_up[:], D["tri_up"], W=["tri_up"])
    dma("pool", maskc[:].rearrange("p a b c -> p (a b c)"), D["maskc"], W=["maskc"])
    dma("pool", winmask0[:].rearrange("p a b -> p (a b)"), D["winmask0"], W=["winmask0"])
    dma("sp", vis[:].rearrange("p a b -> p (a b)"), D["vis"], W=["vis"])
    dma("sp", cstb[:].rearrange("p a b -> p (a b)"), D["cstb"], W=["cstb"])
    if debug:
        op("pool", lambda e: e.memset(o_nsa[:], 0.0), W=["o_nsa_init"])
        op("pool", lambda e: e.memset(gates[:], 0.0), W=["gates_init"])

    with ExitStack() as phg:
        kT_sel = kb.sb("kT_sel", [69, 4096], BF16, phg)
        kT_win = kb.sb("kT_win", [69, 4096], BF16, phg)
        kvc = kb.sb("kvc", [128, 4096], BF16, phg)
        stg = kb.sb("stg", [128, 4096], BF16, phg)
        V_sw = kb.sb("V_sw", [128, 32, 2, 65], BF16, phg)
        qT = kb.sb("qT", [69, 8, 4, 128], BF16, phg)
        kc_aug = kb.sb("kc_aug", [69, 256], BF16, phg)
        vcx = kb.sb("vcx", [128, 2, 129], BF16, phg)
        cbias = kb.sb("cbias", [128, 2, 2], F32, phg)
        dma("pool", kT_sel[64:69, :], D["kaug"], W=["kT_sel_aug"])
        dma("pool", kT_win[64:69, :], D["kaug"], W=["kT_win_aug"])
        dma("pool", kc_aug[64:69, :], D["kcaug"], W=["kc_aug_aug"])
        dma("pool", vcx[:, :, 64:129], D["ovc"], W=["vcx_c"])
        op("pool", lambda e: e.memset(V_sw[:, :, :, 64:65], 1.0), W=["V_ones"])
        op("pool", lambda e: e.memset(kc_aug[0:64, 255:256], 0.0), W=["kc_pad"])

        for g in range(4):
            with ExitStack() as ph:
                wgs = kb.sb("wgs", [128, 16, 652], BF16, ph)
                xt = [kb.sb("xt%d" % i, [128, 16, 512], BF16, ph) for i in range(2)]
                psP = [kb.ps("psP%d" % i, [128, 512], F32, ph) for i in range(2)]
                psQ = kb.ps("psQ", [64, 512], F32, ph)
                psV = kb.ps("psV", [128, 512], F32, ph)
                psGt = kb.ps("psGt", [128, 512], F32, ph)
                dma("pool", wgs[:], D["wg"][g].rearrange("(dc p) c -> p dc c", p=128), W=["wgs"])
                dma("pool", qT[64:69, :, :, :].rearrange("p a b c -> p (a b c)"), D["qaug"][g], W=["qT_aug"])
                pk = 0
                for m in range(8):
                    xb_ = m % 2
                    dma("pool", xt[xb_][:], xT_d[:, :, m * 512:(m + 1) * 512], W=[("xt", xb_)])
                    pairs = [(256, kvc, "kvc_lo", kvc, "kvc_hi"), (384, kT_sel, "kT_sel", stg, "stg")]
                    for (off, dlo, nlo, dhi, nhi) in pairs:
                        pb = pk % 2
                        pk += 1
                        for dc in range(16):
                            op("pe", lambda e, dc=dc: e.matmul(psP[pb][:], wgs[:, dc, off:off + 128], xt[xb_][:, dc, :],
                                                               start=(dc == 0), stop=(dc == 15)),
                               R=["wgs", ("xt", xb_)], W=[("psP", pb)])
                        op("act", lambda e: e.activation(out=dlo[0:64, m * 512:(m + 1) * 512], in_=psP[pb][0:64, :], func=AF.Identity),
                           R=[("psP", pb)], W=[(nlo, m)])
                        op("dve", lambda e: e.tensor_copy(out=dhi[64:128, m * 512:(m + 1) * 512], in_=psP[pb][64:128, :]),
                           R=[("psP", pb)], W=[(nhi, m), ("psP", pb)])
                    for dc in range(16):
                        for hh in range(4):
                            op("pe", lambda e, dc=dc, hh=hh: e.matmul(psQ[:, hh * 128:(hh + 1) * 128],
                                                                      wgs[:, dc, hh * 64:(hh + 1) * 64], xt[xb_][:, dc, 384:512],
                                                                      start=(dc == 0 and hh == 0), stop=(dc == 15),
                                                                      skip_group_check=True),
                               R=["wgs", ("xt", xb_)], W=["psQ"])
                    op("act", lambda e: e.activation(out=qT[0:64, m, :, :].rearrange("p a b -> p (a b)"), in_=psQ[:], func=AF.Identity),
                       R=["psQ"], W=[("qT", m)])
                    for dc in range(16):
                        for sub in range(4):
                            op("pe", lambda e, dc=dc, sub=sub: e.matmul(psV[:, sub * 128:(sub + 1) * 128],
                                                                        xt[xb_][:, dc, sub * 128:(sub + 1) * 128], wgs[:, dc, 512:640],
                                                                        start=(dc == 0 and sub == 0), stop=(dc == 15),
                                                                        skip_group_check=True),
                               R=["wgs", ("xt", xb_)], W=["psV"])
                    op("dve", lambda e: e.tensor_copy(out=V_sw[:, 4 * m:4 * m + 4, :, 0:64],
                                                      in_=psV[:].rearrange("p (a b c) -> p a b c", a=4, b=2)),
                       R=["psV", "V_ones"], W=[("V_sw", m)])
                    for dc in range(16):
                        op("pe", lambda e, dc=dc: e.matmul(psGt[:, 0:12], xt[xb_][:, dc, 384:512], wgs[:, dc, 640:652],
                                                           start=(dc == 0), stop=(dc == 15)),
                           R=["wgs", ("xt", xb_)], W=["psGt"])
                    op("act", lambda e: e.activation(out=gates[:, m, 12 * g:12 * g + 12], in_=psGt[:, 0:12], func=AF.Sigmoid),
                       R=["psGt"], W=[("gates", m, g)])
                dma("sp", kT_win[0:64, :], stg[64:128, :], R=[("stg", m) for m in range(8)], W=[("kT_win", m) for m in range(8)])
                if debug and g == 0:
                    dma("sp", D["d_q"], qT[0:64].rearrange("p a b c -> p (a b c)"), R=[("qT", m) for m in range(8)], W=["d_q"])
                    dma("sp", D["d_ksel"], kT_sel[0:64, :], R=[("kT_sel", m) for m in range(8)], W=["d_ksel"])
                    dma("sp", D["d_vsw"], V_sw[:].rearrange("p a b c -> p (a b c)"), R=[("V_sw", m) for m in range(8)] + ["V_ones"], W=["d_vsw"])
                kb.barrier()
                if stop == "G1":
                    return

            with ExitStack() as ph:
                w1kv = kb.sb("w1kv", [128, 32, 256], BF16, ph)
                w2 = [kb.sb("w2_%d" % i, [128, 2, 64], BF16, ph) for i in range(2)]
                posT = kb.sb("posTkv", [128, 32], BF16, ph)
                hid = [kb.sb("hid%d" % i, [128, 2, 256], BF16, ph) for i in range(2)]
                psH = [kb.ps("psH%d" % i, [128, 512], F32, ph) for i in range(2)]
                psB = kb.ps("psB", [128, 512], F32, ph)
                psO = kb.ps("psO", [128, 512], F32, ph)
                for wi, nm in enumerate(("k", "v")):
                    dma("pool", w1kv[wi * 64:wi * 64 + 64], D["w1" + nm].rearrange("(l d) c -> d l c", d=64), W=[("w1", wi)])
                    dma("pool", w2[wi][:], D["w2" + nm].rearrange("(cc p) d -> p cc d", p=128), W=[("w2", wi)])
                    dma("pool", posT[wi * 64:wi * 64 + 64, :], D["posT" + nm], W=[("posT", wi)])
                for wi in range(2):
                    p0 = wi * 64
                    skeys = [("kvc_lo" if wi == 0 else "kvc_hi", m) for m in range(8)]
                    for cc in range(2):
                        for l in range(32):
                            op("pe", lambda e, l=l, cc=cc: e.matmul(psB[:, wi * 2 + cc:wi * 2 + cc + 1], w1kv[p0:p0 + 64, l, cc * 128:(cc + 1) * 128],
                                                                    posT[p0:p0 + 64, l:l + 1], start=(l == 0 and cc == 0 and wi == 0), stop=(l == 31),
                                                                    skip_group_check=True),
                               R=[("w1", wi), ("posT", wi)], W=["psB"])
                    op("dve", lambda e: e.tensor_copy(out=cbias[:, wi, :], in_=psB[:, wi * 2:wi * 2 + 2]), R=["psB"], W=[("cbias", wi)])
                    for cc in range(2):
                        for l in range(32):
                            op("pe", lambda e, l=l, cc=cc: e.matmul(psH[cc][:, 0:255], w1kv[p0:p0 + 64, l, cc * 128:(cc + 1) * 128],
                                                                    kvc[p0:p0 + 64, l:l + 16 * 254 + 1:16], start=(l == 0), stop=(l == 31)),
                               R=[("w1", wi)] + skeys, W=[("psH", cc)])
                        op("act", lambda e, cc=cc: e.activation(out=hid[wi][:, cc, 0:255], in_=psH[cc][:, 0:255], func=AF.Gelu_apprx_tanh,
                                                                bias=cbias[:, wi, cc:cc + 1]),
                           R=[("psH", cc), ("cbias", wi)], W=[("hid", wi, cc)])
                for cc in range(2):
                    op("pe", lambda e, cc=cc: e.matmul(psO[0:64, 0:255], w2[0][:, cc, :], hid[0][:, cc, 0:255], start=(cc == 0), stop=(cc == 1)),
                       R=[("w2", 0), ("hid", 0, cc)], W=["psO"])
                op("dve", lambda e: e.tensor_copy(out=kc_aug[0:64, 0:255], in_=psO[0:64, 0:255]), R=["psO", "kc_pad"], W=["kc_aug"])
                for ch in range(2):
                    ncol = 128 if ch == 0 else 127
                    for cc in range(2):
                        op("pe", lambda e, cc=cc: e.matmul(psO[0:ncol, 256 + ch * 64:256 + ch * 64 + 64], hid[1][:, cc, ch * 128:ch * 128 + ncol],
                                                           w2[1][:, cc, :], start=(cc == 0), stop=(cc == 1), skip_group_check=True),
                           R=[("w2", 1), ("hid", 1, cc)], W=["psO"])
                op("pool", lambda e: e.memset(vcx[:, 1, 0:64], 0.0), W=["vcx_pad"])
                op("dve", lambda e: e.tensor_copy(out=vcx[:, 0, 0:64], in_=psO[:, 256:320]), R=["psO", "vcx_c"], W=["vcx0"])
                op("dve", lambda e: e.tensor_copy(out=vcx[0:127, 1, 0:64], in_=psO[0:127, 320:384]), R=["psO", "vcx_pad", "vcx_c"], W=["vcx1"])
                if debug:
                    dma("sp", D["d_kc"][:, g, :], kc_aug[0:64, :], R=["kc_aug", "kc_pad"], W=[("d_kc", g)])
                    dma("sp", D["d_vc"][:, g, :, :], vcx[:, :, 0:64], R=["vcx0", "vcx1"], W=[("d_vc", g)])
                kb.barrier()
                if stop == "G2":
                    return

            with ExitStack() as ph:
                _attention(nc, kb, D, debug, ph, g, locals())
                kb.barrier()
                if stop == "G3":
                    dma("sp", D["d_onsa"], o_nsa[:], R=[], W=["d_onsa"])
                    dma("sp", D["d_gates"], gates[:], R=[], W=["d_gates"])
                    return
        if debug:
            dma("sp", D["d_onsa"], o_nsa[:], R=[], W=["d_onsa"])
            dma("sp", D["d_gates"], gates[:], R=[], W=["d_gates"])
            kb.barrier()

    with ExitStack() as ph:
        o_nsaT = kb.sb("o_nsaT", [128, 8, 1024], BF16, ph)
        o_convT = kb.sb("o_convT2", [128, 8, 1024], BF16, ph)
        dma("sp", o_convT[:].rearrange("p a b -> p (a b)"), conv_scr, W=["o_convT2"])
        woutc = [kb.sb("woutc%d" % i, [128, 16, 512], BF16, ph) for i in range(2)]
        g1 = kb.sb("ln1g", [128, 2048], F32, ph)
        b1 = kb.sb("ln1b", [128, 2048], F32, ph)
        xo = [kb.sb("xo%d" % i, [128, 512], F32, ph) for i in range(2)]
        stats = [kb.sb("stats%d" % i, [128, 4, 6], F32, ph) for i in range(2)]
        mv = [kb.sb("mv%d" % i, [128, 2], F32, ph) for i in range(2)]
        rs = [kb.sb("rs%d" % i, [128, 1], F32, ph) for i in range(2)]
        psT = [kb.ps("psT%d" % i, [128, 512], BF16, ph) for i in range(2)]
        psM = [kb.ps("psM%d" % i, [128, 512], F32, ph) for i in range(2)]
        wo_d = D["wout"].rearrange("(fc p) c -> p fc c", p=128)
        dma("sp", g1[:], D["ln1g"], W=["ln1g"])
        dma("sp", b1[:], D["ln1b"], W=["ln1b"])
        k = 0
        for m in range(8):
            for fc in range(8):
                pb = k % 2
                k += 1
                op("pe", lambda e, fc=fc: e.transpose(psT[pb][:, 0:128], o_nsa[:, m, fc * 128:(fc + 1) * 128], identb[:]),
                   R=[("o_nsa", m), "identb"], W=[("psT", pb)])
                if fc % 2 == 0:
                    op("act", lambda e, fc=fc: e.activation(out=o_nsaT[:, fc, m * 128:(m + 1) * 128], in_=psT[pb][:, 0:128], func=AF.Identity),
                       R=[("psT", pb)], W=[("o_nsaT", m)])
                else:
                    op("dve", lambda e, fc=fc: e.tensor_copy(out=o_nsaT[:, fc, m * 128:(m + 1) * 128], in_=psT[pb][:, 0:128]),
                       R=[("psT", pb)], W=[("o_nsaT", m)])
        k = 0
        for cc in range(4):
            wb = cc % 2
            dma("pool", woutc[wb][:], wo_d[:, :, cc * 512:(cc + 1) * 512], W=[("woutc", wb)])
            for m in range(8):
                pb = k % 2
                k += 1
                dma("sp", xo[pb][:], D["x_own"][m * 128:(m + 1) * 128, cc * 512:(cc + 1) * 512], W=[("xo", pb)])
                for fc in range(16):
                    src = o_nsaT[:, fc, m * 128:(m + 1) * 128] if fc < 8 else o_convT[:, fc - 8, m * 128:(m + 1) * 128]
                    op("pe", lambda e, fc=fc, src=src: e.matmul(psM[pb][:], src, woutc[wb][:, fc, :], start=(fc == 0), stop=(fc == 15)),
                       R=[("o_nsaT", m), ("woutc", wb), "o_convT2"], W=[("psM", pb)])
                op("dve", lambda e: e.scalar_tensor_tensor(out=y[:, m, cc * 512:(cc + 1) * 512], in0=xo[pb][:], scalar=DN_ALPHA, in1=psM[pb][:],
                                                           op0=ALU.mult, op1=ALU.add),
                   R=[("xo", pb), ("psM", pb)], W=[("y", m)])
                if cc == 3:
                    _layer_norm(kb, y[:, m, :], y[:, m, :], g1, b1, stats[m % 2], mv[m % 2], rs[m % 2], [("y", m)], [("y", m)], ["ln1g", "ln1b"], epsc, tag=m % 2)
        kb.barrier()

    pA.close()
    if stop == "O":
        for m in range(8):
            dma("sp", D["out"][m * 128:(m + 1) * 128, :], y[:, m, :], R=[("y", m)], W=[("out", m)])
        return
    _peer(nc, kb, D, debug, stop, y, ident, identb, epsc, nch)


def _peer_only(nc, kb, D, debug, stop, nch):
    op, dma = kb.op, kb.dma
    ident = kb.sb("ident", [128, 128], F32)
    identb = kb.sb("identb", [128, 128], BF16)
    epsc = kb.sb("epsc", [128, 1], F32)
    y = kb.sb("y", [128, 8, 2048], F32)
    dma("sp", ident[:], D["ident"], W=["ident"])
    dma("pool", identb[:], D["ident"], W=["identb"])
    op("dve", lambda e: e.memset(epsc[:], LN_EPS), W=["epsc"])
    for m in range(8):
        dma("sp", y[:, m, :], D["y_in"][m * 128:(m + 1) * 128, :], W=[("y", m)])
    _peer(nc, kb, D, debug, stop, y, ident, identb, epsc, nch)


def _peer(nc, kb, D, debug, stop, y, ident, identb, epsc, nch):
    op, dma = kb.op, kb.dma
    with ExitStack() as pp:
        yT = kb.sb("yT", [128, 16, 1024], BF16, pp)
        S2 = kb.sb("S2", [128, 8, 8, 128], F32, pp)
        c3 = kb.sb("c3", [128, 8, 8], F32, pp)
        ec3 = kb.sb("ec3", [128, 8, 8], F32, pp)
        S1d = nc.dram_tensor("S1d", [128, 128, 64], F32, kind="Internal").ap()
        ykeys = [("y", m) for m in range(8)]
        with ExitStack() as ph:
            S1 = kb.sb("S1", [128, 8, 8, 128], F32, ph)
            s1stg = kb.sb("s1stg", [128, 16, 64], F32, ph)
            qc = [kb.sb("qc%d" % i, [128, 1024], BF16, ph) for i in range(2)]
            wqc = [kb.sb("wqc%d" % i, [128, 16, 128], BF16, ph) for i in range(2)]
            keysT = kb.sb("keysT", [128, 16, 128], BF16, ph)
            v32 = kb.sb("v32", [128, 8, 32], F32, ph)
            tmpab = [kb.sb("tmpab%d" % i, [128, 256], F32, ph) for i in range(8)]
            tmpa = [t[:, 0:128] for t in tmpab]
            tmpb = [t[:, 128:256] for t in tmpab]
            cand = [kb.sb("cand%d" % i, [128, 256], F32, ph) for i in range(8)]
            cand2 = tmpab
            t24 = kb.sb("t24", [128, 8, 8, 24], F32, ph)
            d16 = kb.sb("d16", [128, 8, 16], F32, ph)
            sm = kb.sb("sm", [128, 8, 8], F32, ph)
            psA = [kb.ps("ppA%d" % i, [128, 512], F32, ph) for i in range(2)]
            psS = [kb.ps("ppS%d" % i, [128, 1024], F32, ph) for i in range(2)]
            dma("pool", keysT[:].rearrange("p a b -> p (a b)"), D["keysT"], W=["keysT"])
            k = 0
            for m in range(8):
                for dc4 in range(4):
                    pb = k % 2
                    k += 1
                    for i in range(4):
                        dc = dc4 * 4 + i
                        op("pe", lambda e, i=i, dc=dc: e.transpose(psA[pb][:, i * 128:(i + 1) * 128], y[:, m, dc * 128:(dc + 1) * 128], ident[:]),
                           R=[("y", m), "ident"], W=[("ppA", pb)])
                    dst = yT[:, dc4 * 4:dc4 * 4 + 4, m * 128:(m + 1) * 128]
                    if k % 2 == 0:
                        op("act", lambda e: e.activation(out=dst, in_=psA[pb][:].rearrange("p (a b) -> p a b", a=4), func=AF.Identity),
                           R=[("ppA", pb)], W=[("yT", m)])
                    else:
                        op("dve", lambda e: e.tensor_copy(out=dst, in_=psA[pb][:].rearrange("p (a b) -> p a b", a=4)),
                           R=[("ppA", pb)], W=[("yT", m)])
            yTk = [("yT", m) for m in range(8)]
            for m in range(8):
                op("act", lambda e: e.activation(out=y[:, m, :], in_=y[:, m, :], func=AF.Identity, scale=DN_ALPHA),
                   R=[("y", m)], W=[("y", m)])
            k = 0
            for c16 in range(16):
                wb = c16 % 2
                dma("pool", wqc[wb][:].rearrange("p a b -> p (a b)"), D["wq_l"][c16], W=[("wqc", wb)])
                for hf in range(2):
                    pb = k % 2
                    k += 1
                    for dc in range(16):
                        op("pe", lambda e, dc=dc: e.matmul(psA[pb][:], wqc[wb][:, dc, :], yT[:, dc, hf * 512:(hf + 1) * 512],
                                                           start=(dc == 0), stop=(dc == 15)),
                           R=[("wqc", wb)] + yTk, W=[("ppA", pb)])
                    op("act", lambda e: e.activation(out=qc[wb][:, hf * 512:(hf + 1) * 512], in_=psA[pb][:], func=AF.Identity),
                       R=[("ppA", pb)], W=[("qc", wb)])
                for m in range(8):
                    op("pe", lambda e, m=m: e.matmul(psS[wb][:, m * 128:(m + 1) * 128], qc[wb][:, m * 128:(m + 1) * 128], keysT[:, c16, :],
                                                     start=(m % 4 == 0), stop=True, skip_group_check=True),
                       R=[("qc", wb), "keysT"], W=[("ppS", wb)])
                S, sn = (S1, "S1") if c16 % 2 == 0 else (S2, "S2")
                hh = c16 // 2
                op("act", lambda e: e.activation(out=S[:, :, hh, :], in_=psS[wb][:].rearrange("p (a b) -> p a b", a=8), func=AF.Identity),
                   R=[("ppS", wb)], W=[(sn, hh)])
                if c16 % 2 == 1:
                    h = hh
                    chains = []
                    for m in range(8):
                        c = m
                        ta, tb, ca, cb = tmpa[c], tmpb[c], cand[c], cand2[c]
                        vv = v32[:, m, :]
                        tt = t24[:, m, h, :]
                        steps = [
                            lambda m=m, vv=vv, c=c: op("dve", lambda e: e.max(out=vv[:, 0:8], in_=S1[:, m, h, :]), R=[("S1", h)], W=[("v32a", c)]),
                            lambda m=m, vv=vv, c=c: op("dve", lambda e: e.max(out=vv[:, 16:24], in_=S2[:, m, h, :]), R=[("S2", h)], W=[("v32b", c)]),
                            lambda m=m, vv=vv, ta=ta, c=c: op("dve", lambda e: e.match_replace(out=ta, in_to_replace=vv[:, 0:8], in_values=S1[:, m, h, :], imm_value=-1e30),
                                                              R=[("S1", h), ("v32a", c)], W=[("tmpa", c)]),
                            lambda m=m, vv=vv, tb=tb, c=c: op("dve", lambda e: e.match_replace(out=tb, in_to_replace=vv[:, 16:24], in_values=S2[:, m, h, :], imm_value=-1e30),
                                                              R=[("S2", h), ("v32b", c)], W=[("tmpb", c)]),
                            lambda vv=vv, ta=ta, c=c: op("dve", lambda e: e.max(out=vv[:, 8:16], in_=ta), R=[("tmpa", c)], W=[("v32c", c)]),
                            lambda vv=vv, tb=tb, c=c: op("dve", lambda e: e.max(out=vv[:, 24:32], in_=tb), R=[("tmpb", c)], W=[("v32d", c)]),
                            lambda vv=vv, ca=ca, c=c: op("dve", lambda e: e.tensor_tensor(out=ca[:].rearrange("p (a b) -> p a b", a=16), in0=bc(vv[:, 0:16], 16, axis=2),
                                                                                          in1=bc(vv[:, 16:32], 16, axis=1), op=ALU.add),
                                                         R=[("v32a", c), ("v32b", c), ("v32c", c), ("v32d", c)], W=[("cand", c)]),
                            lambda tt=tt, ca=ca, c=c, m=m: op("dve", lambda e: e.max(out=tt[:, 0:8], in_=ca[:]), R=[("cand", c)], W=[("t24a", m, h)]),
                            lambda tt=tt, ca=ca, cb=cb, c=c, m=m: op("dve", lambda e: e.match_replace(out=cb[:], in_to_replace=tt[:, 0:8], in_values=ca[:], imm_value=-1e30),
                                                                     R=[("cand", c), ("t24a", m, h)], W=[("cand2", c)]),
                            lambda tt=tt, cb=cb, c=c, m=m: op("dve", lambda e: e.max(out=tt[:, 8:16], in_=cb[:]), R=[("cand2", c)], W=[("t24b", m, h)]),
                            lambda tt=tt, ca=ca, cb=cb, c=c, m=m: op("dve", lambda e: e.match_replace(out=ca[:], in_to_replace=tt[:, 8:16], in_values=cb[:], imm_value=-1e30),
                                                                     R=[("cand2", c), ("t24b", m, h)], W=[("cand", c)]),
                            lambda tt=tt, ca=ca, c=c, m=m: op("dve", lambda e: e.max(out=tt[:, 16:24], in_=ca[:]), R=[("cand", c)], W=[("t24c", m, h)]),
                        ]
                        chains.append(steps)
                    for si in range(len(chains[0])):
                        for c in range(8):
                            chains[c][si]()
            S1k = [("S1", h) for h in range(8)]
            S2k = [("S2", h) for h in range(8)]
            for m in range(8):
                t24k = [(k_, m, h) for k_ in ("t24a", "t24b", "t24c") for h in range(8)]
                op("dve", lambda e: e.tensor_copy(out=sm[:, 0, :], in_=t24[:, m, :, 0]), R=t24k, W=["sm0"])
                op("dve", lambda e: e.tensor_tensor(out=d16[:], in0=t24[:, m, :, 0:16], in1=bc(sm[:, 0, :], 16, axis=2), op=ALU.subtract),
                   R=t24k + ["sm0"], W=["d16"])
                op("act", lambda e: e.activation(out=d16[:], in_=d16[:], func=AF.Exp), R=["d16"], W=["d16"])
                op("dve", lambda e: e.tensor_reduce(out=sm[:, 1, :], in_=d16[:], axis=AX.X, op=ALU.add), R=["d16"], W=["sm1"])
                op("act", lambda e: e.activation(out=sm[:, 2, :], in_=sm[:, 1, :], func=AF.Ln), R=["sm1"], W=["sm2"])
                op("dve", lambda e: e.tensor_tensor(out=sm[:, 3, :], in0=sm[:, 0, :], in1=sm[:, 2, :], op=ALU.add), R=["sm0", "sm2"], W=["sm3"])
                op("dve", lambda e: e.tensor_tensor(out=sm[:, 4, :], in0=t24[:, m, :, 15], in1=t24[:, m, :, 16], op=ALU.add), R=t24k, W=["sm4"])
                op("dve", lambda e: e.scalar_tensor_tensor(out=c3[:, m, :], in0=sm[:, 4, :], scalar=0.5, in1=sm[:, 3, :], op0=ALU.mult, op1=ALU.subtract),
                   R=["sm4", "sm3"], W=[("c3", m)])
                op("dve", lambda e: e.tensor_scalar(out=sm[:, 5, :], in0=sm[:, 4, :], scalar1=0.5, scalar2=None, op0=ALU.mult), R=["sm4"], W=["sm5"])
                op("dve", lambda e: e.tensor_tensor(out=S2[:, m, :, :], in0=S2[:, m, :, :], in1=bc(sm[:, 5, :], 128, axis=2), op=ALU.subtract),
                   R=S2k + ["sm5"], W=[("S2f", m)])
                op("act", lambda e: e.activation(out=S2[:, m, :, :].rearrange("p a b -> p (a b)"), in_=S2[:, m, :, :].rearrange("p a b -> p (a b)"), func=AF.Exp),
                   R=[("S2f", m)], W=[("S2f", m)])
            op("act", lambda e: e.activation(out=ec3[:], in_=c3[:], func=AF.Exp), R=[("c3", m) for m in range(8)], W=["ec3"])
            if debug:
                dma("sp", D["d_S1"], S1[:].rearrange("p a b c -> p (a b c)"), R=S1k, W=["d_S1"])
            for m in range(8):
                op("act", lambda e: e.activation(out=S1[:, m, :, :].rearrange("p a b -> p (a b)"), in_=S1[:, m, :, :].rearrange("p a b -> p (a b)"), func=AF.Exp),
                   R=S1k + ["d_S1"], W=S1k)
            S1v = S1[:].rearrange("p m h i -> p i (m h)")
            for ib in range(8):
                op("act", lambda e: e.activation(out=s1stg[:], in_=S1v[:, ib * 16:(ib + 1) * 16, :], func=AF.Identity), R=S1k, W=["s1stg"])
                dma("sp", S1d[:, ib * 16:(ib + 1) * 16, :], s1stg[:], R=["s1stg"], W=[("S1d", ib)])
            if debug:
                dma("sp", D["d_S2"], S2[:].rearrange("p a b c -> p (a b c)"), R=[("S2f", m) for m in range(8)], W=["d_S2"])
                dma("sp", D["d_c3"], c3[:].rearrange("p a b -> p (a b)"), R=[("c3", m) for m in range(8)], W=["d_c3"])
            kb.barrier()
        if stop == "P4":
            return
        NCH = nch
        GC = 2
        with ExitStack() as ph:
            UT = [kb.sb("UT%d" % i, [128, 16, 128], BF16, ph) for i in range(2)]
            Vg = [kb.sb("Vg%d" % i, [128, GC, 2048], BF16, ph) for i in range(2)]
            GH = [kb.sb("GH%d" % i, [128, GC, 1024], BF16, ph) for i in range(2)]
            NRB = 3
            POOL_TILES = (2, 6)
            EtB = [kb.sb("EtB%d" % i, [128, 8, 128], F32, ph) for i in range(NRB)]
            NRM = 3
            mkB = [kb.sb("mkB%d" % i, [128, 8, 128], BF16, ph) for i in range(NRM)]
            s1c = [kb.sb("s1c%d" % i, [128, 64], F32, ph) for i in range(3)]
            Dg = kb.sb("Dg", [128, 8, 8, 128], BF16, ph)
            for m in range(8):
                for h in range(8):
                    op("dve", lambda e: e.tensor_scalar(out=Dg[:, m, h, :], in0=identb[:], scalar1=ec3[:, m, h:h + 1], scalar2=None, op0=ALU.mult),
                       R=[], W=[("Dg", m)])
            psH = kb.ps("psH", [128, 1024], F32, ph)
            psG = kb.ps("psG", [128, 1024], F32, ph)
            psY = [kb.ps("psY%d" % i, [128, 1024], F32, ph) for i in range(2)]
            yTk = [("yT", m) for m in range(8)]
            skeys = [("S1", m) for m in range(8)] + [("S2", m) for m in range(8)] + [("c3", m) for m in range(8)]

            def load_u(i):
                dma("pool", UT[i % 2][:].rearrange("p a b -> p (a b)"), D["UT_l"][i], W=[("UT", i % 2)])

            def load_v(i):
                gi = i // GC
                dma("pool", Vg[gi % 2][:, i % GC, :], D["V"][i * 128:(i + 1) * 128, :], W=[("Vg", gi % 2, i % GC)])

            def a_part(i, p):
                hf = p // 4
                for dc in range((p % 4) * 4, (p % 4) * 4 + 4):
                    op("pe", lambda e, dc=dc: e.matmul(psH[:, hf * 512:(hf + 1) * 512], UT[i % 2][:, dc, :], yT[:, dc, hf * 512:(hf + 1) * 512],
                                                       start=(dc == 0), stop=(dc == 15)),
                       R=[("UT", i % 2)] + yTk, W=["psH"])
                if p == 7:
                    op("act", lambda e: e.activation(out=GH[(i // GC) % 2][:, i % GC, :], in_=psH[:], func=AF.Gelu_apprx_tanh),
                       R=["psH"], W=[("GH", (i // GC) % 2, i % GC)])

            cnt = {"r": 0, "y": 0}

            def load_s1(i):
                dma("sp", s1c[i % 3][:], S1d[:, i, :], R=[("S1d", ib) for ib in range(8)], W=[("s1c", i % 3)])

            def b_front(i, m):
                re = cnt["r"] % NRB
                r = cnt["r"] % NRM
                cnt["r"] += 1
                ek = [("EtB", re, h) for h in range(8)]
                if m in POOL_TILES:
                    op("pool", lambda e: e.tensor_tensor(out=EtB[re][:], in0=S2[:, m, :, :], in1=bc(s1c[i % 3][:, m * 8:(m + 1) * 8], 128, axis=2), op=ALU.mult),
                       R=[("s1c", i % 3)], W=ek)
                else:
                    for h in range(8):
                        op("act", lambda e: e.activation(out=EtB[re][:, h, :], in_=S2[:, m, h, :], func=AF.Identity, scale=s1c[i % 3][:, m * 8 + h:m * 8 + h + 1]),
                           R=[("s1c", i % 3)], W=[("EtB", re, h)])
                ef = EtB[re][:].rearrange("p a b -> p (a b)")
                op("dve", lambda e: e.scalar_tensor_tensor(out=mkB[r][:].rearrange("p a b -> p (a b)"), in0=ef, scalar=1.0, in1=ef,
                                                           op0=ALU.is_ge, op1=ALU.mult),
                   R=ek, W=[("mkB", r)] + ek)
                return r

            def b_back(i, m, r):
                for h in range(8):
                    op("pe", lambda e, h=h: e.matmul(psG[:, m * 128:(m + 1) * 128], mkB[r][:, h, :], Dg[:, m, h, :],
                                                     start=(h == 0 and m % 4 == 0), stop=(h == 7), skip_group_check=True),
                       R=[("mkB", r), ("Dg", m)], W=["psG"])

            def gh(i):
                gi = i // GC
                for hf in range(2):
                    dst = GH[gi % 2][:, i % GC, hf * 512:(hf + 1) * 512]
                    op("dve", lambda e: e.tensor_tensor(out=dst, in0=psG[:, hf * 512:(hf + 1) * 512], in1=dst, op=ALU.mult),
                       R=["psG", ("GH", gi % 2, i % GC)], W=[("GH", gi % 2, i % GC)])

            def c_unit(gi, u):
                m, hf = u // 2, u % 2
                yb = cnt["y"] % 2
                cnt["y"] += 1
                for ci in range(GC):
                    for c2 in range(2):
                        col = hf * 1024 + c2 * 512
                        op("pe", lambda e, ci=ci, c2=c2, col=col: e.matmul(psY[yb][:, c2 * 512:(c2 + 1) * 512], GH[gi % 2][:, ci, m * 128:(m + 1) * 128],
                                                                           Vg[gi % 2][:, ci, col:col + 512], start=(ci == 0), stop=(ci == GC - 1)),
                           R=[("GH", gi % 2, c) for c in range(GC)] + [("Vg", gi % 2, c) for c in range(GC)], W=[("psY", yb)])
                op("dve", lambda e: e.tensor_tensor(out=y[:, m, hf * 1024:(hf + 1) * 1024], in0=psY[yb][:], in1=y[:, m, hf * 1024:(hf + 1) * 1024], op=ALU.add),
                   R=[("psY", yb), ("y", m)], W=[("y", m)])

            load_u(0); load_u(1)
            for i in range(2 * GC):
                load_v(i)
            load_s1(0)
            if NCH > 1:
                load_s1(1)
            for p in range(8):
                a_part(0, p)
            queue = []
            for i in range(NCH):
                if i + 2 < NCH:
                    load_u(i + 2)
                    load_s1(i + 2)
                plan = [1] * 8 if i % 2 == 0 else [2, 1, 1, 1, 1, 1, 1, 0]
                for m in range(8):
                    r = b_front(i, m)
                    if i + 1 < NCH:
                        a_part(i + 1, m)
                    for _ in range(plan[m]):
                        if queue:
                            c_unit(*queue.pop(0))
                    b_back(i, m, r)
                gh(i)
                if i % GC == GC - 1:
                    queue += [(i // GC, u) for u in range(16)]
                    g_next = i // GC + 1
                    if g_next >= 2:
                        for ii in range(g_next * GC, (g_next + 1) * GC):
                            if ii < NCH:
                                load_v(ii)
            while queue:
                c_unit(*queue.pop(0))
            kb.barrier()
        if debug:
            for m in range(8):
                dma("sp", D["d_acc"][m * 128:(m + 1) * 128, :], y[:, m, :], R=[("y", m)], W=[("d_acc", m)])
        with ExitStack() as ph:
            g2 = kb.sb("ln2g", [128, 2048], F32, ph)
            b2 = kb.sb("ln2b", [128, 2048], F32, ph)
            stats = [kb.sb("stats2_%d" % i, [128, 4, 6], F32, ph) for i in range(2)]
            mv = [kb.sb("mv2_%d" % i, [128, 2], F32, ph) for i in range(2)]
            rs = [kb.sb("rs2_%d" % i, [128, 1], F32, ph) for i in range(2)]
            dma("sp", g2[:], D["ln2g"], W=["ln2g"])
            dma("sp", b2[:], D["ln2b"], W=["ln2b"])
            for m in range(8):
                _layer_norm(kb, y[:, m, :], y[:, m, :], g2, b2, stats[m % 2], mv[m % 2], rs[m % 2], [("y", m)], [("y", m)], ["ln2g", "ln2b"], epsc, tag=m % 2)
                dma("sp", D["out"][m * 128:(m + 1) * 128, :], y[:, m, :], R=[("y", m)], W=[("out", m)])
            kb.wait_keys("sp", [("out", m) for m in range(8)])
            kb.barrier()


def _layer_norm(kb, z, out_ap, g_t, b_t, stats, mv, rs, zkeys, okeys, gkeys, epsc, tag=0):
    op = kb.op
    kS, kM, kR = ("stats", tag), ("mv", tag), ("rs", tag)
    for c4 in range(4):
        op("dve", lambda e, c4=c4: e.bn_stats(out=stats[:, c4, :], in_=z[:, c4 * 512:(c4 + 1) * 512]), R=zkeys, W=[kS])
    op("dve", lambda e: e.bn_aggr(out=mv[:], in_=stats[:].rearrange("p a b -> p (a b)")), R=[kS], W=[kM])
    op("act", lambda e: e.activation(out=rs[:], in_=mv[:, 1:2], func=AF.Sqrt, bias=epsc[:, 0:1]), R=[kM, "epsc"], W=[kR])
    op("dve", lambda e: e.reciprocal(out=rs[:], in_=rs[:]), R=[kR], W=[kR])
    op("dve", lambda e: e.tensor_scalar(out=mv[:, 1:2], in0=mv[:, 0:1], scalar1=rs[:, 0:1], scalar2=-1.0, op0=ALU.mult, op1=ALU.mult),
       R=[kM, kR], W=[kM])
    op("act", lambda e: e.activation(out=z, in_=z, func=AF.Identity, scale=rs[:, 0:1], bias=mv[:, 1:2]), R=zkeys + [kM, kR], W=zkeys)
    op("dve", lambda e: e.tensor_tensor(out=z, in0=z, in1=g_t[:], op=ALU.mult), R=zkeys + gkeys, W=zkeys)
    op("pool", lambda e: e.tensor_tensor(out=out_ap, in0=z, in1=b_t[:], op=ALU.add), R=zkeys + gkeys, W=okeys)


def _attention(nc, kb, D, debug, ph, g, L):
    op, dma = kb.op, kb.dma
    kT_sel, kT_win, V_sw, qT, kc_aug, vcx = L["kT_sel"], L["kT_win"], L["V_sw"], L["qT"], L["kc_aug"], L["vcx"]
    identb, ident, expand, tri_lo, tri_up = L["identb"], L["ident"], L["expand"], L["tri_lo"], L["tri_up"]
    maskc, winmask0, vis, cstb, o_nsa, gates = L["maskc"], L["winmask0"], L["vis"], L["cstb"], L["o_nsa"], L["gates"]
    NE = 4
    Eb = [kb.sb("Eb%d" % i, [128, 512], BF16, ph) for i in range(NE)]
    NS = 3
    psS = [kb.ps("psS%d" % i, [128, 512], F32, ph) for i in range(NS)]
    pc = kb.ps("pc", [128, 4, 256], F32, ph)
    psel = kb.ps("psel", [128, 512], F32, ph)
    pwin = kb.ps("pwin", [128, 512], F32, ph)
    pT = kb.ps("pT", [128, 512], F32, ph)
    zc = kb.sb("zc", [128, 12], F32, ph)
    rz = kb.sb("rz", [128, 12], F32, ph)
    coef = kb.sb("coef", [128, 12], F32, ph)
    imp = kb.sb("imp", [128, 64], F32, ph)
    impa = kb.sb("impa", [128, 64], F32, ph)
    imp2 = kb.sb("imp2", [128, 64], F32, ph)
    m8 = kb.sb("m8", [128, 16], F32, ph)
    nsel = kb.sb("nsel", [128, 64], F32, ph)
    nselT = kb.sb("nselT", [64, 128], BF16, ph)
    oacc = kb.sb("oacc", [128, 4, 64], F32, ph)
    kaug_keys_s = ["kT_sel_aug"] + [("kT_sel", m) for m in range(8)]
    kaug_keys_w = ["kT_win_aug"] + [("kT_win", m) for m in range(8)]
    st = {"s": 0, "e": 0}

    def emit_score(job):
        mm_list, nrow = job["mm"], job.get("nrow", 128)
        sb_ = st["s"] % NS
        st["s"] += 1
        eb = st["e"] % NE
        st["e"] += 1
        n = len(mm_list)
        for i, (lh, rh, rk) in enumerate(mm_list):
            op("pe", lambda e, lh=lh, rh=rh, i=i: e.matmul(psS[sb_][0:nrow, :] if len(rh.shape) == 2 else
                                                           psS[sb_][0:nrow, :].rearrange("p (a b) -> p a b", a=4),
                                                           lh, rh, start=(i == 0), stop=(i == n - 1)),
               R=rk, W=[("psS", sb_)])
        op("act", lambda e: e.activation(out=Eb[eb][0:nrow, :], in_=psS[sb_][0:nrow, :], func=AF.Exp, scale=0.125),
           R=[("psS", sb_)], W=[("Eb", eb)])
        return (job, eb)

    def run_jobs(jobs):
        pending = []
        for job in jobs:
            pending.append(emit_score(job))
            if len(pending) > 2:
                j_, eb_ = pending.pop(0)
                j_["pv"](eb_)
        for j_, eb_ in pending:
            j_["pv"](eb_)

    for m in range(8):
        J = 4 * m + 3
        qrhs = qT[0:69, m, :, :].rearrange("p a b -> p (a b)")
        qk = [("qT", m), "qT_aug"]
        vkeys = ["V_ones"] + [("V_sw", i) for i in range(8)]
        nch = 1 if 8 * J + 6 < 128 else 2
        jobs = []
        for ch in range(nch):
            nr = 128 if ch == 0 else 127

            def pv_c(eb, ch=ch, nr=nr):
                for hh in range(4):
                    op("pe", lambda e, hh=hh: e.matmul(pc[:, hh, 0:129], Eb[eb][0:nr, hh * 128:(hh + 1) * 128], vcx[0:nr, ch, :],
                                                       start=(ch == 0 and hh % 2 == 0), stop=(ch == nch - 1), skip_group_check=True),
                       R=[("Eb", eb), "vcx0", "vcx1", "vcx_c", "vcx_pad"], W=["pc"])
            jobs.append(dict(mm=[
                (kc_aug[0:69, ch * 128:ch * 128 + nr], qrhs, qk + ["kc_aug", "kc_aug_aug", "kc_pad"]),
                (identb[0:nr, 0:nr], bc(maskc[0:nr, m, ch, :], 4), ["identb", "maskc"]),
            ], nrow=nr, pv=pv_c))
        run_jobs(jobs)
        op("dve", lambda e: e.tensor_scalar(out=zc[:, 0:4], in0=pc[:, :, 64], scalar1=1e-30, scalar2=None, op0=ALU.max), R=["pc"], W=["zc0"])
        op("dve", lambda e: e.reciprocal(out=rz[:, 0:4], in_=zc[:, 0:4]), R=["zc0"], W=["rz0"])
        op("dve", lambda e: e.tensor_scalar(out=imp[:], in0=pc[:, 0, 65:129], scalar1=rz[:, 0:1], scalar2=None, op0=ALU.mult),
           R=["pc", "rz0"], W=["imp"])
        for hh in range(1, 4):
            op("dve", lambda e, hh=hh: e.scalar_tensor_tensor(out=imp[:], in0=pc[:, hh, 65:129], scalar=rz[:, hh:hh + 1], in1=imp[:],
                                                              op0=ALU.mult, op1=ALU.add), R=["pc", "rz0", "imp"], W=["imp"])
        op("dve", lambda e: e.tensor_tensor(out=impa[:], in0=imp[:], in1=vis[:, m, :], op=ALU.mult), R=["imp", "vis"], W=["impa"])
        op("dve", lambda e: e.tensor_tensor(out=impa[:], in0=impa[:], in1=cstb[:, m, :], op=ALU.add), R=["impa", "cstb"], W=["impa"])
        op("dve", lambda e: e.max(out=m8[:, 0:8], in_=impa[:]), R=["impa"], W=["m8a"])
        op("dve", lambda e: e.match_replace(out=imp2[:], in_to_replace=m8[:, 0:8], in_values=impa[:], imm_value=-1e30),
           R=["impa", "m8a"], W=["imp2"])
        op("dve", lambda e: e.max(out=m8[:, 8:16], in_=imp2[:]), R=["imp2"], W=["m8b"])
        op("dve", lambda e: e.tensor_scalar(out=nsel[:], in0=impa[:], scalar1=m8[:, 15:16], scalar2=None, op0=ALU.is_ge),
           R=["impa", "m8b"], W=["nsel"])
        op("dve", lambda e: e.tensor_scalar(out=nsel[:], in0=nsel[:], scalar1=-NEG, scalar2=NEG, op0=ALU.mult, op1=ALU.add),
           R=["nsel"], W=["nsel"])
        if debug:
            dma("sp", D["d_imp"][:, m, :], impa[:], R=["impa"], W=[("d_imp", m, g)])
            dma("sp", D["d_negsel"][:, m, :], nsel[:], R=["nsel"], W=[("d_negsel", m, g)])
        op("dve", lambda e: e.tensor_tensor(out=coef[:, 0:4], in0=rz[:, 0:4], in1=gates[:, m, 12 * g:12 * g + 12].rearrange("p (b a) -> p a b", a=3)[:, 0, :], op=ALU.mult),
           R=["rz0", ("gates", m, g)], W=["coef0"])
        for hh in range(4):
            op("dve", lambda e, hh=hh: e.tensor_scalar(out=oacc[:, hh, :], in0=pc[:, hh, 0:64], scalar1=coef[:, hh:hh + 1], scalar2=None, op0=ALU.mult),
               R=["pc", "coef0"], W=[("oacc", hh)])
        jobs = []
        wlist = [Dd for Dd in (4, 3, 2, 1, 0) if J - Dd >= 0]
        for Dd in wlist:
            kt = J - Dd
            mm = [(kT_win[0:69, kt * 128:(kt + 1) * 128], qrhs, qk + kaug_keys_w)]
            if m == 0:
                mm.append((identb[:], bc(winmask0[:, Dd, :], 4), ["identb", "winmask0"]))
            elif Dd == 0:
                mm.append((identb[:], bc(tri_lo[:], 4), ["identb", "tri_lo"]))
            elif Dd == 4:
                mm.append((identb[:], bc(tri_up[:], 4), ["identb", "tri_up"]))

            def pv_w(eb, kt=kt, Dd=Dd):
                for hh in range(4):
                    op("pe", lambda e, hh=hh: e.matmul(pwin[:, hh * 65:hh * 65 + 65], Eb[eb][:, hh * 128:(hh + 1) * 128], V_sw[:, kt, 1, :],
                                                       start=(Dd == wlist[0] and hh == 0), stop=(Dd == 0), skip_group_check=True),
                       R=[("Eb", eb)] + vkeys, W=["pwin"])
            jobs.append(dict(mm=mm, pv=pv_w))
        run_jobs(jobs)
        op("pe", lambda e: e.transpose(pT[0:64, 0:128], nsel[:], ident[:]), R=["nsel", "ident"], W=["pT"])
        op("dve", lambda e: e.tensor_copy(out=nselT[:], in_=pT[0:64, 0:128]), R=["pT"], W=["nselT"])
        jobs = []
        for kt in range(J + 1):
            mm = [(kT_sel[0:69, kt * 128:(kt + 1) * 128], qrhs, qk + kaug_keys_s),
                  (expand[:, kt, :], bc(nselT[:], 4), ["expand", "nselT"])]
            if kt == J:
                mm.append((identb[:], bc(tri_lo[:], 4), ["identb", "tri_lo"]))

            def pv_s(eb, kt=kt):
                for hh in range(4):
                    op("pe", lambda e, hh=hh: e.matmul(psel[:, hh * 65:hh * 65 + 65], Eb[eb][:, hh * 128:(hh + 1) * 128], V_sw[:, kt, 0, :],
                                                       start=(kt == 0 and hh == 0), stop=(kt == J), skip_group_check=True),
                       R=[("Eb", eb)] + vkeys, W=["psel"])
            jobs.append(dict(mm=mm, pv=pv_s))
        run_jobs(jobs)
        pselv = psel[:, 0:260].rearrange("p (a b) -> p a b", a=4)
        pwinv = pwin[:, 0:260].rearrange("p (a b) -> p a b", a=4)
        op("dve", lambda e: e.tensor_scalar(out=zc[:, 4:8], in0=pselv[:, :, 64], scalar1=1e-30, scalar2=None, op0=ALU.max), R=["psel"], W=["zc1"])
        op("dve", lambda e: e.tensor_scalar(out=zc[:, 8:12], in0=pwinv[:, :, 64], scalar1=1e-30, scalar2=None, op0=ALU.max), R=["pwin"], W=["zc2"])
        op("dve", lambda e: e.reciprocal(out=rz[:, 4:12], in_=zc[:, 4:12]), R=["zc1", "zc2"], W=["rz1"])
        op("dve", lambda e: e.tensor_tensor(out=coef[:, 4:12].rearrange("p (a b) -> p a b", a=2),
                                            in0=rz[:, 4:12].rearrange("p (a b) -> p a b", a=2),
                                            in1=gates[:, m, 12 * g:12 * g + 12].rearrange("p (b a) -> p a b", a=3)[:, 1:3, :], op=ALU.mult),
           R=["rz1", ("gates", m, g)], W=["coef"])
        for hh in range(4):
            op("dve", lambda e, hh=hh: e.scalar_tensor_tensor(out=oacc[:, hh, :], in0=pselv[:, hh, 0:64], scalar=coef[:, 4 + hh:5 + hh], in1=oacc[:, hh, :],
                                                              op0=ALU.mult, op1=ALU.add), R=["psel", "coef", ("oacc", hh)], W=[("oacc", hh)])
            op("dve", lambda e, hh=hh: e.scalar_tensor_tensor(out=o_nsa[:, m, g * 256 + hh * 64:g * 256 + hh * 64 + 64], in0=pwinv[:, hh, 0:64],
                                                              scalar=coef[:, 8 + hh:9 + hh], in1=oacc[:, hh, :], op0=ALU.mult, op1=ALU.add),
               R=["pwin", "coef", ("oacc", hh)], W=[("o_nsa", m)])


_CACHE = {}


def kernel(**inputs):
    maps = _prep_inputs(inputs)
    if "nc" not in _CACHE:
        _CACHE["nc"] = build(False)
    nc = _CACHE["nc"]
    res = run_bass_kernel_spmd(nc, maps, core_ids=list(range(8)))
    out = np.zeros((2, 4096, 2048), np.float32)
    for c in range(8):
        b, s = c // 4, c % 4
        r = res.results[c]["out"]
        for m in range(8):
            j = 4 * m + s
            out[b, j * 128:(j + 1) * 128, :] = r[m * 128:(m + 1) * 128, :]
    return out
```
